# Optimizing a Trainium2 kernel written in Bass

```python
import jax, jax.numpy as jnp
from jax import lax
import numpy as np

D_MODEL = 4096
BATCH = 2
SEQ = 8192
DEPTH = 1

GRID_W = 64
PLE_DIM = 256
HEAD_DIM = 128
N_Q_HEADS = 16
N_KV_HEADS = 4
Q_PER_KV = N_Q_HEADS // N_KV_HEADS
ATTN_DIM = N_Q_HEADS * HEAD_DIM
KV_DIM = N_KV_HEADS * HEAD_DIM
Q_BLOCK = 128
ROPE_THETA = 10000.0
HG_HEADS = 16
HG_KEY_DIM = 128
HG_VAL_DIM = 128
HG_QK_DIM = HG_HEADS * HG_KEY_DIM
HG_V_DIM = HG_HEADS * HG_VAL_DIM
HG_CHUNK = 64
IN_SIZES = (ATTN_DIM, KV_DIM, KV_DIM, HG_QK_DIM, HG_QK_DIM, HG_QK_DIM, HG_V_DIM, HG_V_DIM)
IN_COLS = sum(IN_SIZES)
D_FF = 11008
CONV_WIDTH = 3
RMS_EPS = 1e-6
LN_EPS = 1e-5
DN_ALPHA = (2.0 * DEPTH) ** 0.25
DN_BETA = (8.0 * DEPTH) ** -0.25

kernel_name = "hybrid_gqa_hgrn2_deepnorm_encoder"


def rms_norm(x, gain):
    xf = x.astype(jnp.float32)
    y = xf * lax.rsqrt(jnp.mean(xf * xf, axis=-1, keepdims=True) + RMS_EPS)
    return y * gain.astype(jnp.float32)


def layer_norm(x, g, b):
    xf = x.astype(jnp.float32)
    mu = jnp.mean(xf, axis=-1, keepdims=True)
    var = jnp.mean(jnp.square(xf - mu), axis=-1, keepdims=True)
    y = (xf - mu) * lax.rsqrt(var + LN_EPS) * g.astype(jnp.float32) + b.astype(jnp.float32)
    return y.astype(x.dtype)


def axial_rope_tables(seq_len):
    rows = seq_len // GRID_W
    row = jnp.repeat(jnp.arange(rows), GRID_W).astype(jnp.float32)
    col = jnp.tile(jnp.arange(GRID_W), rows).astype(jnp.float32)
    sec = HEAD_DIM // 2
    inv = ROPE_THETA ** (-jnp.arange(0, sec, 2, dtype=jnp.float32) / sec)
    ang_r = row[:, None] * inv[None, :]
    ang_c = col[:, None] * inv[None, :]
    ang = jnp.concatenate([ang_r, ang_r, ang_c, ang_c], axis=-1)
    return jnp.cos(ang), jnp.sin(ang)


def apply_axial_rope(x, cos, sin):
    xs = x.reshape(x.shape[:-1] + (2, 2, HEAD_DIM // 4))
    rot = jnp.stack([-xs[..., 1, :], xs[..., 0, :]], axis=-2).reshape(x.shape)
    return x * cos[None, :, None, :] + rot * sin[None, :, None, :]


def gqa_axial_attention(q, k, v, cos, sin, q_gain, k_gain):
    B, S, _ = q.shape
    q = apply_axial_rope(rms_norm(q.reshape(B, S, N_Q_HEADS, HEAD_DIM), q_gain), cos, sin).astype(v.dtype)
    k = apply_axial_rope(rms_norm(k.reshape(B, S, N_KV_HEADS, HEAD_DIM), k_gain), cos, sin).astype(v.dtype)
    v = v.reshape(B, S, N_KV_HEADS, HEAD_DIM)
    n_blk = S // Q_BLOCK
    qb = q.reshape(B, n_blk, Q_BLOCK, N_KV_HEADS, Q_PER_KV, HEAD_DIM).transpose(1, 0, 3, 4, 2, 5)
    scale = HEAD_DIM ** -0.5

    def block(q_blk):
        s = jnp.einsum('bhgqd,bkhd->bhgqk', q_blk, k, preferred_element_type=jnp.float32) * scale
        pr = jax.nn.softmax(s, axis=-1).astype(v.dtype)
        return jnp.einsum('bhgqk,bkhd->bqhgd', pr, v)

    o = lax.map(block, qb)
    return o.transpose(1, 0, 2, 3, 4, 5).reshape(B, S, ATTN_DIM)


def hgrn2_chunk_scan(q, k, v, logf):
    B, S, H, dk = q.shape
    dv = v.shape[-1]
    C = HG_CHUNK
    N = S // C
    q, k, v, g = (t.reshape(B, N, C, H, t.shape[-1]) for t in (q, k, v, logf))
    b = jnp.cumsum(g, axis=2)
    b_ref = b[:, :, C // 2 - 1:C // 2]
    q_in = q * jnp.exp(b - b_ref)
    k_in = k * jnp.exp(b_ref - b)
    A = jnp.einsum('bnthd,bnshd->bnhts', q_in, k_in)
    A = jnp.where(jnp.tril(jnp.ones((C, C), dtype=bool)), A, 0.0)
    o_intra = jnp.einsum('bnhts,bnshe->bnthe', A, v)
    b_last = b[:, :, -1:]
    U = jnp.einsum('bnshd,bnshe->bnhde', k * jnp.exp(b_last - b), v)
    decay = jnp.exp(b[:, :, -1])

    def step(state, inp):
        dec, u = inp
        return dec[..., None] * state + u, state

    _, s_before = lax.scan(step, jnp.zeros((B, H, dk, dv), jnp.float32),
                           (decay.transpose(1, 0, 2, 3), U.transpose(1, 0, 2, 3, 4)))
    s_before = s_before.transpose(1, 0, 2, 3, 4)
    o_inter = jnp.einsum('bnthd,bnhde->bnthe', q * jnp.exp(b), s_before)
    return (o_intra + o_inter).reshape(B, S, H, dv)


def hgrn2_bidirectional(q_raw, zf_raw, zb_raw, i_raw, g_raw, lb_fwd, lb_bwd, norm_gain):
    B, S, _ = q_raw.shape
    heads = lambda t, d: t.astype(jnp.float32).reshape(B, S, HG_HEADS, d)
    q = jax.nn.silu(heads(q_raw, HG_KEY_DIM))
    v = heads(i_raw, HG_VAL_DIM)

    def gates(z_raw, lb):
        z = heads(z_raw, HG_KEY_DIM)
        lb = lb.reshape(HG_HEADS, HG_KEY_DIM)
        logf = jnp.logaddexp(jnp.log(lb), jnp.log1p(-lb) + jax.nn.log_sigmoid(z))
        return (1.0 - lb) * jax.nn.sigmoid(-z), logf

    k_f, g_f = gates(zf_raw, lb_fwd)
    k_b, g_b = gates(zb_raw, lb_bwd)
    flip = lambda t: jnp.flip(t, axis=1)
    o = hgrn2_chunk_scan(q, k_f, v, g_f) + flip(hgrn2_chunk_scan(flip(q), flip(k_b), flip(v), flip(g_b)))
    o = rms_norm(o, norm_gain.reshape(HG_HEADS, HG_VAL_DIM)).reshape(B, S, HG_V_DIM)
    return (o * jax.nn.silu(g_raw.astype(jnp.float32))).astype(q_raw.dtype)


def setup_inputs(seed: int = 0) -> dict:
    key = jax.random.key(seed)
    ks = jax.random.split(key, 24)
    nrm = lambda k, shape, scale: jax.random.normal(k, shape, jnp.float32) * scale
    gain = lambda k, shape: 1.0 + 0.01 * jax.random.normal(k, shape, jnp.float32)
    L, D = DEPTH, D_MODEL
    return {
        "x": nrm(ks[0], (BATCH, SEQ, D), 1.0),
        "p": nrm(ks[1], (L, BATCH, SEQ, PLE_DIM), 1.0),
        "w_in": nrm(ks[2], (L, D, IN_COLS), D ** -0.5),
        "q_norm": gain(ks[3], (L, HEAD_DIM)),
        "k_norm": gain(ks[4], (L, HEAD_DIM)),
        "lb_logits": nrm(ks[5], (2, L + 1, HG_QK_DIM), 0.5),
        "hg_norm": gain(ks[6], (L, HG_V_DIM)),
        "w_pa": nrm(ks[7], (L, ATTN_DIM, D), ATTN_DIM ** -0.5 * DN_BETA),
        "w_pb": nrm(ks[8], (L, HG_V_DIM, D), HG_V_DIM ** -0.5 * DN_BETA),
        "w_gate": nrm(ks[9], (L, D, 2 * D), D ** -0.5),
        "b_gate": nrm(ks[10], (L, 2 * D), 0.01),
        "w_o": nrm(ks[11], (L, D, D), D ** -0.5 * DN_BETA),
        "ln1_g": gain(ks[12], (L, D)),
        "ln1_b": nrm(ks[13], (L, D), 0.01),
        "w_up": nrm(ks[14], (L, D, 2 * D_FF), D ** -0.5),
        "conv_w": nrm(ks[15], (L, CONV_WIDTH, D_FF), CONV_WIDTH ** -0.5),
        "conv_b": nrm(ks[16], (L, D_FF), 0.01),
        "w_down": nrm(ks[17], (L, D_FF, D), D_FF ** -0.5 * DN_BETA),
        "ln2_g": gain(ks[18], (L, D)),
        "ln2_b": nrm(ks[19], (L, D), 0.01),
        "w_pg": nrm(ks[20], (L, D, D), D ** -0.5),
        "w_ple": nrm(ks[21], (L, PLE_DIM, D), PLE_DIM ** -0.5),
    }


def reference(x, p, w_in, q_norm, k_norm, lb_logits, hg_norm, w_pa, w_pb, w_gate, b_gate, w_o,
              ln1_g, ln1_b, w_up, conv_w, conv_b, w_down, ln2_g, ln2_b, w_pg, w_ple):
    B, S, _ = x.shape
    cos, sin = axial_rope_tables(S)
    lb_all = jnp.cumsum(jax.nn.softmax(lb_logits.astype(jnp.float32), axis=1), axis=1)
    split_at = np.cumsum(IN_SIZES)[:-1].tolist()
    for i in range(DEPTH):
        proj = x @ w_in[i]
        qa, ka, va, qh, zf, zb, ih, gh = jnp.split(proj, split_at, axis=-1)
        y_attn = gqa_axial_attention(qa, ka, va, cos, sin, q_norm[i], k_norm[i])
        y_hgrn = hgrn2_bidirectional(qh, zf, zb, ih, gh, lb_all[0, i], lb_all[1, i], hg_norm[i])
        g_attn, g_hgrn = jnp.split(jax.nn.sigmoid(x @ w_gate[i] + b_gate[i]), 2, axis=-1)
        mixed = (g_attn * (y_attn @ w_pa[i]) + g_hgrn * (y_hgrn @ w_pb[i])) @ w_o[i]
        x = layer_norm(DN_ALPHA * x + mixed, ln1_g[i], ln1_b[i])
        u, gpre = jnp.split(x @ w_up[i], 2, axis=-1)
        gp = jnp.pad(gpre, ((0, 0), (1, 1), (0, 0)))
        gconv = (gp[:, :-2] * conv_w[i, 0] + gp[:, 1:-1] * conv_w[i, 1]
                 + gp[:, 2:] * conv_w[i, 2] + conv_b[i])
        ffn = (jax.nn.silu(gconv) * u) @ w_down[i]
        x = layer_norm(DN_ALPHA * x + ffn, ln2_g[i], ln2_b[i])
        x = x + jax.nn.sigmoid(x @ w_pg[i]) * (p[i] @ w_ple[i])
    return x
```

```python
from contextlib import ExitStack
import os
import numpy as np
import concourse.bass as bass
import concourse.mybir as mybir
from concourse.bass_utils import run_bass_kernel_spmd

F32 = mybir.dt.float32
BF16 = mybir.dt.bfloat16
AF = mybir.ActivationFunctionType
ALU = mybir.AluOpType

D = 4096
TOK = 2048
NT = 512
NTT = 4
DFF = 11008
NFB = 86
ALPHA = 2.0 ** 0.25
RMS_EPS = 1e-6
LN_EPS = 1e-5
SCALE = 128 ** -0.5
ESHIFT = -4.0

V_BG = 0
V_L1G = 64
V_L1B = 96
V_L2G = 128
V_L2B = 160
V_CW = 192
V_CB = 450
V_QN = 536
V_KN = 537
V_HGN = 538
V_LBL = 554
V_MF = 618
V_MB = 622
V_ML = 626
V_MR = 630
V_OMF = 634
V_OMB = 638
NV = 642
C_RT, C_ID, C_ONE, C_MF, C_MB = 0, 128, 256, 384, 512


class _Stop(Exception):
    pass


class Buf:
    __slots__ = ("name", "t", "last_write", "reads", "sem")

    def __init__(self, name, t=None):
        self.name = name
        self.t = t
        self.last_write = None
        self.reads = []
        self.sem = None

    def __getitem__(self, idx):
        return self.t[idx]


class Trk:
    ENG = ("pe", "act", "dve", "pool", "sp")

    def __init__(self, nc, stack):
        self.nc = nc
        self.stack = stack
        self.eng = {"pe": nc.tensor, "act": nc.scalar, "dve": nc.vector, "pool": nc.gpsimd, "sp": nc.sync}
        self.sem = {}
        self.cnt = {}
        for e in ("pe", "act", "dve", "pool"):
            self.sem[e] = stack.enter_context(nc.semaphore("c_" + e))
            self.cnt[e] = 0
        self.waited = {e: {} for e in self.ENG}
        self.dma_sems = []
        self.free_sems = {}
        self.nwaits = 0
        self.ninst = 0

    def _wait(self, e, ev):
        kind, s, v = ev
        if kind == "c":
            if s == e and e == "pe":
                return
            sem = self.sem[s]
            key = "c" + s
        else:
            sem = s[0]
            key = id(s)
            v = s[1]
        w = self.waited[e]
        if w.get(key, -1) >= v:
            return
        w[key] = v
        self.eng[e].wait_ge(sem, v)
        self.nwaits += 1

    def _deps(self, e, reads, writes):
        for b in reads:
            if b.last_write is not None:
                self._wait(e, b.last_write)
        for b in writes:
            if b.last_write is not None:
                self._wait(e, b.last_write)
            for ev in b.reads:
                self._wait(e, ev)

    def _reg(self, ev, reads, writes):
        for b in reads:
            b.reads.append(ev)
        for b in writes:
            b.last_write = ev
            b.reads = []

    def op(self, e, fn, reads=(), writes=(), signal=True):
        self._deps(e, reads, writes)
        inst = fn(self.eng[e])
        self.ninst += 1
        if not signal:
            return None
        self.cnt[e] += 1
        inst.then_inc(self.sem[e], 1)
        ev = ("c", e, self.cnt[e])
        self._reg(ev, reads, writes)
        return ev

    def _dsem(self, b, q):
        if b.sem is None:
            b.sem = {}
        if q not in b.sem:
            fl = self.free_sems.setdefault(q, [])
            if fl:
                b.sem[q] = fl.pop()
            else:
                rec = [self.stack.enter_context(self.nc.semaphore("d_%d" % len(self.dma_sems))), 0]
                self.dma_sems.append(rec)
                b.sem[q] = rec
        return b.sem[q]

    def dma(self, q, out, in_, reads=(), writes=(), sembuf=None):
        self._deps(q, reads, writes)
        s = self._dsem(sembuf, q)
        inst = self.eng[q].dma_start(out=out, in_=in_)
        s[1] += 16
        inst.then_inc(s[0], 16)
        self.ninst += 1
        ev = ("d", s, s[1])
        self._reg(ev, reads, writes)
        return ev

    def custom(self, e, fn, owner, inc, reads=(), writes=()):
        self._deps(e, reads, writes)
        s = self._dsem(owner, "cc")
        inst = fn(self.eng[e])
        s[1] += inc
        inst.then_inc(s[0], inc)
        ev = ("d", s, s[1])
        self._reg(ev, reads, writes)
        return ev

    def barrier(self, release=()):
        for e in self.ENG:
            for s in ("pe", "act", "dve", "pool"):
                if self.cnt[s] > 0:
                    self._wait(e, ("c", s, self.cnt[s]))
            for s in self.dma_sems:
                if s[1] > 0:
                    self._wait(e, ("d", s, s[1]))
        for b in release:
            if b.sem is not None:
                for q, rec in b.sem.items():
                    self.free_sems.setdefault(q, []).append(rec)
                b.sem = None


class Arena:
    def __init__(self, nc, base=0):
        self.nc = nc
        self.base = base
        self.off = base
        self.n = 0
        self.bufs = []

    def reset(self):
        self.off = self.base
        b = self.bufs
        self.bufs = []
        return b

    def sb(self, name, shape, dt):
        esz = 4 if dt == F32 else 2
        per = esz
        for s in shape[1:]:
            per *= s
        self.off = (self.off + 63) // 64 * 64
        self.n += 1
        t = self.nc.alloc_sbuf_tensor_at("%s_%d" % (name, self.n), list(shape), dt, offset=self.off)
        self.off += per
        assert self.off <= 16384 + 211000, (name, self.off)
        b = Buf(name, t)
        self.bufs.append(b)
        return b


def build_nc():
    nc = bass.Bass("TRN2", target_bir_lowering=False)
    _NC_CACHE["nc"] = nc

    KSTOP = int(os.environ.get("KSTOP", "99"))
    KP1T = int(os.environ.get("KP1T", str(NTT)))
    KP1C = int(os.environ.get("KP1C", "104"))
    KCOLL = int(os.environ.get("KCOLL", "1"))
    KH = int(os.environ.get("KH", "16"))
    KG = int(os.environ.get("KG", "4"))
    KQT = int(os.environ.get("KQT", str(NTT)))
    KT4 = int(os.environ.get("KT4", str(NTT)))
    KT5 = int(os.environ.get("KT5", str(NTT)))
    KCB = int(os.environ.get("KCB", "32"))
    KFB = int(os.environ.get("KFB", str(NFB)))
    names = _NC_CACHE.setdefault("names", set())

    def din(name, shape, dt=F32):
        if KSTOP <= 3 and name in ("pT", "w_pa", "w_pb", "w_gate", "w_o", "w_up", "w_down", "w_pg", "w_ple"):
            return None
        names.add(name)
        return nc.dram_tensor(name, shape, dt, kind="ExternalInput").ap()

    xT = din("xT", [D, TOK])
    pT = din("pT", [256, TOK])
    w_in = din("w_in", [D, 13312])
    w_pa = din("w_pa", [2048, D])
    w_pb = din("w_pb", [2048, D])
    w_gate = din("w_gate", [D, 2 * D])
    w_o = din("w_o", [D, D])
    w_up = din("w_up", [D, 2 * DFF])
    w_down = din("w_down", [DFF, D])
    w_pg = din("w_pg", [D, D])
    w_ple = din("w_ple", [256, D])
    vec_d = din("vec", [128, NV])
    cst_d = din("cst", [128, 640])
    cos_d = din("cosT", [128, TOK])
    sin_d = din("sinT", [128, TOK])
    outT = nc.dram_tensor("outT", [D, TOK], F32, kind="ExternalOutput").ap()

    def scr(name, shape, dt):
        t = nc.dram_tensor(name, shape, dt)
        _NC_CACHE.setdefault("scratch", []).append(name)
        return Buf(name, t.ap())

    QS = scr("QS", [2048, TOK], BF16)
    KCI = [scr("KCI%d" % g, [128, TOK], BF16) for g in range(4)]
    KCO = [scr("KCO%d" % g, [512, TOK], BF16) for g in range(4)]
    VCI = [scr("VCI%d" % g, [TOK, 128], BF16) for g in range(4)]
    VCO = [scr("VCO%d" % g, [4 * TOK, 128], BF16) for g in range(4)]
    HQ = scr("HQ", [2048, TOK], F32)
    HZF = scr("HZF", [2048, TOK], F32)
    HZB = scr("HZB", [2048, TOK], F32)
    HG = scr("HG", [2048, TOK], F32)
    HV = scr("HV", [TOK, 2048], BF16)
    OLOC = scr("OLOC", [2048, TOK], F32)
    QGF = scr("QGF", [2048, TOK], BF16)
    QGB = scr("QGB", [2048, TOK], BF16)
    SCI = [scr("SCI%d" % i, [8 * 128, 128], F32) for i in range(4)]
    SCO = [scr("SCO%d" % i, [4 * 8 * 128, 128], F32) for i in range(4)]
    DCI = scr("DCI", [128, 32], F32)
    DCO = scr("DCO", [512, 32], F32)
    YA = scr("YA", [2048, TOK], BF16)
    YH = scr("YH", [2048, TOK], BF16)
    H1 = scr("H1", [D, TOK + 2], F32)
    EDI = scr("EDI", [D, 2], F32)
    EDO = scr("EDO", [4 * D, 2], F32)
    WOT_t = nc.dram_tensor("WOT", [16, 128, 32 * 256], BF16)
    WPT_t = nc.dram_tensor("WPT", [16, 128, 32 * 256], BF16)
    WDT_t = nc.dram_tensor("WDT", [16, 4, 128, 22 * 256], BF16)
    WAT_t = nc.dram_tensor("WAT", [16, 128, 16 * 256], BF16)
    WBT_t = nc.dram_tensor("WBT", [16, 128, 16 * 256], BF16)
    WGT_t = nc.dram_tensor("WGT", [16, 2, 128, 32 * 256], BF16)
    WAT = [Buf("WAT%d" % i, WAT_t.ap()[i]) for i in range(16)]
    WBT = [Buf("WBT%d" % i, WBT_t.ap()[i]) for i in range(16)]
    WGT = [[Buf("WGT%d_%d" % (i, j), WGT_t.ap()[i, j]) for j in range(2)] for i in range(16)]
    WOT = [Buf("WOT%d" % i, WOT_t.ap()[i]) for i in range(16)]
    WPT = [Buf("WPT%d" % i, WPT_t.ap()[i]) for i in range(16)]
    WDT = [[Buf("WDT%d_%d" % (i, q), WDT_t.ap()[i, q]) for q in range(4)] for i in range(16)]
    QK4 = [(0, 22), (22, 21), (43, 22), (65, 21)]
    convsem = Buf("convsem")
    GROUPS = [[0, 1, 2, 3], [4, 5, 6, 7]]

    with ExitStack() as st:
        st.enter_context(nc.allow_non_contiguous_dma(reason="single-column halo/edge transfers"))
        T = Trk(nc, st)
        PSF = [Buf("psf%d" % i, st.enter_context(nc.psum_tensor("psf%d" % i, [128, 512], F32))) for i in range(6)]
        PTB = [Buf("ptb%d" % i, st.enter_context(nc.psum_tensor("ptb%d" % i, [128, 8, 128], BF16))[:, 0:4, :]) for i in range(2)]

        A0 = Arena(nc, 16384)
        vec = A0.sb("vec", [128, NV], F32)
        cst = A0.sb("cst", [128, 640], F32)
        cb16 = A0.sb("cb16", [128, 640], BF16)
        rtq = A0.sb("rtq", [128, 128], BF16)
        rtk = A0.sb("rtk", [128, 128], BF16)
        lbt = A0.sb("lbt", [128, 64], F32)
        A = Arena(nc, A0.off)

        T.dma("sp", vec[:], vec_d[:, :], writes=[vec], sembuf=vec)
        T.dma("sp", cst[:], cst_d[:, :], writes=[cst], sembuf=cst)
        T.op("dve", lambda e: e.tensor_copy(out=cb16[:], in_=cst[:]), reads=[cst], writes=[cb16])
        T.op("dve", lambda e: e.tensor_scalar(out=rtq[:], in0=cst[:, C_RT:C_RT + 128], scalar1=vec[:, V_QN:V_QN + 1],
                                              scalar2=None, op0=ALU.mult), reads=[cst, vec], writes=[rtq])
        T.op("dve", lambda e: e.tensor_scalar(out=rtk[:], in0=cst[:, C_RT:C_RT + 128], scalar1=vec[:, V_KN:V_KN + 1],
                                              scalar2=None, op0=ALU.mult), reads=[cst, vec], writes=[rtk])
        for d in range(2):
            T.op("dve", lambda e, d=d: e.tensor_tensor(out=lbt[:, d * 16:(d + 1) * 16],
                                                       in0=vec[:, V_LBL + (d * 2) * 16:V_LBL + (d * 2) * 16 + 16],
                                                       in1=vec[:, V_LBL + (d * 2 + 1) * 16:V_LBL + (d * 2 + 1) * 16 + 16],
                                                       op=ALU.subtract), reads=[vec], writes=[lbt])
        T.op("act", lambda e: e.activation(out=lbt[:, 0:32], in_=lbt[:, 0:32], func=AF.Sigmoid), reads=[lbt], writes=[lbt])
        T.op("dve", lambda e: e.tensor_scalar(out=lbt[:, 32:64], in0=lbt[:, 0:32], scalar1=-1.0, scalar2=1.0,
                                              op0=ALU.mult, op1=ALU.add), reads=[lbt], writes=[lbt])
        onesb = cb16[:, C_ONE:C_ONE + 128]
        identb = cb16[:, C_ID:C_ID + 128]

        def mm(out, lhsT, rhs, start, stop, reads, writes, signal):
            T.op("pe", lambda e: e.matmul(out, lhsT, rhs, start=start, stop=stop), reads=reads, writes=writes, signal=signal)

        def act(out, in_, func, reads, writes, **kw):
            T.op("act", lambda e: e.activation(out=out, in_=in_, func=func, **kw), reads=reads, writes=writes)

        def tt(out, in0, in1, op, reads, writes, eng="dve"):
            T.op(eng, lambda e: e.tensor_tensor(out=out, in0=in0, in1=in1, op=op), reads=reads, writes=writes)

        def ts(out, in0, s1, s2, op0, op1, reads, writes, eng="dve"):
            if s2 is None:
                T.op(eng, lambda e: e.tensor_scalar(out=out, in0=in0, scalar1=s1, scalar2=None, op0=op0), reads=reads, writes=writes)
            else:
                T.op(eng, lambda e: e.tensor_scalar(out=out, in0=in0, scalar1=s1, scalar2=s2, op0=op0, op1=op1),
                     reads=reads, writes=writes)

        def stt(out, in0, sc, in1, op0, op1, reads, writes):
            T.op("dve", lambda e: e.scalar_tensor_tensor(out=out, in0=in0, scalar=sc, in1=in1, op0=op0, op1=op1),
                 reads=reads, writes=writes)

        def cp(out, in_, reads, writes, eng="dve"):
            T.op(eng, lambda e: e.tensor_copy(out=out, in_=in_), reads=reads, writes=writes)

        def rows(ap2d, p=128):
            return ap2d.rearrange("(k p) c -> p k c", p=p)

        def load_w(slab_view, w2d, c0, ncols, k0, kc, slab):
            src = rows(w2d)[:, k0:k0 + kc, c0:c0 + ncols]
            h = (kc + 1) // 2
            T.dma("pool", slab_view[:, 0:h, :], src[:, 0:h, :], writes=[slab], sembuf=slab)
            if kc > h:
                T.dma("pool", slab_view[:, h:kc, :], src[:, h:kc, :], writes=[slab], sembuf=slab)

        def rstd_from_ss(dst, ss_ap, ssbuf, tmpbuf, scale, eps, tmp_ap=None):
            ta = tmpbuf[:] if tmp_ap is None else tmp_ap
            ts(ta, ss_ap, scale, eps, ALU.mult, ALU.add, [ssbuf], [tmpbuf])
            act(ta, ta, AF.Ln, [tmpbuf], [tmpbuf])
            act(dst[:], ta, AF.Exp, [tmpbuf], [dst], scale=-0.5)

        xbs = [A.sb("xb%d" % i, [128, 32, NT], BF16) for i in range(2)]
        slabs = [A.sb("ws%d" % i, [128, 32, 256], BF16) for i in range(3)]
        cosbs = [A.sb("cos%d" % i, [128, NT], F32) for i in range(2)]
        sinbs = [A.sb("sin%d" % i, [128, NT], F32) for i in range(2)]
        TMP = [A.sb("tmp%d" % i, [128, NT], F32) for i in range(8)]
        TB = [A.sb("tb%d" % i, [128, NT], BF16) for i in range(8)]
        VT = [A.sb("vt%d" % i, [128, 4, 128], BF16) for i in range(2)]
        ntmp = [0]

        def tmp():
            ntmp[0] += 1
            return TMP[ntmp[0] % len(TMP)]

        ntb = [0]

        def tb():
            ntb[0] += 1
            return TB[ntb[0] % len(TB)]

        it = 0
        for st_ in range(KP1T // 2):
          for half in range(2):
            t0 = (st_ * 2 + half) * NT
            xsrc = rows(xT)[:, :, t0:t0 + NT]
            T.dma("pool", xbs[half][:, 0:16, :], xsrc[:, 0:16, :], writes=[xbs[half]], sembuf=xbs[half])
            T.dma("pool", xbs[half][:, 16:32, :], xsrc[:, 16:32, :], writes=[xbs[half]], sembuf=xbs[half])
            T.dma("sp", cosbs[half][:], cos_d[:, t0:t0 + NT], writes=[cosbs[half]], sembuf=cosbs[half])
            T.dma("sp", sinbs[half][:], sin_d[:, t0:t0 + NT], writes=[sinbs[half]], sembuf=sinbs[half])
          for cbk in range(KP1C):
            if cbk % 2 == 0:
                slab = slabs[(cbk // 2) % 3]
                load_w(slab[:], w_in, cbk * 128, 256, 0, 32, slab)
            for half in range(2):
                ti = st_ * 2 + half
                t0 = ti * NT
                xb, cosb, sinb = xbs[half], cosbs[half], sinbs[half]
                jo = (cbk % 2) * 128
                acc = PSF[it % 3]
                it += 1
                for k in range(32):
                    mm(acc[:], slab[:, k, jo:jo + 128], xb[:, k, :], k == 0, k == 31, [slab, xb], [acc], k == 31)
                if cbk < 20:
                    isq = cbk < 16
                    gcol = V_QN if isq else V_KN
                    rt = rtq if isq else rtk
                    qb_, sq_ = tb(), tb()
                    act(qb_[:], acc[:], AF.Copy, [acc], [qb_])
                    act(sq_[:], acc[:], AF.Square, [acc], [sq_])
                    ss, rq = PSF[3], PSF[4]
                    mm(ss[:], onesb, sq_[:], True, True, [cb16, sq_], [ss], True)
                    mm(rq[:], rt[:], qb_[:], True, True, [rt, qb_], [rq], True)
                    rstd, tl = tmp(), tmp()
                    rstd_from_ss(rstd, ss[:], ss, tl, 1.0 / 128, RMS_EPS)
                    a_, b_ = tmp(), tmp()
                    stt(a_[:], acc[:], vec[:, gcol:gcol + 1], cosb[:], ALU.mult, ALU.mult, [acc, vec, cosb], [a_])
                    tt(b_[:], rq[:], sinb[:], ALU.mult, [rq, sinb], [b_])
                    tt(a_[:], a_[:], b_[:], ALU.add, [a_, b_], [a_])
                    ob = tb()
                    tt(ob[:], a_[:], rstd[:], ALU.mult, [a_, rstd], [ob])
                    if isq:
                        T.dma("sp", QS[cbk * 128:(cbk + 1) * 128, t0:t0 + NT], ob[:], reads=[ob], writes=[QS], sembuf=ob)
                    else:
                        g = cbk - 16
                        T.dma("sp", KCI[g][:, t0:t0 + NT], ob[:], reads=[ob], writes=[KCI[g]], sembuf=ob)
                elif cbk < 24 or 72 <= cbk < 88:
                    vb_ = tb()
                    act(vb_[:], acc[:], AF.Copy, [acc], [vb_])
                    pt = PTB[it % 2]
                    for j in range(4):
                        T.op("pe", lambda e, j=j, pt=pt, vb_=vb_: e.transpose(pt[:, j, :], vb_[:, j * 128:(j + 1) * 128], identb),
                             reads=[vb_, cb16], writes=[pt], signal=(j == 3))
                    vt = VT[it % 2]
                    cp(vt[:], pt[:], [pt], [vt])
                    if cbk < 24:
                        g = cbk - 20
                        dst = rows(VCI[g][:, :])[:, ti * 4:ti * 4 + 4, :]
                        T.dma("sp", dst, vt[:], reads=[vt], writes=[VCI[g]], sembuf=vt)
                    else:
                        h = cbk - 72
                        dst = rows(HV[:, :])[:, ti * 4:ti * 4 + 4, h * 128:(h + 1) * 128]
                        T.dma("sp", dst, vt[:], reads=[vt], writes=[HV], sembuf=vt)
                else:
                    if cbk < 40:
                        dstb, h = HQ, cbk - 24
                    elif cbk < 56:
                        dstb, h = HZF, cbk - 40
                    elif cbk < 72:
                        dstb, h = HZB, cbk - 56
                    else:
                        dstb, h = HG, cbk - 88
                    o_ = tmp()
                    if cbk % 2 == 0:
                        act(o_[:], acc[:], AF.Copy, [acc], [o_])
                    else:
                        cp(o_[:], acc[:], [acc], [o_])
                    T.dma("sp", dstb[h * 128:(h + 1) * 128, t0:t0 + NT], o_[:], reads=[o_], writes=[dstb], sembuf=o_)

        def allgather(src, dst):
            T.custom("pool", lambda e: e.collective_compute("AllGather", ALU.bypass, replica_groups=GROUPS,
                                                            ins=[src.t.opt()], outs=[dst.t.opt()]),
                     dst, 1, reads=[src], writes=[dst])

        if KCOLL:
            for g in range(4):
                allgather(KCI[g], KCO[g])
                allgather(VCI[g], VCO[g])
        T.barrier(A.reset())
        if KSTOP <= 1:
            T.barrier()
            raise _Stop()

        qf = A.sb("qf", [128, TOK], F32)
        zr = [A.sb("zr%d" % i, [128, TOK], F32) for i in range(2)]
        kf = A.sb("kf", [128, TOK], F32)
        Pp = A.sb("Pp", [128, TOK + 64], F32)
        onesf = A.sb("onesf", [128, TOK], F32)
        dt_ = A.sb("dt", [128, TOK], F32)
        et_ = A.sb("et", [128, TOK], F32)
        oacc2 = [A.sb("oacc%d" % i, [128, TOK], F32) for i in range(2)]
        vtok2 = [A.sb("vtok%d" % i, [128, 16, 128], BF16) for i in range(2)]
        qin2 = [[A.sb("qin%d" % i, [128, TOK], BF16) for i in range(2)] for _ in range(2)]
        kin2 = [[A.sb("kin%d" % i, [128, TOK], BF16) for i in range(2)] for _ in range(2)]
        kdec = A.sb("kdec", [128, TOK], BF16)
        kdt2 = [[A.sb("kdt%d" % i, [128, 16, 128], BF16) for i in range(2)] for _ in range(2)]
        qdec2 = [[A.sb("qdec%d" % i, [128, TOK], BF16) for i in range(2)] for _ in range(2)]
        qgl = A.sb("qgl", [128, TOK], BF16)
        dec2 = [[A.sb("dec%d" % i, [128, 32], F32) for i in range(2)] for _ in range(2)]
        dct = A.sb("dct", [128, 32], F32)
        Sf = [A.sb("Sf%d" % i, [128, 128], F32) for i in range(2)]
        Sb_ = [A.sb("Sb%d" % i, [128, 128], BF16) for i in range(2)]
        amt = [A.sb("amt%d" % i, [128, 128], BF16) for i in range(2)]
        T.op("pool", lambda e: e.memset(onesf[:], 1.0), writes=[onesf])
        T.op("pool", lambda e: e.memset(dct[:], 0.0), writes=[dct])

        def bc(ref):
            return ref.unsqueeze(2).to_broadcast([128, 32, 64])

        def v3(ap):
            return ap.rearrange("p (c t) -> p c t", t=64)

        def h1_prep(h):
                hs = slice(h * 128, (h + 1) * 128)
                hp = h % 2
                oacc, vtok, qin, kin, kdt, qdec, dec = oacc2[hp], vtok2[hp], qin2[hp], kin2[hp], kdt2[hp], qdec2[hp], dec2[hp]
                T.dma("sp", qf[:], HQ[hs, :], reads=[HQ], writes=[qf], sembuf=qf)
                yield
                T.dma("sp", zr[0][:], HZF[hs, :], reads=[HZF], writes=[zr[0]], sembuf=zr[0])
                yield
                T.dma("sp", zr[1][:], HZB[hs, :], reads=[HZB], writes=[zr[1]], sembuf=zr[1])
                yield
                T.dma("sp", vtok[:], rows(HV[:, :])[:, :, hs], reads=[HV], writes=[vtok], sembuf=vtok)
                yield
                act(qf[:], qf[:], AF.Silu, [qf], [qf])
                yield
                T.op("pool", lambda e: e.memset(oacc[:], 0.0), writes=[oacc])
                yield
                for d in range(2):
                    sg = 1.0 if d == 0 else -1.0
                    o = 1 if d == 0 else 0
                    z = zr[d]
                    act(kf[:], z[:], AF.Sigmoid, [z], [kf], scale=-1.0)
                    yield
                    ts(kf[:], kf[:], lbt[:, 32 + d * 16 + h:32 + d * 16 + h + 1], None, ALU.mult, None, [kf, lbt], [kf])
                    yield
                    act(z[:], kf[:], AF.Ln, [kf], [z], scale=-1.0, bias=1.0)
                    yield
                    T.op("dve", lambda e: e.memset(Pp[:, 0:1], 0.0), writes=[Pp])
                    yield
                    T.op("dve", lambda e, z=z: e.tensor_tensor_scan(out=Pp[:, 1:TOK + 1], data0=onesf[:], data1=z[:], initial=0.0,
                                                                   op0=ALU.mult, op1=ALU.add), reads=[onesf, z], writes=[Pp])
                    yield
                    E3 = v3(Pp[:, o:o + TOK])
                    Mr = Pp[:, 32:TOK:64]
                    Lr = Pp[:, 64:TOK + 1:64] if d == 0 else Pp[:, 0:TOK:64]
                    Vr = Pp[:, 0:TOK:64] if d == 0 else Pp[:, 64:TOK + 1:64]
                    tt(v3(dt_[:]), E3, bc(Mr), ALU.subtract, [Pp], [dt_])
                    yield
                    act(et_[:], dt_[:], AF.Exp, [dt_], [et_], scale=sg)
                    yield
                    tt(qin[d][:], qf[:], et_[:], ALU.mult, [qf, et_], [qin[d]])
                    yield
                    act(et_[:], dt_[:], AF.Exp, [dt_], [et_], scale=-sg)
                    yield
                    tt(kin[d][:], kf[:], et_[:], ALU.mult, [kf, et_], [kin[d]])
                    yield
                    tt(v3(dt_[:]), E3, bc(Lr), ALU.subtract, [Pp], [dt_])
                    yield
                    act(et_[:], dt_[:], AF.Exp, [dt_], [et_], scale=-sg)
                    yield
                    tt(kdec[:], kf[:], et_[:], ALU.mult, [kf, et_], [kdec])
                    yield
                    tt(v3(dt_[:]), E3, bc(Vr), ALU.subtract, [Pp], [dt_])
                    yield
                    act(et_[:], dt_[:], AF.Exp, [dt_], [et_], scale=sg)
                    yield
                    tt(qdec[d][:], qf[:], et_[:], ALU.mult, [qf, et_], [qdec[d]])
                    yield
                    if d == 0:
                        act(et_[:], Pp[:, 1:TOK + 1], AF.Exp, [Pp], [et_])
                        yield
                    else:
                        act(et_[:], Pp[:, 0:TOK], AF.Exp, [Pp], [et_], scale=-1.0, bias=Pp[:, TOK:TOK + 1])
                        yield
                    tt(qgl[:], qf[:], et_[:], ALU.mult, [qf, et_], [qgl])
                    yield
                    T.dma("sp", (QGF if d == 0 else QGB)[hs, :], qgl[:], reads=[qgl], writes=[QGF if d == 0 else QGB], sembuf=qgl)
                    yield
                    tt(dec[d][:], Pp[:, 64:TOK + 1:64], Pp[:, 0:TOK:64], ALU.subtract, [Pp], [dec[d]])
                    yield
                    act(dec[d][:], dec[d][:], AF.Exp, [dec[d]], [dec[d]])
                    yield
                    act(dct[:, h * 2 + d:h * 2 + d + 1], Pp[:, TOK:TOK + 1], AF.Exp, [Pp], [dct])
                    yield
                    for j4 in range(4):
                        pt = PTB[j4 % 2]
                        for j in range(4):
                            jj = j4 * 4 + j
                            T.op("pe", lambda e, j=j, jj=jj, pt=pt: e.transpose(pt[:, j, :], kdec[:, jj * 128:(jj + 1) * 128], identb),
                                 reads=[kdec, cb16], writes=[pt], signal=(j == 3))
                            yield
                        cp(kdt[d][:, j4 * 4:j4 * 4 + 4, :], pt[:], [pt], [kdt[d]])
                        yield
                yield

        def h1_loop(h):
                hs = slice(h * 128, (h + 1) * 128)
                hp = h % 2
                oacc, vtok, qin, kin, kdt, qdec, dec = oacc2[hp], vtok2[hp], qin2[hp], kin2[hp], kdt2[hp], qdec2[hp], dec2[hp]
                for d in range(2):
                    T.op("pool", lambda e, d=d: e.memset(Sf[d][:], 0.0), writes=[Sf[d]])
                    T.op("pool", lambda e, d=d: e.memset(Sb_[d][:], 0.0), writes=[Sb_[d]])
                for step in range(16):
                    for d in range(2):
                        blk = step if d == 0 else 15 - step
                        bs = slice(blk * 128, (blk + 1) * 128)
                        mk = cst[:, C_MF:C_MF + 128] if d == 0 else cst[:, C_MB:C_MB + 128]
                        AT = PSF[d]
                        oI = PSF[2 + d]
                        oA = PSF[4 + d]
                        Uv = AT[:, 256:384]
                        mm(AT[:, 0:128], kin[d][:, bs], qin[d][:, bs], True, True, [kin[d], qin[d]], [AT], True)
                        tt(amt[d][:], AT[:, 0:128], mk, ALU.mult, [AT, cst], [amt[d]])
                        for ci in ((0, 1) if d == 0 else (1, 0)):
                            c = blk * 2 + ci
                            cs_ = slice(blk * 128 + ci * 64, blk * 128 + ci * 64 + 64)
                            ps_ = slice(ci * 64, ci * 64 + 64)
                            mm(oI[:, ci * 64:ci * 64 + 64], Sb_[d][:], qdec[d][:, cs_], True, True, [Sb_[d], qdec[d]], [oI], True)
                            mm(Uv, kdt[d][ps_, blk, :], vtok[ps_, blk, :], True, True, [kdt[d], vtok], [AT], True)
                            stt(Sf[d][:], Sf[d][:], dec[d][:, c:c + 1], Uv, ALU.mult, ALU.add, [Sf[d], dec[d], AT], [Sf[d]])
                            act(Sb_[d][:], Sf[d][:], AF.Copy, [Sf[d]], [Sb_[d]])
                        mm(oA[:, 0:128], vtok[:, blk, :], amt[d][:], True, True, [vtok, amt[d]], [oA], True)
                        tt(oacc[:, bs], oacc[:, bs], oA[:, 0:128], ALU.add, [oacc, oA], [oacc])
                        tt(oacc[:, bs], oacc[:, bs], oI[:, 0:128], ALU.add, [oacc, oI], [oacc])
                        yield
                for d in range(2):
                    hd = h * 2 + d
                    T.dma("sp", SCI[hd // 8][(hd % 8) * 128:(hd % 8 + 1) * 128, :], Sf[d][:], reads=[Sf[d]], writes=[SCI[hd // 8]],
                          sembuf=Sf[d])
                T.dma("sp", OLOC[hs, :], oacc[:], reads=[oacc], writes=[OLOC], sembuf=oacc)

                yield

        def run_some(gen, n):
            if gen is None:
                return None
            for _ in range(n):
                try:
                    next(gen)
                except StopIteration:
                    return None
            return gen

        g = h1_prep(0)
        while g is not None:
            g = run_some(g, 1000)
        for h in range(KH):
            lp = h1_loop(h)
            pp = h1_prep(h + 1) if h + 1 < KH else None
            while lp is not None or pp is not None:
                lp = run_some(lp, 1)
                pp = run_some(pp, 3)
        T.dma("sp", DCI[:, :], dct[:], reads=[dct], writes=[DCI], sembuf=dct)
        for i in range(4):
            allgather(SCI[i], SCO[i])
        allgather(DCI, DCO)
        T.barrier(A.reset())
        if KSTOP <= 2:
            T.barrier()
            raise _Stop()

        if KSTOP > 3:
            for i in range(16):
                T.dma("pool", WAT[i][:, :].rearrange("p (k c) -> p k c", c=256), rows(w_pa)[:, :, i * 256:(i + 1) * 256],
                      writes=[WAT[i]], sembuf=convsem)
                T.dma("pool", WBT[i][:, :].rearrange("p (k c) -> p k c", c=256), rows(w_pb)[:, :, i * 256:(i + 1) * 256],
                      writes=[WBT[i]], sembuf=convsem)
                for j in range(2):
                    T.dma("pool", WGT[i][j][:, :].rearrange("p (k c) -> p k c", c=256),
                          rows(w_gate)[:, :, j * D + i * 256:j * D + (i + 1) * 256], writes=[WGT[i][j]], sembuf=convsem)
            for i in range(16):
                T.dma("pool", WOT[i][:, :].rearrange("p (k c) -> p k c", c=256), rows(w_o)[:, :, i * 256:(i + 1) * 256],
                      writes=[WOT[i]], sembuf=convsem)
            for i in range(16):
                for q, (k0, kq) in enumerate(QK4):
                    T.dma("pool", WDT[i][q][:, 0:kq * 256].rearrange("p (k c) -> p k c", c=256),
                          rows(w_down)[:, k0:k0 + kq, i * 256:(i + 1) * 256], writes=[WDT[i][q]], sembuf=convsem)
            for i in range(16):
                T.dma("pool", WPT[i][:, :].rearrange("p (k c) -> p k c", c=256), rows(w_pg)[:, :, i * 256:(i + 1) * 256],
                      writes=[WPT[i]], sembuf=convsem)
        KT2 = [A.sb("KT%d" % i, [128, 4 * TOK], BF16) for i in range(2)]
        VG2 = [A.sb("VG%d" % i, [128, 64, 128], BF16) for i in range(2)]

        def load_kv(g):
            KT, VG = KT2[g % 2], VG2[g % 2]
            for r in range(4):
                T.dma("sp", KT[:, r * TOK:(r + 1) * TOK], KCO[g][r * 128:(r + 1) * 128, :],
                      reads=[KCO[g]], writes=[KT], sembuf=KT)
                T.dma("sp", VG[:, r * 16:(r + 1) * 16, :], rows(VCO[g][:, :])[:, r * 16:(r + 1) * 16, :],
                      reads=[VCO[g]], writes=[VG], sembuf=VG)
        qT = [A.sb("qT%d" % i, [128, NT], BF16) for i in range(2)]
        pTs = [A.sb("pT%d" % i, [128, NT], BF16) for i in range(6)]
        rl = [A.sb("rl%d" % i, [128, NT], F32) for i in range(2)]
        yo = [A.sb("yo%d" % i, [128, NT], BF16) for i in range(2)]
        accD = [A.sb("accD%d" % i, [128, NT], F32) for i in range(2)]
        accP = [A.sb("accP%d" % i, [128, NT], F32) for i in range(2)]
        lhi = [A.sb("lhi%d" % i, [128, NT], BF16) for i in range(2)]
        llo = [A.sb("llo%d" % i, [128, NT], BF16) for i in range(2)]
        LA = 2
        u = 0
        load_kv(0)
        for g in range(KG):
            KT, VG = KT2[g % 2], VG2[g % 2]
            if g + 1 < KG:
                load_kv(g + 1)
            for qt in range(KQT):
                for hh in range(4):
                    h = g * 4 + hh
                    q_ = qT[u % 2]
                    T.dma("sp", q_[:], QS[h * 128:(h + 1) * 128, qt * NT:(qt + 1) * NT], reads=[QS], writes=[q_], sembuf=q_)
                    oT = PSF[4 + u % 2]
                    lT = PSF[3]
                    aD, aP = accD[u % 2], accP[u % 2]

                    def qk(kb, q_=q_):
                        sT_ = PSF[kb % 3]
                        mm(sT_[:], KT[:, kb * 128:(kb + 1) * 128], q_[:], True, True, [KT, q_], [sT_], True)

                    for kb in range(LA):
                        qk(kb)
                    for kb in range(64):
                        if kb + LA < 64:
                            qk(kb + LA)
                        sT = PSF[kb % 3]
                        p_ = pTs[kb % 6]
                        act(p_[:], sT[:], AF.Exp, [sT], [p_], scale=SCALE, bias=ESHIFT)
                        mm(oT[:], VG[:, kb, :], p_[:], kb == 0, kb == 63, [VG, p_], [oT], True)
                        if kb % 2 == 0:
                            mm(lT[:], onesb, p_[:], kb == 0, False, [cb16, p_], [lT], True)
                        elif kb == 1:
                            cp(aD[:], p_[:], [p_], [aD])
                        else:
                            tt(aD[:], aD[:], p_[:], ALU.add, [aD, p_], [aD])
                    hi_, lo_ = lhi[u % 2], llo[u % 2]
                    act(hi_[:], aD[:], AF.Copy, [aD], [hi_])
                    tt(lo_[:], aD[:], hi_[:], ALU.subtract, [aD, hi_], [lo_])
                    mm(lT[:], onesb, hi_[:], False, False, [cb16, hi_], [lT], False)
                    mm(lT[:], onesb, lo_[:], False, True, [cb16, hi_, lo_], [lT], True)
                    r_ = rl[u % 2]
                    y_ = yo[u % 2]
                    T.op("dve", lambda e, r_=r_, lT=lT: e.reciprocal(out=r_[:], in_=lT[:]), reads=[lT], writes=[r_])
                    tt(y_[:], oT[:], r_[:], ALU.mult, [oT, r_], [y_])
                    T.dma("sp", YA[h * 128:(h + 1) * 128, qt * NT:(qt + 1) * NT], y_[:], reads=[y_], writes=[YA], sembuf=y_)
                    u += 1
        T.barrier(A.reset())
        if KSTOP <= 3:
            T.barrier()
            raise _Stop()

        Dall = A.sb("Dall", [128, 4, 32], F32)
        Dm = [A.sb("Dm%d" % i, [128, 4, 32], F32) for i in range(2)]
        Ur = [A.sb("Ur%d" % i, [128, 4, 128], F32) for i in range(2)]
        Sacc = A.sb("Sacc", [128, 128], F32)
        Sinb = A.sb("Sinb", [128, 32, 128], BF16)
        oloc = A.sb("oloc", [128, TOK], F32)
        ghs = A.sb("ghs", [128, TOK], F32)
        qg = [A.sb("qg%d" % i, [128, TOK], BF16) for i in range(2)]
        TMP = [A.sb("tmp%d" % i, [128, NT], F32) for i in range(6)]
        TB = [A.sb("tb%d" % i, [128, NT], BF16) for i in range(4)]
        T.dma("sp", Dall[:], DCO[:, :].rearrange("(r p) c -> p r c", p=128), reads=[DCO], writes=[Dall], sembuf=Dall)
        for d in range(2):
            mcol = V_MF if d == 0 else V_MB
            ocol = V_OMF if d == 0 else V_OMB
            for r in range(4):
                ts(Dm[d][:, r, :], Dall[:, r, :], vec[:, mcol + r:mcol + r + 1], vec[:, ocol + r:ocol + r + 1],
                   ALU.mult, ALU.add, [Dall, vec], [Dm[d]])
        SCO4 = [SCO[i][:, :].rearrange("(r x p) e -> p r x e", r=4, p=128) for i in range(4)]
        for hd in range(2 * KH):
            d = hd % 2
            mcol = V_MF if d == 0 else V_MB
            ur = Ur[hd % 2]
            T.dma("sp", ur[:], SCO4[hd // 8][:, :, hd % 8, :], reads=[SCO[hd // 8]], writes=[ur], sembuf=ur)
            T.op("pool", lambda e: e.memset(Sacc[:], 0.0), writes=[Sacc])
            for r in ((0, 1, 2, 3) if d == 0 else (3, 2, 1, 0)):
                ts(Sacc[:], Sacc[:], Dm[d][:, r, hd:hd + 1], None, ALU.mult, None, [Sacc, Dm[d]], [Sacc])
                stt(Sacc[:], ur[:, r, :], vec[:, mcol + r:mcol + r + 1], Sacc[:], ALU.mult, ALU.add, [ur, vec, Sacc], [Sacc])
            cp(Sinb[:, hd, :], Sacc[:], [Sacc], [Sinb])
        for h in range(KH):
            hs = slice(h * 128, (h + 1) * 128)
            T.dma("sp", oloc[:], OLOC[hs, :], reads=[OLOC], writes=[oloc], sembuf=oloc)
            T.dma("sp", ghs[:], HG[hs, :], reads=[HG], writes=[ghs], sembuf=ghs)
            T.dma("sp", qg[0][:], QGF[hs, :], reads=[QGF], writes=[qg[0]], sembuf=qg[0])
            T.dma("sp", qg[1][:], QGB[hs, :], reads=[QGB], writes=[qg[1]], sembuf=qg[1])
            act(ghs[:], ghs[:], AF.Silu, [ghs], [ghs])
            for ti in range(NTT):
                tsl = slice(ti * NT, (ti + 1) * NT)
                cps = PSF[ti % 2]
                mm(cps[:], Sinb[:, 2 * h, :], qg[0][:, tsl], True, False, [Sinb, qg[0]], [cps], False)
                mm(cps[:], Sinb[:, 2 * h + 1, :], qg[1][:, tsl], False, True, [Sinb, qg[0], qg[1]], [cps], True)
                o_ = TMP[(ti * 3) % 6]
                tt(o_[:], oloc[:, tsl], cps[:], ALU.add, [oloc, cps], [o_])
                sq_ = TB[(ti * 2) % 4]
                act(sq_[:], o_[:], AF.Square, [o_], [sq_])
                ss = PSF[2 + ti % 2]
                mm(ss[:], onesb, sq_[:], True, True, [cb16, sq_], [ss], True)
                rstd, tl = TMP[(ti * 3 + 1) % 6], TMP[(ti * 3 + 2) % 6]
                rstd_from_ss(rstd, ss[:], ss, tl, 1.0 / 128, RMS_EPS)
                stt(o_[:], o_[:], vec[:, V_HGN + h:V_HGN + h + 1], rstd[:], ALU.mult, ALU.mult, [o_, vec, rstd], [o_])
                yb = TB[(ti * 2 + 1) % 4]
                tt(yb[:], o_[:], ghs[:, tsl], ALU.mult, [o_, ghs], [yb])
                T.dma("sp", YH[hs, tsl], yb[:], reads=[yb], writes=[YH], sembuf=yb)
        T.barrier(A.reset())
        if KSTOP <= 4:
            T.barrier()
            raise _Stop()

        def stat_mm(s1, s2, item):
            cb, rb, rsq = item
            mm(s1[:], onesb, rb[:], cb == 0, cb == KCB - 1, [cb16, rb], [s1], True)
            mm(s2[:], onesb, rsq[:], cb == 0, cb == KCB - 1, [cb16, rsq], [s2], True)

        def ln_finish(s1, s2, mean, rstd, nmr, eps):
            ts(mean[:], s1[:], 1.0 / D, None, ALU.mult, None, [s1], [mean])
            tt(nmr[:], mean[:], mean[:], ALU.mult, [mean], [nmr])
            stt(rstd[:], s2[:], 1.0 / D, nmr[:], ALU.mult, ALU.subtract, [s2, nmr], [rstd])
            ts(rstd[:], rstd[:], eps, None, ALU.add, None, [rstd], [rstd])
            act(rstd[:], rstd[:], AF.Ln, [rstd], [rstd])
            act(rstd[:], rstd[:], AF.Exp, [rstd], [rstd], scale=-0.5)
            stt(nmr[:], mean[:], -1.0, rstd[:], ALU.mult, ALU.mult, [mean, rstd], [nmr])

        for ti in range(KT4):
            t0 = ti * NT
            ya = A.sb("ya", [128, 16, NT], BF16)
            yh = A.sb("yh", [128, 16, NT], BF16)
            xb = A.sb("xb", [128, 32, NT], BF16)
            mT = A.sb("mT", [128, 32, NT], BF16)
            off_keep = A.off
            slabs = [A.sb("ws%d" % i, [128, 96, 256], BF16) for i in range(2)]
            TMP = [A.sb("tmp%d" % i, [128, NT], F32) for i in range(2)]
            T.dma("sp", ya[:], rows(YA[:, :])[:, :, t0:t0 + NT], reads=[YA], writes=[ya], sembuf=ya)
            T.dma("sp", yh[:], rows(YH[:, :])[:, :, t0:t0 + NT], reads=[YH], writes=[yh], sembuf=yh)
            xsrc = rows(xT)[:, :, t0:t0 + NT]
            T.dma("pool", xb[:, 0:16, :], xsrc[:, 0:16, :], writes=[xb], sembuf=xb)
            T.dma("pool", xb[:, 16:32, :], xsrc[:, 16:32, :], writes=[xb], sembuf=xb)
            for cb in range(KCB):
                if cb % 2 == 0:
                    slab = slabs[(cb // 2) % 2]
                    i2 = cb // 2
                    T.dma("sp", slab[:, 0:16, :].rearrange("p k c -> p (k c)"), WAT[i2][:, :], reads=[WAT[i2]], writes=[slab], sembuf=slab)
                    T.dma("sp", slab[:, 16:32, :].rearrange("p k c -> p (k c)"), WBT[i2][:, :], reads=[WBT[i2]], writes=[slab], sembuf=slab)
                    T.dma("sp", slab[:, 32:64, :].rearrange("p k c -> p (k c)"), WGT[i2][0][:, :], reads=[WGT[i2][0]], writes=[slab],
                          sembuf=slab)
                    T.dma("sp", slab[:, 64:96, :].rearrange("p k c -> p (k c)"), WGT[i2][1][:, :], reads=[WGT[i2][1]], writes=[slab],
                          sembuf=slab)
                jo = (cb % 2) * 128
                js = slice(jo, jo + 128)
                pa, pb, ga, gh_ = PSF[0], PSF[1], PSF[2], PSF[3]
                for k in range(16):
                    mm(pa[:], slab[:, k, js], ya[:, k, :], k == 0, k == 15, [slab, ya], [pa], k == 15)
                for k in range(16):
                    mm(pb[:], slab[:, 16 + k, js], yh[:, k, :], k == 0, k == 15, [slab, yh], [pb], k == 15)
                for k in range(32):
                    mm(ga[:], slab[:, 32 + k, js], xb[:, k, :], k == 0, k == 31, [slab, xb], [ga], k == 31)
                for k in range(32):
                    mm(gh_[:], slab[:, 64 + k, js], xb[:, k, :], k == 0, k == 31, [slab, xb], [gh_], k == 31)
                sa, sh = TMP[0], TMP[1]
                act(sa[:], ga[:], AF.Sigmoid, [ga, vec], [sa], bias=vec[:, V_BG + cb:V_BG + cb + 1])
                act(sh[:], gh_[:], AF.Sigmoid, [gh_, vec], [sh], bias=vec[:, V_BG + 32 + cb:V_BG + 32 + cb + 1])
                tt(sa[:], sa[:], pa[:], ALU.mult, [sa, pa], [sa])
                tt(sh[:], sh[:], pb[:], ALU.mult, [sh, pb], [sh])
                tt(mT[:, cb, :], sa[:], sh[:], ALU.add, [sa, sh], [mT])
            T.barrier()
            A.off = A.base
            rT = A.sb("rT", [128, 32, NT], F32)
            assert A.off <= off_keep - 32 * NT * 2
            mT2 = mT
            A.off = off_keep
            slabs = [A.sb("wo%d" % i, [128, 32, 256], BF16) for i in range(3)]
            TMP = [A.sb("tq%d" % i, [128, NT], F32) for i in range(6)]
            TB = [A.sb("tbq%d" % i, [128, NT], BF16) for i in range(4)]
            s1, s2 = PSF[4], PSF[5]
            pend = []
            for cb in range(KCB):
                if cb % 2 == 0:
                    slab = slabs[(cb // 2) % 3]
                    T.dma("sp", slab[:].rearrange("p k c -> p (k c)"), WOT[cb // 2][:, :], reads=[WOT[cb // 2]], writes=[slab], sembuf=slab)
                js = slice((cb % 2) * 128, (cb % 2) * 128 + 128)
                acc = PSF[cb % 2]
                xf = TMP[cb % 2]
                T.dma("sp", xf[:], xT[cb * 128:(cb + 1) * 128, t0:t0 + NT], writes=[xf], sembuf=xf)
                for k in range(32):
                    mm(acc[:], slab[:, k, js], mT2[:, k, :], k == 0, k == 31, [slab, mT2], [acc], k == 31)
                while len(pend) > 1:
                    stat_mm(s1, s2, pend.pop(0))
                stt(rT[:, cb, :], xf[:], ALPHA, acc[:], ALU.mult, ALU.add, [xf, acc], [rT])
                rb, rsq = TB[(cb * 2) % 4], TB[(cb * 2 + 1) % 4]
                act(rb[:], rT[:, cb, :], AF.Copy, [rT], [rb])
                act(rsq[:], rT[:, cb, :], AF.Square, [rT], [rsq])
                pend.append((cb, rb, rsq))
            while pend:
                stat_mm(s1, s2, pend.pop(0))
            mean, rstd, nmr = TMP[2], TMP[3], TMP[4]
            ln_finish(s1, s2, mean, rstd, nmr, LN_EPS)
            hbufs = [TMP[0], TMP[1], TMP[5]]
            for cb in range(KCB):
                hb = hbufs[cb % 3]
                tt(hb[:], rT[:, cb, :], rstd[:], ALU.mult, [rT, rstd], [hb])
                tt(hb[:], hb[:], nmr[:], ALU.add, [hb, nmr], [hb])
                ts(hb[:], hb[:], vec[:, V_L1G + cb:V_L1G + cb + 1], vec[:, V_L1B + cb:V_L1B + cb + 1], ALU.mult, ALU.add,
                   [hb, vec], [hb])
                T.dma("sp", H1[cb * 128:(cb + 1) * 128, 1 + t0:1 + t0 + NT], hb[:], reads=[hb], writes=[H1], sembuf=hb)
                if ti == 0:
                    T.dma("sp", EDI[cb * 128:(cb + 1) * 128, 0:1], hb[:, 0:1], reads=[hb], writes=[EDI], sembuf=hb)
                if ti == NTT - 1:
                    T.dma("sp", EDI[cb * 128:(cb + 1) * 128, 1:2], hb[:, NT - 1:NT], reads=[hb], writes=[EDI], sembuf=hb)
            T.barrier(A.reset())

        allgather(EDI, EDO)
        Eg = A.sb("Eg", [128, 4, 32, 2], F32)
        hal = A.sb("hal", [128, 32, 2], F32)
        T.dma("sp", Eg[:], EDO[:, :].rearrange("(r k p) c -> p r k c", r=4, p=128), reads=[EDO], writes=[Eg], sembuf=Eg)
        T.op("pool", lambda e: e.memset(hal[:], 0.0), writes=[hal])
        for r in range(4):
            stt(hal[:, :, 0], Eg[:, r, :, 1], vec[:, V_ML + r:V_ML + r + 1], hal[:, :, 0], ALU.mult, ALU.add, [Eg, vec, hal], [hal])
            stt(hal[:, :, 1], Eg[:, r, :, 0], vec[:, V_MR + r:V_MR + r + 1], hal[:, :, 1], ALU.mult, ALU.add, [Eg, vec, hal], [hal])
        H1r = rows(H1[:, :])
        T.dma("sp", H1r[:, :, 0:1], hal[:, :, 0:1], reads=[hal], writes=[H1], sembuf=hal)
        T.dma("sp", H1r[:, :, TOK + 1:TOK + 2], hal[:, :, 1:2], reads=[hal], writes=[H1], sembuf=hal)
        T.barrier(A.reset())
        if KSTOP <= 5:
            T.barrier()
            raise _Stop()

        for ti in range(KT5):
            t0 = ti * NT
            aT = A.sb("aT", [128, NFB, NT], BF16)
            off_keep = A.off
            h1b = A.sb("h1b", [128, 32, NT + 2], BF16)
            slabs = [A.sb("wu%d" % i, [128, 64, 256], BF16) for i in range(2)]
            gsb = [A.sb("gsb%d" % i, [128, NT + 2], F32) for i in range(2)]
            TMP = [A.sb("tmp%d" % i, [128, NT], F32) for i in range(4)]
            hsrc = H1r[:, :, t0:t0 + NT + 2]
            T.dma("pool", h1b[:, 0:16, :], hsrc[:, 0:16, :], reads=[H1], writes=[h1b], sembuf=h1b)
            T.dma("pool", h1b[:, 16:32, :], hsrc[:, 16:32, :], reads=[H1], writes=[h1b], sembuf=h1b)
            for cb in range(KFB):
                if cb % 2 == 0:
                    slab = slabs[(cb // 2) % 2]
                    load_w(slab[:, 0:32, :], w_up, cb * 128, 256, 0, 32, slab)
                    load_w(slab[:, 32:64, :], w_up, DFF + cb * 128, 256, 0, 32, slab)
                js = slice((cb % 2) * 128, (cb % 2) * 128 + 128)
                up, gp, gh_ = PSF[cb % 2], PSF[2 + cb % 2], PSF[4 + cb % 2]
                for k in range(32):
                    mm(up[:], slab[:, k, js], h1b[:, k, 1:NT + 1], k == 0, k == 31, [slab, h1b], [up], k == 31)
                for k in range(32):
                    mm(gp[:], slab[:, 32 + k, js], h1b[:, k, 1:NT + 1], k == 0, k == 31, [slab, h1b], [gp], k == 31)
                for k in range(32):
                    mm(gh_[:, 0:2], slab[:, 32 + k, js], h1b[:, k, 0:NT + 2:NT + 1], k == 0, k == 31, [slab, h1b], [gh_], k == 31)
                gs = gsb[cb % 2]
                act(gs[:, 1:NT + 1], gp[:], AF.Copy, [gp], [gs])
                cp(gs[:, 0:NT + 2:NT + 1], gh_[:, 0:2], [gh_], [gs])
                c_ = TMP[cb % 2]
                ts(c_[:], gs[:, 0:NT], vec[:, V_CW + cb:V_CW + cb + 1], vec[:, V_CB + cb:V_CB + cb + 1], ALU.mult, ALU.add,
                   [gs, vec], [c_])
                stt(c_[:], gs[:, 1:NT + 1], vec[:, V_CW + NFB + cb:V_CW + NFB + cb + 1], c_[:], ALU.mult, ALU.add, [gs, vec, c_], [c_])
                stt(c_[:], gs[:, 2:NT + 2], vec[:, V_CW + 2 * NFB + cb:V_CW + 2 * NFB + cb + 1], c_[:], ALU.mult, ALU.add,
                    [gs, vec, c_], [c_])
                act(c_[:], c_[:], AF.Silu, [c_], [c_])
                tt(aT[:, cb, :], c_[:], up[:], ALU.mult, [c_, up], [aT])
            T.barrier()
            A.off = off_keep
            rT = A.sb("rT", [128, 32, NT], F32)
            TMP = [A.sb("tq%d" % i, [128, NT], F32) for i in range(5)]
            TB = [A.sb("tbq%d" % i, [128, NT], BF16) for i in range(4)]
            off_slabs = A.off
            slabs = [A.sb("wd%d" % i, [128, 22, 256], BF16) for i in range(3)]
            s1, s2 = PSF[4], PSF[5]
            pend = []
            for cb2 in range(KCB // 2):
                accs = [PSF[(cb2 % 2) * 2], PSF[(cb2 % 2) * 2 + 1]]
                for q, (k0, kq) in enumerate(QK4):
                    slab = slabs[(cb2 * 4 + q) % 3]
                    T.dma("sp", slab[:, 0:kq, :].rearrange("p k c -> p (k c)"), WDT[cb2][q][:, 0:kq * 256], reads=[WDT[cb2][q]],
                          writes=[slab], sembuf=slab)
                    for j in range(2):
                        for k in range(kq):
                            mm(accs[j][:], slab[:, k, j * 128:(j + 1) * 128], aT[:, k0 + k, :], q == 0 and k == 0,
                               q == 3 and k == kq - 1, [slab, aT], [accs[j]], k == kq - 1)
                while pend:
                    stat_mm(s1, s2, pend.pop(0))
                for j in range(2):
                    cb = cb2 * 2 + j
                    acc = accs[j]
                    hf = TMP[cb % 2]
                    T.dma("sp", hf[:], H1[cb * 128:(cb + 1) * 128, 1 + t0:1 + t0 + NT], reads=[H1], writes=[hf], sembuf=hf)
                    stt(rT[:, cb, :], hf[:], ALPHA, acc[:], ALU.mult, ALU.add, [hf, acc], [rT])
                    rb, rsq = TB[(cb * 2) % 4], TB[(cb * 2 + 1) % 4]
                    act(rb[:], rT[:, cb, :], AF.Copy, [rT], [rb])
                    act(rsq[:], rT[:, cb, :], AF.Square, [rT], [rsq])
                    pend.append((cb, rb, rsq))
            while pend:
                stat_mm(s1, s2, pend.pop(0))
            mean, rstd, nmr = TMP[2], TMP[3], TMP[4]
            ln_finish(s1, s2, mean, rstd, nmr, LN_EPS)
            T.barrier()
            A.off = A.base
            x2b = A.sb("x2b", [128, 32, NT], BF16)
            pTb = A.sb("pTb", [128, 2, NT], BF16)
            wple = A.sb("wple", [128, 2, D], BF16)
            assert A.off <= off_keep
            A.off = off_slabs
            slabs = [A.sb("wg%d" % i, [128, 32, 256], BF16) for i in range(2)]
            T.dma("pool", pTb[:], rows(pT)[:, :, t0:t0 + NT], writes=[pTb], sembuf=pTb)
            T.dma("pool", wple[:], rows(w_ple)[:, :, :], writes=[wple], sembuf=wple)
            for cb in range(KCB):
                r_ = rT[:, cb, :]
                tt(r_, r_, rstd[:], ALU.mult, [rT, rstd], [rT])
                tt(r_, r_, nmr[:], ALU.add, [rT, nmr], [rT])
                ts(r_, r_, vec[:, V_L2G + cb:V_L2G + cb + 1], vec[:, V_L2B + cb:V_L2B + cb + 1], ALU.mult, ALU.add, [rT, vec], [rT])
                act(x2b[:, cb, :], r_, AF.Copy, [rT], [x2b])
            for cb in range(KCB):
                if cb % 2 == 0:
                    slab = slabs[(cb // 2) % 2]
                    T.dma("sp", slab[:].rearrange("p k c -> p (k c)"), WPT[cb // 2][:, :], reads=[WPT[cb // 2]], writes=[slab], sembuf=slab)
                js = slice((cb % 2) * 128, (cb % 2) * 128 + 128)
                pg, pl = PSF[cb % 2], PSF[2 + cb % 2]
                for k in range(32):
                    mm(pg[:], slab[:, k, js], x2b[:, k, :], k == 0, k == 31, [slab, x2b], [pg], k == 31)
                for k in range(2):
                    mm(pl[:], wple[:, k, cb * 128:(cb + 1) * 128], pTb[:, k, :], k == 0, k == 1, [wple, pTb], [pl], k == 1)
                s_ = TMP[cb % 2]
                act(s_[:], pg[:], AF.Sigmoid, [pg], [s_])
                tt(s_[:], s_[:], pl[:], ALU.mult, [s_, pl], [s_])
                tt(s_[:], s_[:], rT[:, cb, :], ALU.add, [s_, rT], [s_])
                T.dma("sp", outT[cb * 128:(cb + 1) * 128, t0:t0 + NT], s_[:], reads=[s_], sembuf=s_)
            T.barrier(A.reset())
        T.barrier()
        print("kernel build: ninst=%d nwaits=%d dma_sems=%d" % (T.ninst, T.nwaits, len(T.dma_sems)), flush=True)
    return nc


def _consts():
    i = np.arange(128)
    R = np.zeros((128, 128), np.float32)
    for a in range(128):
        sec = a // 64
        loc = a % 64
        if loc < 32:
            R[a, sec * 64 + loc + 32] = -1.0
        else:
            R[a, sec * 64 + loc - 32] = 1.0
    RT = R.T.copy()
    ident = np.eye(128, dtype=np.float32)
    ones = np.ones((128, 128), np.float32)
    s = i[:, None]
    t = i[None, :]
    same = (s // 64) == (t // 64)
    maskF = (same & (s <= t)).astype(np.float32)
    maskB = (same & (s >= t)).astype(np.float32)
    return np.concatenate([RT, ident, ones, maskF, maskB], axis=1).astype(np.float32)


def _rope_tables(tok0):
    t = np.arange(tok0, tok0 + TOK)
    row = (t // 64).astype(np.float32)
    col = (t % 64).astype(np.float32)
    sec = 64
    inv = (10000.0 ** (-np.arange(0, sec, 2, dtype=np.float32) / sec)).astype(np.float32)
    ang_r = row[:, None] * inv[None, :]
    ang_c = col[:, None] * inv[None, :]
    ang = np.concatenate([ang_r, ang_r, ang_c, ang_c], axis=-1).astype(np.float32)
    return np.ascontiguousarray(np.cos(ang).T.astype(np.float32)), np.ascontiguousarray(np.sin(ang).T.astype(np.float32))


_NC_CACHE = {}


def kernel(x, p, w_in, q_norm, k_norm, lb_logits, hg_norm, w_pa, w_pb, w_gate, b_gate, w_o,
           ln1_g, ln1_b, w_up, conv_w, conv_b, w_down, ln2_g, ln2_b, w_pg, w_ple):
    f = lambda a: np.ascontiguousarray(np.asarray(a, dtype=np.float32))
    x = f(x); p = f(p)
    col = lambda v, n: f(v).reshape(n, 128).T
    vec = np.zeros((128, NV), np.float32)
    vec[:, V_BG:V_BG + 64] = col(b_gate[0], 64)
    vec[:, V_L1G:V_L1G + 32] = col(ln1_g[0], 32)
    vec[:, V_L1B:V_L1B + 32] = col(ln1_b[0], 32)
    vec[:, V_L2G:V_L2G + 32] = col(ln2_g[0], 32)
    vec[:, V_L2B:V_L2B + 32] = col(ln2_b[0], 32)
    cw = f(conv_w)[0]
    for tap in range(3):
        vec[:, V_CW + tap * NFB:V_CW + (tap + 1) * NFB] = col(cw[tap], NFB)
    vec[:, V_CB:V_CB + NFB] = col(conv_b[0], NFB)
    vec[:, V_QN] = f(q_norm)[0]
    vec[:, V_KN] = f(k_norm)[0]
    vec[:, V_HGN:V_HGN + 16] = col(hg_norm[0], 16)
    lbl = f(lb_logits)
    for d in range(2):
        for l in range(2):
            vec[:, V_LBL + (d * 2 + l) * 16:V_LBL + (d * 2 + l) * 16 + 16] = col(lbl[d, l], 16)
    cst = _consts()
    weights = {"w_in": f(w_in)[0], "w_pa": f(w_pa)[0], "w_pb": f(w_pb)[0], "w_gate": f(w_gate)[0], "w_o": f(w_o)[0],
               "w_up": f(w_up)[0], "w_down": f(w_down)[0], "w_pg": f(w_pg)[0], "w_ple": f(w_ple)[0]}
    in_maps = []
    for c in range(8):
        b, s = c // 4, c % 4
        v = vec.copy()
        for r in range(4):
            v[:, V_MF + r] = 1.0 if r < s else 0.0
            v[:, V_MB + r] = 1.0 if r > s else 0.0
            v[:, V_ML + r] = 1.0 if r == s - 1 else 0.0
            v[:, V_MR + r] = 1.0 if r == s + 1 else 0.0
            v[:, V_OMF + r] = 0.0 if r < s else 1.0
            v[:, V_OMB + r] = 0.0 if r > s else 1.0
        cosT, sinT = _rope_tables(s * TOK)
        m = {"xT": np.ascontiguousarray(x[b, s * TOK:(s + 1) * TOK, :].T),
             "pT": np.ascontiguousarray(p[0, b, s * TOK:(s + 1) * TOK, :].T),
             "vec": v, "cst": cst, "cosT": cosT, "sinT": sinT}
        m.update(weights)
        in_maps.append(m)
    if "nc" not in _NC_CACHE:
        try:
            build_nc()
        except _Stop:
            pass
    in_maps = [{k: v for k, v in m.items() if k in _NC_CACHE["names"]} for m in in_maps]
    res = run_bass_kernel_spmd(_NC_CACHE["nc"], in_maps, core_ids=list(range(8)))
    out = np.empty((2, 4 * TOK, D), np.float32)
    for c in range(8):
        b, s = c // 4, c % 4
        out[b, s * TOK:(s + 1) * TOK, :] = res.results[c]["outT"].T
    return out
```

```python
from contextlib import ExitStack
import os
import numpy as np
import concourse.bass as bass
import concourse.mybir as mybir
from concourse.bass_utils import run_bass_kernel_spmd

F32 = mybir.dt.float32
BF16 = mybir.dt.bfloat16
AF = mybir.ActivationFunctionType
ALU = mybir.AluOpType

D = 4096
TOK = 2048
NT = 512
NTT = 4
DFF = 11008
NFB = 86
ALPHA = 2.0 ** 0.25
RMS_EPS = 1e-6
LN_EPS = 1e-5
SCALE = 128 ** -0.5
ESHIFT = -4.0

V_BG = 0
V_L1G = 64
V_L1B = 96
V_L2G = 128
V_L2B = 160
V_CW = 192
V_CB = 450
V_QN = 536
V_KN = 537
V_HGN = 538
V_LBL = 554
V_MF = 618
V_MB = 622
V_ML = 626
V_MR = 630
V_OMF = 634
V_OMB = 638
NV = 642
C_RT, C_ID, C_ONE, C_MF, C_MB = 0, 128, 256, 384, 512


class _Stop(Exception):
    pass


class Buf:
    __slots__ = ("name", "t", "last_write", "reads", "sem")

    def __init__(self, name, t=None):
        self.name = name
        self.t = t
        self.last_write = None
        self.reads = []
        self.sem = None

    def __getitem__(self, idx):
        return self.t[idx]


class Trk:
    ENG = ("pe", "act", "dve", "pool", "sp")

    def __init__(self, nc, stack):
        self.nc = nc
        self.stack = stack
        self.eng = {"pe": nc.tensor, "act": nc.scalar, "dve": nc.vector, "pool": nc.gpsimd, "sp": nc.sync}
        self.sem = {}
        self.cnt = {}
        for e in ("pe", "act", "dve", "pool"):
            self.sem[e] = stack.enter_context(nc.semaphore("c_" + e))
            self.cnt[e] = 0
        self.waited = {e: {} for e in self.ENG}
        self.dma_sems = []
        self.free_sems = {}
        self.nwaits = 0
        self.ninst = 0

    def _wait(self, e, ev):
        kind, s, v = ev
        if kind == "c":
            if s == e and e == "pe":
                return
            sem = self.sem[s]
            key = "c" + s
        else:
            sem = s[0]
            key = id(s)
            v = s[1]
        w = self.waited[e]
        if w.get(key, -1) >= v:
            return
        w[key] = v
        self.eng[e].wait_ge(sem, v)
        self.nwaits += 1

    def _deps(self, e, reads, writes):
        for b in reads:
            if b.last_write is not None:
                self._wait(e, b.last_write)
        for b in writes:
            if b.last_write is not None:
                self._wait(e, b.last_write)
            for ev in b.reads:
                self._wait(e, ev)

    def _reg(self, ev, reads, writes):
        for b in reads:
            b.reads.append(ev)
        for b in writes:
            b.last_write = ev
            b.reads = []

    def op(self, e, fn, reads=(), writes=(), signal=True):
        self._deps(e, reads, writes)
        inst = fn(self.eng[e])
        self.ninst += 1
        if not signal:
            return None
        self.cnt[e] += 1
        inst.then_inc(self.sem[e], 1)
        ev = ("c", e, self.cnt[e])
        self._reg(ev, reads, writes)
        return ev

    def _dsem(self, b, q):
        if b.sem is None:
            b.sem = {}
        if q not in b.sem:
            fl = self.free_sems.setdefault(q, [])
            if fl:
                b.sem[q] = fl.pop()
            else:
                rec = [self.stack.enter_context(self.nc.semaphore("d_%d" % len(self.dma_sems))), 0]
                self.dma_sems.append(rec)
                b.sem[q] = rec
        return b.sem[q]

    def dma(self, q, out, in_, reads=(), writes=(), sembuf=None):
        self._deps(q, reads, writes)
        s = self._dsem(sembuf, q)
        inst = self.eng[q].dma_start(out=out, in_=in_)
        s[1] += 16
        inst.then_inc(s[0], 16)
        self.ninst += 1
        ev = ("d", s, s[1])
        self._reg(ev, reads, writes)
        return ev

    def custom(self, e, fn, owner, inc, reads=(), writes=()):
        self._deps(e, reads, writes)
        s = self._dsem(owner, "cc")
        inst = fn(self.eng[e])
        s[1] += inc
        inst.then_inc(s[0], inc)
        ev = ("d", s, s[1])
        self._reg(ev, reads, writes)
        return ev

    def barrier(self, release=()):
        for e in self.ENG:
            for s in ("pe", "act", "dve", "pool"):
                if self.cnt[s] > 0:
                    self._wait(e, ("c", s, self.cnt[s]))
            for s in self.dma_sems:
                if s[1] > 0:
                    self._wait(e, ("d", s, s[1]))
        for b in release:
            if b.sem is not None:
                for q, rec in b.sem.items():
                    self.free_sems.setdefault(q, []).append(rec)
                b.sem = None


class Arena:
    def __init__(self, nc, base=0):
        self.nc = nc
        self.base = base
        self.off = base
        self.n = 0
        self.bufs = []

    def reset(self):
        self.off = self.base
        b = self.bufs
        self.bufs = []
        return b

    def sb(self, name, shape, dt):
        esz = 4 if dt == F32 else 2
        per = esz
        for s in shape[1:]:
            per *= s
        self.off = (self.off + 63) // 64 * 64
        self.n += 1
        t = self.nc.alloc_sbuf_tensor_at("%s_%d" % (name, self.n), list(shape), dt, offset=self.off)
        self.off += per
        assert self.off <= 16384 + 211000, (name, self.off)
        b = Buf(name, t)
        self.bufs.append(b)
        return b


def build_nc():
    nc = bass.Bass("TRN2", target_bir_lowering=False)
    _NC_CACHE["nc"] = nc

    KSTOP = int(os.environ.get("KSTOP", "99"))
    KP1T = int(os.environ.get("KP1T", str(NTT)))
    KP1C = int(os.environ.get("KP1C", "104"))
    KCOLL = int(os.environ.get("KCOLL", "1"))
    KH = int(os.environ.get("KH", "16"))
    KG = int(os.environ.get("KG", "4"))
    KQT = int(os.environ.get("KQT", str(NTT)))
    KT4 = int(os.environ.get("KT4", str(NTT)))
    KT5 = int(os.environ.get("KT5", str(NTT)))
    KCB = int(os.environ.get("KCB", "32"))
    KFB = int(os.environ.get("KFB", str(NFB)))
    names = _NC_CACHE.setdefault("names", set())

    def din(name, shape, dt=F32):
        if KSTOP <= 3 and name in ("pT", "w_pa", "w_pb", "w_gate", "w_o", "w_up", "w_down", "w_pg", "w_ple"):
            return None
        names.add(name)
        return nc.dram_tensor(name, shape, dt, kind="ExternalInput").ap()

    xT = din("xT", [D, TOK])
    pT = din("pT", [256, TOK])
    w_in = din("w_in", [D, 13312])
    w_pa = din("w_pa", [2048, D])
    w_pb = din("w_pb", [2048, D])
    w_gate = din("w_gate", [D, 2 * D])
    w_o = din("w_o", [D, D])
    w_up = din("w_up", [D, 2 * DFF])
    w_down = din("w_down", [DFF, D])
    w_pg = din("w_pg", [D, D])
    w_ple = din("w_ple", [256, D])
    vec_d = din("vec", [128, NV])
    cst_d = din("cst", [128, 640])
    cos_d = din("cosT", [128, TOK])
    sin_d = din("sinT", [128, TOK])
    outT = nc.dram_tensor("outT", [D, TOK], F32, kind="ExternalOutput").ap()

    def scr(name, shape, dt):
        t = nc.dram_tensor(name, shape, dt)
        _NC_CACHE.setdefault("scratch", []).append(name)
        return Buf(name, t.ap())

    QS = scr("QS", [2048, TOK], BF16)
    KCI = [scr("KCI%d" % g, [128, TOK], BF16) for g in range(4)]
    KCO = [scr("KCO%d" % g, [512, TOK], BF16) for g in range(4)]
    VCI = [scr("VCI%d" % g, [TOK, 128], BF16) for g in range(4)]
    VCO = [scr("VCO%d" % g, [4 * TOK, 128], BF16) for g in range(4)]
    HQ = scr("HQ", [2048, TOK], F32)
    HZF = scr("HZF", [2048, TOK], F32)
    HZB = scr("HZB", [2048, TOK], F32)
    HG = scr("HG", [2048, TOK], F32)
    HV = scr("HV", [TOK, 2048], BF16)
    OLOC = scr("OLOC", [2048, TOK], F32)
    QGF = scr("QGF", [2048, TOK], BF16)
    QGB = scr("QGB", [2048, TOK], BF16)
    SCI = [scr("SCI%d" % i, [8 * 128, 128], F32) for i in range(4)]
    SCO = [scr("SCO%d" % i, [4 * 8 * 128, 128], F32) for i in range(4)]
    DCI = scr("DCI", [128, 32], F32)
    DCO = scr("DCO", [512, 32], F32)
    YA = scr("YA", [2048, TOK], BF16)
    YH = scr("YH", [2048, TOK], BF16)
    H1 = scr("H1", [D, TOK + 2], F32)
    EDI = scr("EDI", [D, 2], F32)
    EDO = scr("EDO", [4 * D, 2], F32)
    WOT_t = nc.dram_tensor("WOT", [16, 128, 32 * 256], BF16)
    WPT_t = nc.dram_tensor("WPT", [16, 128, 32 * 256], BF16)
    WDT_t = nc.dram_tensor("WDT", [16, 4, 128, 22 * 256], BF16)
    WAT_t = nc.dram_tensor("WAT", [16, 128, 16 * 256], BF16)
    WBT_t = nc.dram_tensor("WBT", [16, 128, 16 * 256], BF16)
    WGT_t = nc.dram_tensor("WGT", [16, 2, 128, 32 * 256], BF16)
    WAT = [Buf("WAT%d" % i, WAT_t.ap()[i]) for i in range(16)]
    WBT = [Buf("WBT%d" % i, WBT_t.ap()[i]) for i in range(16)]
    WGT = [[Buf("WGT%d_%d" % (i, j), WGT_t.ap()[i, j]) for j in range(2)] for i in range(16)]
    WOT = [Buf("WOT%d" % i, WOT_t.ap()[i]) for i in range(16)]
    WPT = [Buf("WPT%d" % i, WPT_t.ap()[i]) for i in range(16)]
    WDT = [[Buf("WDT%d_%d" % (i, q), WDT_t.ap()[i, q]) for q in range(4)] for i in range(16)]
    QK4 = [(0, 22), (22, 21), (43, 22), (65, 21)]
    convsem = Buf("convsem")
    GROUPS = [[0, 1, 2, 3], [4, 5, 6, 7]]

    with ExitStack() as st:
        st.enter_context(nc.allow_non_contiguous_dma(reason="single-column halo/edge transfers"))
        T = Trk(nc, st)
        PSF = [Buf("psf%d" % i, st.enter_context(nc.psum_tensor("psf%d" % i, [128, 512], F32))) for i in range(6)]
        PTB = [Buf("ptb%d" % i, st.enter_context(nc.psum_tensor("ptb%d" % i, [128, 8, 128], BF16))[:, 0:4, :]) for i in range(2)]

        A0 = Arena(nc, 16384)
        vec = A0.sb("vec", [128, NV], F32)
        cst = A0.sb("cst", [128, 640], F32)
        cb16 = A0.sb("cb16", [128, 640], BF16)
        rtq = A0.sb("rtq", [128, 128], BF16)
        rtk = A0.sb("rtk", [128, 128], BF16)
        lbt = A0.sb("lbt", [128, 64], F32)
        A = Arena(nc, A0.off)

        T.dma("sp", vec[:], vec_d[:, :], writes=[vec], sembuf=vec)
        T.dma("sp", cst[:], cst_d[:, :], writes=[cst], sembuf=cst)
        T.op("dve", lambda e: e.tensor_copy(out=cb16[:], in_=cst[:]), reads=[cst], writes=[cb16])
        T.op("dve", lambda e: e.tensor_scalar(out=rtq[:], in0=cst[:, C_RT:C_RT + 128], scalar1=vec[:, V_QN:V_QN + 1],
                                              scalar2=None, op0=ALU.mult), reads=[cst, vec], writes=[rtq])
        T.op("dve", lambda e: e.tensor_scalar(out=rtk[:], in0=cst[:, C_RT:C_RT + 128], scalar1=vec[:, V_KN:V_KN + 1],
                                              scalar2=None, op0=ALU.mult), reads=[cst, vec], writes=[rtk])
        for d in range(2):
            T.op("dve", lambda e, d=d: e.tensor_tensor(out=lbt[:, d * 16:(d + 1) * 16],
                                                       in0=vec[:, V_LBL + (d * 2) * 16:V_LBL + (d * 2) * 16 + 16],
                                                       in1=vec[:, V_LBL + (d * 2 + 1) * 16:V_LBL + (d * 2 + 1) * 16 + 16],
                                                       op=ALU.subtract), reads=[vec], writes=[lbt])
        T.op("act", lambda e: e.activation(out=lbt[:, 0:32], in_=lbt[:, 0:32], func=AF.Sigmoid), reads=[lbt], writes=[lbt])
        T.op("dve", lambda e: e.tensor_scalar(out=lbt[:, 32:64], in0=lbt[:, 0:32], scalar1=-1.0, scalar2=1.0,
                                              op0=ALU.mult, op1=ALU.add), reads=[lbt], writes=[lbt])
        onesb = cb16[:, C_ONE:C_ONE + 128]
        identb = cb16[:, C_ID:C_ID + 128]

        def mm(out, lhsT, rhs, start, stop, reads, writes, signal):
            T.op("pe", lambda e: e.matmul(out, lhsT, rhs, start=start, stop=stop), reads=reads, writes=writes, signal=signal)

        def act(out, in_, func, reads, writes, **kw):
            T.op("act", lambda e: e.activation(out=out, in_=in_, func=func, **kw), reads=reads, writes=writes)

        def tt(out, in0, in1, op, reads, writes, eng="dve"):
            T.op(eng, lambda e: e.tensor_tensor(out=out, in0=in0, in1=in1, op=op), reads=reads, writes=writes)

        def ts(out, in0, s1, s2, op0, op1, reads, writes, eng="dve"):
            if s2 is None:
                T.op(eng, lambda e: e.tensor_scalar(out=out, in0=in0, scalar1=s1, scalar2=None, op0=op0), reads=reads, writes=writes)
            else:
                T.op(eng, lambda e: e.tensor_scalar(out=out, in0=in0, scalar1=s1, scalar2=s2, op0=op0, op1=op1),
                     reads=reads, writes=writes)

        def stt(out, in0, sc, in1, op0, op1, reads, writes):
            T.op("dve", lambda e: e.scalar_tensor_tensor(out=out, in0=in0, scalar=sc, in1=in1, op0=op0, op1=op1),
                 reads=reads, writes=writes)

        def cp(out, in_, reads, writes, eng="dve"):
            T.op(eng, lambda e: e.tensor_copy(out=out, in_=in_), reads=reads, writes=writes)

        def rows(ap2d, p=128):
            return ap2d.rearrange("(k p) c -> p k c", p=p)

        def load_w(slab_view, w2d, c0, ncols, k0, kc, slab):
            src = rows(w2d)[:, k0:k0 + kc, c0:c0 + ncols]
            h = (kc + 1) // 2
            T.dma("pool", slab_view[:, 0:h, :], src[:, 0:h, :], writes=[slab], sembuf=slab)
            if kc > h:
                T.dma("pool", slab_view[:, h:kc, :], src[:, h:kc, :], writes=[slab], sembuf=slab)

        def rstd_from_ss(dst, ss_ap, ssbuf, tmpbuf, scale, eps, tmp_ap=None):
            ta = tmpbuf[:] if tmp_ap is None else tmp_ap
            ts(ta, ss_ap, scale, eps, ALU.mult, ALU.add, [ssbuf], [tmpbuf])
            act(ta, ta, AF.Ln, [tmpbuf], [tmpbuf])
            act(dst[:], ta, AF.Exp, [tmpbuf], [dst], scale=-0.5)

        xbs = [A.sb("xb%d" % i, [128, 32, NT], BF16) for i in range(2)]
        slabs = [A.sb("ws%d" % i, [128, 32, 256], BF16) for i in range(3)]
        cosbs = [A.sb("cos%d" % i, [128, NT], F32) for i in range(2)]
        sinbs = [A.sb("sin%d" % i, [128, NT], F32) for i in range(2)]
        TMP = [A.sb("tmp%d" % i, [128, NT], F32) for i in range(8)]
        TB = [A.sb("tb%d" % i, [128, NT], BF16) for i in range(8)]
        VT = [A.sb("vt%d" % i, [128, 4, 128], BF16) for i in range(2)]
        ntmp = [0]

        def tmp():
            ntmp[0] += 1
            return TMP[ntmp[0] % len(TMP)]

        ntb = [0]

        def tb():
            ntb[0] += 1
            return TB[ntb[0] % len(TB)]

        it = 0
        for st_ in range(KP1T // 2):
          for half in range(2):
            t0 = (st_ * 2 + half) * NT
            xsrc = rows(xT)[:, :, t0:t0 + NT]
            T.dma("pool", xbs[half][:, 0:16, :], xsrc[:, 0:16, :], writes=[xbs[half]], sembuf=xbs[half])
            T.dma("pool", xbs[half][:, 16:32, :], xsrc[:, 16:32, :], writes=[xbs[half]], sembuf=xbs[half])
            T.dma("sp", cosbs[half][:], cos_d[:, t0:t0 + NT], writes=[cosbs[half]], sembuf=cosbs[half])
            T.dma("sp", sinbs[half][:], sin_d[:, t0:t0 + NT], writes=[sinbs[half]], sembuf=sinbs[half])
          for cbk in range(KP1C):
            if cbk % 2 == 0:
                slab = slabs[(cbk // 2) % 3]
                load_w(slab[:], w_in, cbk * 128, 256, 0, 32, slab)
            for half in range(2):
                ti = st_ * 2 + half
                t0 = ti * NT
                xb, cosb, sinb = xbs[half], cosbs[half], sinbs[half]
                jo = (cbk % 2) * 128
                acc = PSF[it % 3]
                it += 1
                for k in range(32):
                    mm(acc[:], slab[:, k, jo:jo + 128], xb[:, k, :], k == 0, k == 31, [slab, xb], [acc], k == 31)
                if cbk < 20:
                    isq = cbk < 16
                    gcol = V_QN if isq else V_KN
                    rt = rtq if isq else rtk
                    qb_, sq_ = tb(), tb()
                    act(qb_[:], acc[:], AF.Copy, [acc], [qb_])
                    act(sq_[:], acc[:], AF.Square, [acc], [sq_])
                    ss, rq = PSF[3], PSF[4]
                    mm(ss[:], onesb, sq_[:], True, True, [cb16, sq_], [ss], True)
                    mm(rq[:], rt[:], qb_[:], True, True, [rt, qb_], [rq], True)
                    rstd, tl = tmp(), tmp()
                    rstd_from_ss(rstd, ss[:], ss, tl, 1.0 / 128, RMS_EPS)
                    a_, b_ = tmp(), tmp()
                    stt(a_[:], acc[:], vec[:, gcol:gcol + 1], cosb[:], ALU.mult, ALU.mult, [acc, vec, cosb], [a_])
                    tt(b_[:], rq[:], sinb[:], ALU.mult, [rq, sinb], [b_])
                    tt(a_[:], a_[:], b_[:], ALU.add, [a_, b_], [a_])
                    ob = tb()
                    tt(ob[:], a_[:], rstd[:], ALU.mult, [a_, rstd], [ob])
                    if isq:
                        T.dma("sp", QS[cbk * 128:(cbk + 1) * 128, t0:t0 + NT], ob[:], reads=[ob], writes=[QS], sembuf=ob)
                    else:
                        g = cbk - 16
                        T.dma("sp", KCI[g][:, t0:t0 + NT], ob[:], reads=[ob], writes=[KCI[g]], sembuf=ob)
                elif cbk < 24 or 72 <= cbk < 88:
                    vb_ = tb()
                    act(vb_[:], acc[:], AF.Copy, [acc], [vb_])
                    pt = PTB[it % 2]
                    for j in range(4):
                        T.op("pe", lambda e, j=j, pt=pt, vb_=vb_: e.transpose(pt[:, j, :], vb_[:, j * 128:(j + 1) * 128], identb),
                             reads=[vb_, cb16], writes=[pt], signal=(j == 3))
                    vt = VT[it % 2]
                    cp(vt[:], pt[:], [pt], [vt])
                    if cbk < 24:
                        g = cbk - 20
                        dst = rows(VCI[g][:, :])[:, ti * 4:ti * 4 + 4, :]
                        T.dma("sp", dst, vt[:], reads=[vt], writes=[VCI[g]], sembuf=vt)
                    else:
                        h = cbk - 72
                        dst = rows(HV[:, :])[:, ti * 4:ti * 4 + 4, h * 128:(h + 1) * 128]
                        T.dma("sp", dst, vt[:], reads=[vt], writes=[HV], sembuf=vt)
                else:
                    if cbk < 40:
                        dstb, h = HQ, cbk - 24
                    elif cbk < 56:
                        dstb, h = HZF, cbk - 40
                    elif cbk < 72:
                        dstb, h = HZB, cbk - 56
                    else:
                        dstb, h = HG, cbk - 88
                    o_ = tmp()
                    if cbk % 2 == 0:
                        act(o_[:], acc[:], AF.Copy, [acc], [o_])
                    else:
                        cp(o_[:], acc[:], [acc], [o_])
                    T.dma("sp", dstb[h * 128:(h + 1) * 128, t0:t0 + NT], o_[:], reads=[o_], writes=[dstb], sembuf=o_)

        def allgather(src, dst):
            T.custom("pool", lambda e: e.collective_compute("AllGather", ALU.bypass, replica_groups=GROUPS,
                                                            ins=[src.t.opt()], outs=[dst.t.opt()]),
                     dst, 1, reads=[src], writes=[dst])

        if KCOLL:
            for g in range(4):
                allgather(KCI[g], KCO[g])
                allgather(VCI[g], VCO[g])
        T.barrier(A.reset())
        if KSTOP <= 1:
            T.barrier()
            raise _Stop()

        qf = A.sb("qf", [128, TOK], F32)
        zr = [A.sb("zr%d" % i, [128, TOK], F32) for i in range(2)]
        kf = A.sb("kf", [128, TOK], F32)
        Pp = A.sb("Pp", [128, TOK + 64], F32)
        onesf = A.sb("onesf", [128, TOK], F32)
        dt_ = A.sb("dt", [128, TOK], F32)
        et_ = A.sb("et", [128, TOK], F32)
        oacc2 = [A.sb("oacc%d" % i, [128, TOK], F32) for i in range(2)]
        vtok2 = [A.sb("vtok%d" % i, [128, 16, 128], BF16) for i in range(2)]
        qin2 = [[A.sb("qin%d" % i, [128, TOK], BF16) for i in range(2)] for _ in range(2)]
        kin2 = [[A.sb("kin%d" % i, [128, TOK], BF16) for i in range(2)] for _ in range(2)]
        kdec = A.sb("kdec", [128, TOK], BF16)
        kdt2 = [[A.sb("kdt%d" % i, [128, 16, 128], BF16) for i in range(2)] for _ in range(2)]
        qdec2 = [[A.sb("qdec%d" % i, [128, TOK], BF16) for i in range(2)] for _ in range(2)]
        qgl = A.sb("qgl", [128, TOK], BF16)
        dec2 = [[A.sb("dec%d" % i, [128, 32], F32) for i in range(2)] for _ in range(2)]
        dct = A.sb("dct", [128, 32], F32)
        Sf = [A.sb("Sf%d" % i, [128, 128], F32) for i in range(2)]
        Sb_ = [A.sb("Sb%d" % i, [128, 128], BF16) for i in range(2)]
        amt = [A.sb("amt%d" % i, [128, 128], BF16) for i in range(2)]
        T.op("pool", lambda e: e.memset(onesf[:], 1.0), writes=[onesf])
        T.op("pool", lambda e: e.memset(dct[:], 0.0), writes=[dct])

        def bc(ref):
            return ref.unsqueeze(2).to_broadcast([128, 32, 64])

        def v3(ap):
            return ap.rearrange("p (c t) -> p c t", t=64)

        def h1_prep(h):
                hs = slice(h * 128, (h + 1) * 128)
                hp = h % 2
                oacc, vtok, qin, kin, kdt, qdec, dec = oacc2[hp], vtok2[hp], qin2[hp], kin2[hp], kdt2[hp], qdec2[hp], dec2[hp]
                T.dma("sp", qf[:], HQ[hs, :], reads=[HQ], writes=[qf], sembuf=qf)
                yield
                T.dma("sp", zr[0][:], HZF[hs, :], reads=[HZF], writes=[zr[0]], sembuf=zr[0])
                yield
                T.dma("sp", zr[1][:], HZB[hs, :], reads=[HZB], writes=[zr[1]], sembuf=zr[1])
                yield
                T.dma("sp", vtok[:], rows(HV[:, :])[:, :, hs], reads=[HV], writes=[vtok], sembuf=vtok)
                yield
                act(qf[:], qf[:], AF.Silu, [qf], [qf])
                yield
                T.op("pool", lambda e: e.memset(oacc[:], 0.0), writes=[oacc])
                yield
                for d in range(2):
                    sg = 1.0 if d == 0 else -1.0
                    o = 1 if d == 0 else 0
                    z = zr[d]
                    act(kf[:], z[:], AF.Sigmoid, [z], [kf], scale=-1.0)
                    yield
                    ts(kf[:], kf[:], lbt[:, 32 + d * 16 + h:32 + d * 16 + h + 1], None, ALU.mult, None, [kf, lbt], [kf])
                    yield
                    act(z[:], kf[:], AF.Ln, [kf], [z], scale=-1.0, bias=1.0)
                    yield
                    T.op("dve", lambda e: e.memset(Pp[:, 0:1], 0.0), writes=[Pp])
                    yield
                    T.op("dve", lambda e, z=z: e.tensor_tensor_scan(out=Pp[:, 1:TOK + 1], data0=onesf[:], data1=z[:], initial=0.0,
                                                                   op0=ALU.mult, op1=ALU.add), reads=[onesf, z], writes=[Pp])
                    yield
                    E3 = v3(Pp[:, o:o + TOK])
                    Mr = Pp[:, 32:TOK:64]
                    Lr = Pp[:, 64:TOK + 1:64] if d == 0 else Pp[:, 0:TOK:64]
                    Vr = Pp[:, 0:TOK:64] if d == 0 else Pp[:, 64:TOK + 1:64]
                    tt(v3(dt_[:]), E3, bc(Mr), ALU.subtract, [Pp], [dt_])
                    yield
                    act(et_[:], dt_[:], AF.Exp, [dt_], [et_], scale=sg)
                    yield
                    tt(qin[d][:], qf[:], et_[:], ALU.mult, [qf, et_], [qin[d]])
                    yield
                    act(et_[:], dt_[:], AF.Exp, [dt_], [et_], scale=-sg)
                    yield
                    tt(kin[d][:], kf[:], et_[:], ALU.mult, [kf, et_], [kin[d]])
                    yield
                    tt(v3(dt_[:]), E3, bc(Lr), ALU.subtract, [Pp], [dt_])
                    yield
                    act(et_[:], dt_[:], AF.Exp, [dt_], [et_], scale=-sg)
                    yield
                    tt(kdec[:], kf[:], et_[:], ALU.mult, [kf, et_], [kdec])
                    yield
                    tt(v3(dt_[:]), E3, bc(Vr), ALU.subtract, [Pp], [dt_])
                    yield
                    act(et_[:], dt_[:], AF.Exp, [dt_], [et_], scale=sg)
                    yield
                    tt(qdec[d][:], qf[:], et_[:], ALU.mult, [qf, et_], [qdec[d]])
                    yield
                    if d == 0:
                        act(et_[:], Pp[:, 1:TOK + 1], AF.Exp, [Pp], [et_])
                        yield
                    else:
                        act(et_[:], Pp[:, 0:TOK], AF.Exp, [Pp], [et_], scale=-1.0, bias=Pp[:, TOK:TOK + 1])
                        yield
                    tt(qgl[:], qf[:], et_[:], ALU.mult, [qf, et_], [qgl])
                    yield
                    T.dma("sp", (QGF if d == 0 else QGB)[hs, :], qgl[:], reads=[qgl], writes=[QGF if d == 0 else QGB], sembuf=qgl)
                    yield
                    tt(dec[d][:], Pp[:, 64:TOK + 1:64], Pp[:, 0:TOK:64], ALU.subtract, [Pp], [dec[d]])
                    yield
                    act(dec[d][:], dec[d][:], AF.Exp, [dec[d]], [dec[d]])
                    yield
                    act(dct[:, h * 2 + d:h * 2 + d + 1], Pp[:, TOK:TOK + 1], AF.Exp, [Pp], [dct])
                    yield
                    for j4 in range(4):
                        pt = PTB[j4 % 2]
                        for j in range(4):
                            jj = j4 * 4 + j
                            T.op("pe", lambda e, j=j, jj=jj, pt=pt: e.transpose(pt[:, j, :], kdec[:, jj * 128:(jj + 1) * 128], identb),
                                 reads=[kdec, cb16], writes=[pt], signal=(j == 3))
                            yield
                        cp(kdt[d][:, j4 * 4:j4 * 4 + 4, :], pt[:], [pt], [kdt[d]])
                        yield
                yield

        def h1_loop(h):
                hs = slice(h * 128, (h + 1) * 128)
                hp = h % 2
                oacc, vtok, qin, kin, kdt, qdec, dec = oacc2[hp], vtok2[hp], qin2[hp], kin2[hp], kdt2[hp], qdec2[hp], dec2[hp]
                for d in range(2):
                    T.op("pool", lambda e, d=d: e.memset(Sf[d][:], 0.0), writes=[Sf[d]])
                    T.op("pool", lambda e, d=d: e.memset(Sb_[d][:], 0.0), writes=[Sb_[d]])
                for step in range(16):
                    for d in range(2):
                        blk = step if d == 0 else 15 - step
                        bs = slice(blk * 128, (blk + 1) * 128)
                        mk = cst[:, C_MF:C_MF + 128] if d == 0 else cst[:, C_MB:C_MB + 128]
                        AT = PSF[d]
                        oI = PSF[2 + d]
                        oA = PSF[4 + d]
                        Uv = AT[:, 256:384]
                        mm(AT[:, 0:128], kin[d][:, bs], qin[d][:, bs], True, True, [kin[d], qin[d]], [AT], True)
                        tt(amt[d][:], AT[:, 0:128], mk, ALU.mult, [AT, cst], [amt[d]])
                        for ci in ((0, 1) if d == 0 else (1, 0)):
                            c = blk * 2 + ci
                            cs_ = slice(blk * 128 + ci * 64, blk * 128 + ci * 64 + 64)
                            ps_ = slice(ci * 64, ci * 64 + 64)
                            mm(oI[:, ci * 64:ci * 64 + 64], Sb_[d][:], qdec[d][:, cs_], True, True, [Sb_[d], qdec[d]], [oI], True)
                            mm(Uv, kdt[d][ps_, blk, :], vtok[ps_, blk, :], True, True, [kdt[d], vtok], [AT], True)
                            stt(Sf[d][:], Sf[d][:], dec[d][:, c:c + 1], Uv, ALU.mult, ALU.add, [Sf[d], dec[d], AT], [Sf[d]])
                            act(Sb_[d][:], Sf[d][:], AF.Copy, [Sf[d]], [Sb_[d]])
                        mm(oA[:, 0:128], vtok[:, blk, :], amt[d][:], True, True, [vtok, amt[d]], [oA], True)
                        tt(oacc[:, bs], oacc[:, bs], oA[:, 0:128], ALU.add, [oacc, oA], [oacc])
                        tt(oacc[:, bs], oacc[:, bs], oI[:, 0:128], ALU.add, [oacc, oI], [oacc])
                        yield
                for d in range(2):
                    hd = h * 2 + d
                    T.dma("sp", SCI[hd // 8][(hd % 8) * 128:(hd % 8 + 1) * 128, :], Sf[d][:], reads=[Sf[d]], writes=[SCI[hd // 8]],
                          sembuf=Sf[d])
                T.dma("sp", OLOC[hs, :], oacc[:], reads=[oacc], writes=[OLOC], sembuf=oacc)

                yield

        def run_some(gen, n):
            if gen is None:
                return None
            for _ in range(n):
                try:
                    next(gen)
                except StopIteration:
                    return None
            return gen

        g = h1_prep(0)
        while g is not None:
            g = run_some(g, 1000)
        for h in range(KH):
            lp = h1_loop(h)
            pp = h1_prep(h + 1) if h + 1 < KH else None
            while lp is not None or pp is not None:
                lp = run_some(lp, 1)
                pp = run_some(pp, 3)
        T.dma("sp", DCI[:, :], dct[:], reads=[dct], writes=[DCI], sembuf=dct)
        for i in range(4):
            allgather(SCI[i], SCO[i])
        allgather(DCI, DCO)
        T.barrier(A.reset())
        if KSTOP <= 2:
            T.barrier()
            raise _Stop()

        if KSTOP > 3:
            for i in range(16):
                T.dma("pool", WAT[i][:, :].rearrange("p (k c) -> p k c", c=256), rows(w_pa)[:, :, i * 256:(i + 1) * 256],
                      writes=[WAT[i]], sembuf=convsem)
                T.dma("pool", WBT[i][:, :].rearrange("p (k c) -> p k c", c=256), rows(w_pb)[:, :, i * 256:(i + 1) * 256],
                      writes=[WBT[i]], sembuf=convsem)
                for j in range(2):
                    T.dma("pool", WGT[i][j][:, :].rearrange("p (k c) -> p k c", c=256),
                          rows(w_gate)[:, :, j * D + i * 256:j * D + (i + 1) * 256], writes=[WGT[i][j]], sembuf=convsem)
            for i in range(16):
                T.dma("pool", WOT[i][:, :].rearrange("p (k c) -> p k c", c=256), rows(w_o)[:, :, i * 256:(i + 1) * 256],
                      writes=[WOT[i]], sembuf=convsem)
            for i in range(16):
                for q, (k0, kq) in enumerate(QK4):
                    T.dma("pool", WDT[i][q][:, 0:kq * 256].rearrange("p (k c) -> p k c", c=256),
                          rows(w_down)[:, k0:k0 + kq, i * 256:(i + 1) * 256], writes=[WDT[i][q]], sembuf=convsem)
            for i in range(16):
                T.dma("pool", WPT[i][:, :].rearrange("p (k c) -> p k c", c=256), rows(w_pg)[:, :, i * 256:(i + 1) * 256],
                      writes=[WPT[i]], sembuf=convsem)
        KT2 = [A.sb("KT%d" % i, [128, 4 * TOK], BF16) for i in range(2)]
        VG2 = [A.sb("VG%d" % i, [128, 64, 128], BF16) for i in range(2)]

        def load_kv(g):
            KT, VG = KT2[g % 2], VG2[g % 2]
            for r in range(4):
                T.dma("sp", KT[:, r * TOK:(r + 1) * TOK], KCO[g][r * 128:(r + 1) * 128, :],
                      reads=[KCO[g]], writes=[KT], sembuf=KT)
                T.dma("sp", VG[:, r * 16:(r + 1) * 16, :], rows(VCO[g][:, :])[:, r * 16:(r + 1) * 16, :],
                      reads=[VCO[g]], writes=[VG], sembuf=VG)
        qT = [A.sb("qT%d" % i, [128, NT], BF16) for i in range(2)]
        pTs = [A.sb("pT%d" % i, [128, NT], BF16) for i in range(6)]
        rl = [A.sb("rl%d" % i, [128, NT], F32) for i in range(2)]
        yo = [A.sb("yo%d" % i, [128, NT], BF16) for i in range(2)]
        accD = [A.sb("accD%d" % i, [128, NT], F32) for i in range(2)]
        accP = [A.sb("accP%d" % i, [128, NT], F32) for i in range(2)]
        lhi = [A.sb("lhi%d" % i, [128, NT], BF16) for i in range(2)]
        llo = [A.sb("llo%d" % i, [128, NT], BF16) for i in range(2)]
        LA = 2
        u = 0
        load_kv(0)
        for g in range(KG):
            KT, VG = KT2[g % 2], VG2[g % 2]
            if g + 1 < KG:
                load_kv(g + 1)
            for qt in range(KQT):
                for hh in range(4):
                    h = g * 4 + hh
                    q_ = qT[u % 2]
                    T.dma("sp", q_[:], QS[h * 128:(h + 1) * 128, qt * NT:(qt + 1) * NT], reads=[QS], writes=[q_], sembuf=q_)
                    oT = PSF[4 + u % 2]
                    lT = PSF[3]
                    aD, aP = accD[u % 2], accP[u % 2]
                    hi_, lo_ = lhi[u % 2], llo[u % 2]
                    LAST_DVE = 55

                    def qk(kb, q_=q_):
                        sT_ = PSF[kb % 3]
                        mm(sT_[:], KT[:, kb * 128:(kb + 1) * 128], q_[:], True, True, [KT, q_], [sT_], True)

                    for kb in range(LA):
                        qk(kb)
                    for kb in range(64):
                        if kb + LA < 64:
                            qk(kb + LA)
                        sT = PSF[kb % 3]
                        p_ = pTs[kb % 6]
                        act(p_[:], sT[:], AF.Exp, [sT], [p_], scale=SCALE, bias=ESHIFT)
                        mm(oT[:], VG[:, kb, :], p_[:], kb == 0, kb == 63, [VG, p_], [oT], True)
                        if kb % 2 == 0 or kb > LAST_DVE:
                            mm(lT[:], onesb, p_[:], kb == 0, False, [cb16, p_], [lT], True)
                        elif kb == 1:
                            cp(aD[:], p_[:], [p_], [aD])
                        else:
                            tt(aD[:], aD[:], p_[:], ALU.add, [aD, p_], [aD])
                        if kb == LAST_DVE:
                            act(hi_[:], aD[:], AF.Copy, [aD], [hi_])
                            tt(lo_[:], aD[:], hi_[:], ALU.subtract, [aD, hi_], [lo_])
                    mm(lT[:], onesb, hi_[:], False, False, [cb16, hi_], [lT], False)
                    mm(lT[:], onesb, lo_[:], False, True, [cb16, hi_, lo_], [lT], True)
                    r_ = rl[u % 2]
                    y_ = yo[u % 2]
                    T.op("dve", lambda e, r_=r_, lT=lT: e.reciprocal(out=r_[:], in_=lT[:]), reads=[lT], writes=[r_])
                    tt(y_[:], oT[:], r_[:], ALU.mult, [oT, r_], [y_])
                    T.dma("sp", YA[h * 128:(h + 1) * 128, qt * NT:(qt + 1) * NT], y_[:], reads=[y_], writes=[YA], sembuf=y_)
                    u += 1
        T.barrier(A.reset())
        if KSTOP <= 3:
            T.barrier()
            raise _Stop()

        Dall = A.sb("Dall", [128, 4, 32], F32)
        Dm = [A.sb("Dm%d" % i, [128, 4, 32], F32) for i in range(2)]
        Ur = [A.sb("Ur%d" % i, [128, 4, 128], F32) for i in range(2)]
        Sacc = A.sb("Sacc", [128, 128], F32)
        Sinb = A.sb("Sinb", [128, 32, 128], BF16)
        oloc2 = [A.sb("oloc%d" % i, [128, TOK], F32) for i in range(2)]
        ghs2 = [A.sb("ghs%d" % i, [128, TOK], F32) for i in range(2)]
        qg2 = [[A.sb("qg%d" % i, [128, TOK], BF16) for i in range(2)] for _ in range(2)]

        def h2_load(h):
            hs = slice(h * 128, (h + 1) * 128)
            oloc, ghs, qg = oloc2[h % 2], ghs2[h % 2], qg2[h % 2]
            T.dma("sp", oloc[:], OLOC[hs, :], reads=[OLOC], writes=[oloc], sembuf=oloc)
            T.dma("sp", ghs[:], HG[hs, :], reads=[HG], writes=[ghs], sembuf=ghs)
            T.dma("sp", qg[0][:], QGF[hs, :], reads=[QGF], writes=[qg[0]], sembuf=qg[0])
            T.dma("sp", qg[1][:], QGB[hs, :], reads=[QGB], writes=[qg[1]], sembuf=qg[1])
        TMP = [A.sb("tmp%d" % i, [128, NT], F32) for i in range(6)]
        TB = [A.sb("tb%d" % i, [128, NT], BF16) for i in range(4)]
        T.dma("sp", Dall[:], DCO[:, :].rearrange("(r p) c -> p r c", p=128), reads=[DCO], writes=[Dall], sembuf=Dall)
        for d in range(2):
            mcol = V_MF if d == 0 else V_MB
            ocol = V_OMF if d == 0 else V_OMB
            for r in range(4):
                ts(Dm[d][:, r, :], Dall[:, r, :], vec[:, mcol + r:mcol + r + 1], vec[:, ocol + r:ocol + r + 1],
                   ALU.mult, ALU.add, [Dall, vec], [Dm[d]])
        SCO4 = [SCO[i][:, :].rearrange("(r x p) e -> p r x e", r=4, p=128) for i in range(4)]
        for hd in range(2 * KH):
            d = hd % 2
            mcol = V_MF if d == 0 else V_MB
            ur = Ur[hd % 2]
            T.dma("sp", ur[:], SCO4[hd // 8][:, :, hd % 8, :], reads=[SCO[hd // 8]], writes=[ur], sembuf=ur)
            T.op("pool", lambda e: e.memset(Sacc[:], 0.0), writes=[Sacc])
            for r in ((0, 1, 2, 3) if d == 0 else (3, 2, 1, 0)):
                ts(Sacc[:], Sacc[:], Dm[d][:, r, hd:hd + 1], None, ALU.mult, None, [Sacc, Dm[d]], [Sacc])
                stt(Sacc[:], ur[:, r, :], vec[:, mcol + r:mcol + r + 1], Sacc[:], ALU.mult, ALU.add, [ur, vec, Sacc], [Sacc])
            cp(Sinb[:, hd, :], Sacc[:], [Sacc], [Sinb])
        for h in range(KH):
            hs = slice(h * 128, (h + 1) * 128)
            oloc, ghs, qg = oloc2[h % 2], ghs2[h % 2], qg2[h % 2]
            if h == 0:
                h2_load(0)
            if h + 1 < KH:
                h2_load(h + 1)
            act(ghs[:], ghs[:], AF.Silu, [ghs], [ghs])
            for ti in range(NTT):
                tsl = slice(ti * NT, (ti + 1) * NT)
                cps = PSF[ti % 2]
                mm(cps[:], Sinb[:, 2 * h, :], qg[0][:, tsl], True, False, [Sinb, qg[0]], [cps], False)
                mm(cps[:], Sinb[:, 2 * h + 1, :], qg[1][:, tsl], False, True, [Sinb, qg[0], qg[1]], [cps], True)
                o_ = TMP[(ti * 3) % 6]
                tt(o_[:], oloc[:, tsl], cps[:], ALU.add, [oloc, cps], [o_])
                sq_ = TB[(ti * 2) % 4]
                act(sq_[:], o_[:], AF.Square, [o_], [sq_])
                ss = PSF[2 + ti % 2]
                mm(ss[:], onesb, sq_[:], True, True, [cb16, sq_], [ss], True)
                rstd, tl = TMP[(ti * 3 + 1) % 6], TMP[(ti * 3 + 2) % 6]
                rstd_from_ss(rstd, ss[:], ss, tl, 1.0 / 128, RMS_EPS)
                stt(o_[:], o_[:], vec[:, V_HGN + h:V_HGN + h + 1], rstd[:], ALU.mult, ALU.mult, [o_, vec, rstd], [o_])
                yb = TB[(ti * 2 + 1) % 4]
                tt(yb[:], o_[:], ghs[:, tsl], ALU.mult, [o_, ghs], [yb])
                T.dma("sp", YH[hs, tsl], yb[:], reads=[yb], writes=[YH], sembuf=yb)
        T.barrier(A.reset())
        if KSTOP <= 4:
            T.barrier()
            raise _Stop()

        def stat_mm(s1, s2, item):
            cb, rb, rsq = item
            mm(s1[:], onesb, rb[:], cb == 0, cb == KCB - 1, [cb16, rb], [s1], True)
            mm(s2[:], onesb, rsq[:], cb == 0, cb == KCB - 1, [cb16, rsq], [s2], True)

        def ln_finish(s1, s2, mean, rstd, nmr, eps):
            ts(mean[:], s1[:], 1.0 / D, None, ALU.mult, None, [s1], [mean])
            tt(nmr[:], mean[:], mean[:], ALU.mult, [mean], [nmr])
            stt(rstd[:], s2[:], 1.0 / D, nmr[:], ALU.mult, ALU.subtract, [s2, nmr], [rstd])
            ts(rstd[:], rstd[:], eps, None, ALU.add, None, [rstd], [rstd])
            act(rstd[:], rstd[:], AF.Ln, [rstd], [rstd])
            act(rstd[:], rstd[:], AF.Exp, [rstd], [rstd], scale=-0.5)
            stt(nmr[:], mean[:], -1.0, rstd[:], ALU.mult, ALU.mult, [mean, rstd], [nmr])

        for ti in range(KT4):
            t0 = ti * NT
            ya = A.sb("ya", [128, 16, NT], BF16)
            yh = A.sb("yh", [128, 16, NT], BF16)
            xb = A.sb("xb", [128, 32, NT], BF16)
            mT = A.sb("mT", [128, 32, NT], BF16)
            off_keep = A.off
            slabs = [A.sb("ws%d" % i, [128, 96, 256], BF16) for i in range(2)]
            TMP = [A.sb("tmp%d" % i, [128, NT], F32) for i in range(2)]
            T.dma("sp", ya[:], rows(YA[:, :])[:, :, t0:t0 + NT], reads=[YA], writes=[ya], sembuf=ya)
            T.dma("sp", yh[:], rows(YH[:, :])[:, :, t0:t0 + NT], reads=[YH], writes=[yh], sembuf=yh)
            xsrc = rows(xT)[:, :, t0:t0 + NT]
            T.dma("pool", xb[:, 0:16, :], xsrc[:, 0:16, :], writes=[xb], sembuf=xb)
            T.dma("pool", xb[:, 16:32, :], xsrc[:, 16:32, :], writes=[xb], sembuf=xb)
            for cb in range(KCB):
                if cb % 2 == 0:
                    slab = slabs[(cb // 2) % 2]
                    i2 = cb // 2
                    T.dma("sp", slab[:, 0:16, :].rearrange("p k c -> p (k c)"), WAT[i2][:, :], reads=[WAT[i2]], writes=[slab], sembuf=slab)
                    T.dma("sp", slab[:, 16:32, :].rearrange("p k c -> p (k c)"), WBT[i2][:, :], reads=[WBT[i2]], writes=[slab], sembuf=slab)
                    T.dma("sp", slab[:, 32:64, :].rearrange("p k c -> p (k c)"), WGT[i2][0][:, :], reads=[WGT[i2][0]], writes=[slab],
                          sembuf=slab)
                    T.dma("sp", slab[:, 64:96, :].rearrange("p k c -> p (k c)"), WGT[i2][1][:, :], reads=[WGT[i2][1]], writes=[slab],
                          sembuf=slab)
                jo = (cb % 2) * 128
                js = slice(jo, jo + 128)
                pa, pb, ga, gh_ = PSF[0], PSF[1], PSF[2], PSF[3]
                for k in range(16):
                    mm(pa[:], slab[:, k, js], ya[:, k, :], k == 0, k == 15, [slab, ya], [pa], k == 15)
                for k in range(16):
                    mm(pb[:], slab[:, 16 + k, js], yh[:, k, :], k == 0, k == 15, [slab, yh], [pb], k == 15)
                for k in range(32):
                    mm(ga[:], slab[:, 32 + k, js], xb[:, k, :], k == 0, k == 31, [slab, xb], [ga], k == 31)
                for k in range(32):
                    mm(gh_[:], slab[:, 64 + k, js], xb[:, k, :], k == 0, k == 31, [slab, xb], [gh_], k == 31)
                sa, sh = TMP[0], TMP[1]
                act(sa[:], ga[:], AF.Sigmoid, [ga, vec], [sa], bias=vec[:, V_BG + cb:V_BG + cb + 1])
                act(sh[:], gh_[:], AF.Sigmoid, [gh_, vec], [sh], bias=vec[:, V_BG + 32 + cb:V_BG + 32 + cb + 1])
                tt(sa[:], sa[:], pa[:], ALU.mult, [sa, pa], [sa])
                tt(sh[:], sh[:], pb[:], ALU.mult, [sh, pb], [sh])
                tt(mT[:, cb, :], sa[:], sh[:], ALU.add, [sa, sh], [mT])
            T.barrier()
            A.off = A.base
            rT = A.sb("rT", [128, 32, NT], F32)
            assert A.off <= off_keep - 32 * NT * 2
            mT2 = mT
            A.off = off_keep
            slabs = [A.sb("wo%d" % i, [128, 32, 256], BF16) for i in range(3)]
            TMP = [A.sb("tq%d" % i, [128, NT], F32) for i in range(6)]
            TB = [A.sb("tbq%d" % i, [128, NT], BF16) for i in range(4)]
            s1, s2 = PSF[4], PSF[5]
            pend = []
            for cb in range(KCB):
                if cb % 2 == 0:
                    slab = slabs[(cb // 2) % 3]
                    T.dma("sp", slab[:].rearrange("p k c -> p (k c)"), WOT[cb // 2][:, :], reads=[WOT[cb // 2]], writes=[slab], sembuf=slab)
                js = slice((cb % 2) * 128, (cb % 2) * 128 + 128)
                acc = PSF[cb % 2]
                xf = TMP[cb % 2]
                T.dma("sp", xf[:], xT[cb * 128:(cb + 1) * 128, t0:t0 + NT], writes=[xf], sembuf=xf)
                for k in range(32):
                    mm(acc[:], slab[:, k, js], mT2[:, k, :], k == 0, k == 31, [slab, mT2], [acc], k == 31)
                while len(pend) > 1:
                    stat_mm(s1, s2, pend.pop(0))
                stt(rT[:, cb, :], xf[:], ALPHA, acc[:], ALU.mult, ALU.add, [xf, acc], [rT])
                rb, rsq = TB[(cb * 2) % 4], TB[(cb * 2 + 1) % 4]
                act(rb[:], rT[:, cb, :], AF.Copy, [rT], [rb])
                act(rsq[:], rT[:, cb, :], AF.Square, [rT], [rsq])
                pend.append((cb, rb, rsq))
            while pend:
                stat_mm(s1, s2, pend.pop(0))
            mean, rstd, nmr = TMP[2], TMP[3], TMP[4]
            ln_finish(s1, s2, mean, rstd, nmr, LN_EPS)
            hbufs = [TMP[0], TMP[1], TMP[5]]
            for cb in range(KCB):
                hb = hbufs[cb % 3]
                tt(hb[:], rT[:, cb, :], rstd[:], ALU.mult, [rT, rstd], [hb])
                tt(hb[:], hb[:], nmr[:], ALU.add, [hb, nmr], [hb])
                ts(hb[:], hb[:], vec[:, V_L1G + cb:V_L1G + cb + 1], vec[:, V_L1B + cb:V_L1B + cb + 1], ALU.mult, ALU.add,
                   [hb, vec], [hb])
                T.dma("sp", H1[cb * 128:(cb + 1) * 128, 1 + t0:1 + t0 + NT], hb[:], reads=[hb], writes=[H1], sembuf=hb)
                if ti == 0:
                    T.dma("sp", EDI[cb * 128:(cb + 1) * 128, 0:1], hb[:, 0:1], reads=[hb], writes=[EDI], sembuf=hb)
                if ti == NTT - 1:
                    T.dma("sp", EDI[cb * 128:(cb + 1) * 128, 1:2], hb[:, NT - 1:NT], reads=[hb], writes=[EDI], sembuf=hb)
            T.barrier(A.reset())

        allgather(EDI, EDO)
        Eg = A.sb("Eg", [128, 4, 32, 2], F32)
        hal = A.sb("hal", [128, 32, 2], F32)
        T.dma("sp", Eg[:], EDO[:, :].rearrange("(r k p) c -> p r k c", r=4, p=128), reads=[EDO], writes=[Eg], sembuf=Eg)
        T.op("pool", lambda e: e.memset(hal[:], 0.0), writes=[hal])
        for r in range(4):
            stt(hal[:, :, 0], Eg[:, r, :, 1], vec[:, V_ML + r:V_ML + r + 1], hal[:, :, 0], ALU.mult, ALU.add, [Eg, vec, hal], [hal])
            stt(hal[:, :, 1], Eg[:, r, :, 0], vec[:, V_MR + r:V_MR + r + 1], hal[:, :, 1], ALU.mult, ALU.add, [Eg, vec, hal], [hal])
        H1r = rows(H1[:, :])
        T.dma("sp", H1r[:, :, 0:1], hal[:, :, 0:1], reads=[hal], writes=[H1], sembuf=hal)
        T.dma("sp", H1r[:, :, TOK + 1:TOK + 2], hal[:, :, 1:2], reads=[hal], writes=[H1], sembuf=hal)
        T.barrier(A.reset())
        if KSTOP <= 5:
            T.barrier()
            raise _Stop()

        for ti in range(KT5):
            t0 = ti * NT
            aT = A.sb("aT", [128, NFB, NT], BF16)
            off_keep = A.off
            h1b = A.sb("h1b", [128, 32, NT + 2], BF16)
            slabs = [A.sb("wu%d" % i, [128, 64, 256], BF16) for i in range(2)]
            gsb = [A.sb("gsb%d" % i, [128, NT + 2], F32) for i in range(2)]
            TMP = [A.sb("tmp%d" % i, [128, NT], F32) for i in range(4)]
            hsrc = H1r[:, :, t0:t0 + NT + 2]
            T.dma("pool", h1b[:, 0:16, :], hsrc[:, 0:16, :], reads=[H1], writes=[h1b], sembuf=h1b)
            T.dma("pool", h1b[:, 16:32, :], hsrc[:, 16:32, :], reads=[H1], writes=[h1b], sembuf=h1b)
            for cb in range(KFB):
                if cb % 2 == 0:
                    slab = slabs[(cb // 2) % 2]
                    load_w(slab[:, 0:32, :], w_up, cb * 128, 256, 0, 32, slab)
                    load_w(slab[:, 32:64, :], w_up, DFF + cb * 128, 256, 0, 32, slab)
                js = slice((cb % 2) * 128, (cb % 2) * 128 + 128)
                up, gp, gh_ = PSF[cb % 2], PSF[2 + cb % 2], PSF[4 + cb % 2]
                for k in range(32):
                    mm(up[:], slab[:, k, js], h1b[:, k, 1:NT + 1], k == 0, k == 31, [slab, h1b], [up], k == 31)
                for k in range(32):
                    mm(gp[:], slab[:, 32 + k, js], h1b[:, k, 1:NT + 1], k == 0, k == 31, [slab, h1b], [gp], k == 31)
                for k in range(32):
                    mm(gh_[:, 0:2], slab[:, 32 + k, js], h1b[:, k, 0:NT + 2:NT + 1], k == 0, k == 31, [slab, h1b], [gh_], k == 31)
                gs = gsb[cb % 2]
                act(gs[:, 1:NT + 1], gp[:], AF.Copy, [gp], [gs])
                cp(gs[:, 0:NT + 2:NT + 1], gh_[:, 0:2], [gh_], [gs])
                c_ = TMP[cb % 2]
                ts(c_[:], gs[:, 0:NT], vec[:, V_CW + cb:V_CW + cb + 1], vec[:, V_CB + cb:V_CB + cb + 1], ALU.mult, ALU.add,
                   [gs, vec], [c_])
                stt(c_[:], gs[:, 1:NT + 1], vec[:, V_CW + NFB + cb:V_CW + NFB + cb + 1], c_[:], ALU.mult, ALU.add, [gs, vec, c_], [c_])
                stt(c_[:], gs[:, 2:NT + 2], vec[:, V_CW + 2 * NFB + cb:V_CW + 2 * NFB + cb + 1], c_[:], ALU.mult, ALU.add,
                    [gs, vec, c_], [c_])
                act(c_[:], c_[:], AF.Silu, [c_], [c_])
                tt(aT[:, cb, :], c_[:], up[:], ALU.mult, [c_, up], [aT])
            T.barrier()
            A.off = off_keep
            rT = A.sb("rT", [128, 32, NT], F32)
            TMP = [A.sb("tq%d" % i, [128, NT], F32) for i in range(5)]
            TB = [A.sb("tbq%d" % i, [128, NT], BF16) for i in range(4)]
            off_slabs = A.off
            slabs = [A.sb("wd%d" % i, [128, 22, 256], BF16) for i in range(3)]
            s1, s2 = PSF[4], PSF[5]
            pend = []
            for cb2 in range(KCB // 2):
                accs = [PSF[(cb2 % 2) * 2], PSF[(cb2 % 2) * 2 + 1]]
                for q, (k0, kq) in enumerate(QK4):
                    slab = slabs[(cb2 * 4 + q) % 3]
                    T.dma("sp", slab[:, 0:kq, :].rearrange("p k c -> p (k c)"), WDT[cb2][q][:, 0:kq * 256], reads=[WDT[cb2][q]],
                          writes=[slab], sembuf=slab)
                    for j in range(2):
                        for k in range(kq):
                            mm(accs[j][:], slab[:, k, j * 128:(j + 1) * 128], aT[:, k0 + k, :], q == 0 and k == 0,
                               q == 3 and k == kq - 1, [slab, aT], [accs[j]], k == kq - 1)
                while pend:
                    stat_mm(s1, s2, pend.pop(0))
                for j in range(2):
                    cb = cb2 * 2 + j
                    acc = accs[j]
                    hf = TMP[cb % 2]
                    T.dma("sp", hf[:], H1[cb * 128:(cb + 1) * 128, 1 + t0:1 + t0 + NT], reads=[H1], writes=[hf], sembuf=hf)
                    stt(rT[:, cb, :], hf[:], ALPHA, acc[:], ALU.mult, ALU.add, [hf, acc], [rT])
                    rb, rsq = TB[(cb * 2) % 4], TB[(cb * 2 + 1) % 4]
                    act(rb[:], rT[:, cb, :], AF.Copy, [rT], [rb])
                    act(rsq[:], rT[:, cb, :], AF.Square, [rT], [rsq])
                    pend.append((cb, rb, rsq))
            while pend:
                stat_mm(s1, s2, pend.pop(0))
            mean, rstd, nmr = TMP[2], TMP[3], TMP[4]
            ln_finish(s1, s2, mean, rstd, nmr, LN_EPS)
            T.barrier()
            A.off = A.base
            x2b = A.sb("x2b", [128, 32, NT], BF16)
            pTb = A.sb("pTb", [128, 2, NT], BF16)
            wple = A.sb("wple", [128, 2, D], BF16)
            assert A.off <= off_keep
            A.off = off_slabs
            slabs = [A.sb("wg%d" % i, [128, 32, 256], BF16) for i in range(2)]
            T.dma("pool", pTb[:], rows(pT)[:, :, t0:t0 + NT], writes=[pTb], sembuf=pTb)
            T.dma("pool", wple[:], rows(w_ple)[:, :, :], writes=[wple], sembuf=wple)
            for cb in range(KCB):
                r_ = rT[:, cb, :]
                tt(r_, r_, rstd[:], ALU.mult, [rT, rstd], [rT])
                tt(r_, r_, nmr[:], ALU.add, [rT, nmr], [rT])
                ts(r_, r_, vec[:, V_L2G + cb:V_L2G + cb + 1], vec[:, V_L2B + cb:V_L2B + cb + 1], ALU.mult, ALU.add, [rT, vec], [rT])
                act(x2b[:, cb, :], r_, AF.Copy, [rT], [x2b])
            for cb in range(KCB):
                if cb % 2 == 0:
                    slab = slabs[(cb // 2) % 2]
                    T.dma("sp", slab[:].rearrange("p k c -> p (k c)"), WPT[cb // 2][:, :], reads=[WPT[cb // 2]], writes=[slab], sembuf=slab)
                js = slice((cb % 2) * 128, (cb % 2) * 128 + 128)
                pg, pl = PSF[cb % 2], PSF[2 + cb % 2]
                for k in range(32):
                    mm(pg[:], slab[:, k, js], x2b[:, k, :], k == 0, k == 31, [slab, x2b], [pg], k == 31)
                for k in range(2):
                    mm(pl[:], wple[:, k, cb * 128:(cb + 1) * 128], pTb[:, k, :], k == 0, k == 1, [wple, pTb], [pl], k == 1)
                s_ = TMP[cb % 2]
                act(s_[:], pg[:], AF.Sigmoid, [pg], [s_])
                tt(s_[:], s_[:], pl[:], ALU.mult, [s_, pl], [s_])
                tt(s_[:], s_[:], rT[:, cb, :], ALU.add, [s_, rT], [s_])
                T.dma("sp", outT[cb * 128:(cb + 1) * 128, t0:t0 + NT], s_[:], reads=[s_], sembuf=s_)
            T.barrier(A.reset())
        T.barrier()
        print("kernel build: ninst=%d nwaits=%d dma_sems=%d" % (T.ninst, T.nwaits, len(T.dma_sems)), flush=True)
    return nc


def _consts():
    i = np.arange(128)
    R = np.zeros((128, 128), np.float32)
    for a in range(128):
        sec = a // 64
        loc = a % 64
        if loc < 32:
            R[a, sec * 64 + loc + 32] = -1.0
        else:
            R[a, sec * 64 + loc - 32] = 1.0
    RT = R.T.copy()
    ident = np.eye(128, dtype=np.float32)
    ones = np.ones((128, 128), np.float32)
    s = i[:, None]
    t = i[None, :]
    same = (s // 64) == (t // 64)
    maskF = (same & (s <= t)).astype(np.float32)
    maskB = (same & (s >= t)).astype(np.float32)
    return np.concatenate([RT, ident, ones, maskF, maskB], axis=1).astype(np.float32)


def _rope_tables(tok0):
    t = np.arange(tok0, tok0 + TOK)
    row = (t // 64).astype(np.float32)
    col = (t % 64).astype(np.float32)
    sec = 64
    inv = (10000.0 ** (-np.arange(0, sec, 2, dtype=np.float32) / sec)).astype(np.float32)
    ang_r = row[:, None] * inv[None, :]
    ang_c = col[:, None] * inv[None, :]
    ang = np.concatenate([ang_r, ang_r, ang_c, ang_c], axis=-1).astype(np.float32)
    return np.ascontiguousarray(np.cos(ang).T.astype(np.float32)), np.ascontiguousarray(np.sin(ang).T.astype(np.float32))


_NC_CACHE = {}


def kernel(x, p, w_in, q_norm, k_norm, lb_logits, hg_norm, w_pa, w_pb, w_gate, b_gate, w_o,
           ln1_g, ln1_b, w_up, conv_w, conv_b, w_down, ln2_g, ln2_b, w_pg, w_ple):
    f = lambda a: np.ascontiguousarray(np.asarray(a, dtype=np.float32))
    x = f(x); p = f(p)
    col = lambda v, n: f(v).reshape(n, 128).T
    vec = np.zeros((128, NV), np.float32)
    vec[:, V_BG:V_BG + 64] = col(b_gate[0], 64)
    vec[:, V_L1G:V_L1G + 32] = col(ln1_g[0], 32)
    vec[:, V_L1B:V_L1B + 32] = col(ln1_b[0], 32)
    vec[:, V_L2G:V_L2G + 32] = col(ln2_g[0], 32)
    vec[:, V_L2B:V_L2B + 32] = col(ln2_b[0], 32)
    cw = f(conv_w)[0]
    for tap in range(3):
        vec[:, V_CW + tap * NFB:V_CW + (tap + 1) * NFB] = col(cw[tap], NFB)
    vec[:, V_CB:V_CB + NFB] = col(conv_b[0], NFB)
    vec[:, V_QN] = f(q_norm)[0]
    vec[:, V_KN] = f(k_norm)[0]
    vec[:, V_HGN:V_HGN + 16] = col(hg_norm[0], 16)
    lbl = f(lb_logits)
    for d in range(2):
        for l in range(2):
            vec[:, V_LBL + (d * 2 + l) * 16:V_LBL + (d * 2 + l) * 16 + 16] = col(lbl[d, l], 16)
    cst = _consts()
    weights = {"w_in": f(w_in)[0], "w_pa": f(w_pa)[0], "w_pb": f(w_pb)[0], "w_gate": f(w_gate)[0], "w_o": f(w_o)[0],
               "w_up": f(w_up)[0], "w_down": f(w_down)[0], "w_pg": f(w_pg)[0], "w_ple": f(w_ple)[0]}
    in_maps = []
    for c in range(8):
        b, s = c // 4, c % 4
        v = vec.copy()
        for r in range(4):
            v[:, V_MF + r] = 1.0 if r < s else 0.0
            v[:, V_MB + r] = 1.0 if r > s else 0.0
            v[:, V_ML + r] = 1.0 if r == s - 1 else 0.0
            v[:, V_MR + r] = 1.0 if r == s + 1 else 0.0
            v[:, V_OMF + r] = 0.0 if r < s else 1.0
            v[:, V_OMB + r] = 0.0 if r > s else 1.0
        cosT, sinT = _rope_tables(s * TOK)
        m = {"xT": np.ascontiguousarray(x[b, s * TOK:(s + 1) * TOK, :].T),
             "pT": np.ascontiguousarray(p[0, b, s * TOK:(s + 1) * TOK, :].T),
             "vec": v, "cst": cst, "cosT": cosT, "sinT": sinT}
        m.update(weights)
        in_maps.append(m)
    if "nc" not in _NC_CACHE:
        try:
            build_nc()
        except _Stop:
            pass
    in_maps = [{k: v for k, v in m.items() if k in _NC_CACHE["names"]} for m in in_maps]
    res = run_bass_kernel_spmd(_NC_CACHE["nc"], in_maps, core_ids=list(range(8)))
    out = np.empty((2, 4 * TOK, D), np.float32)
    for c in range(8):
        b, s = c // 4, c % 4
        out[b, s * TOK:(s + 1) * TOK, :] = res.results[c]["outT"].T
    return out
```

```python
from contextlib import ExitStack
import os
import numpy as np
import concourse.bass as bass
import concourse.mybir as mybir
from concourse.bass_utils import run_bass_kernel_spmd

F32 = mybir.dt.float32
BF16 = mybir.dt.bfloat16
AF = mybir.ActivationFunctionType
ALU = mybir.AluOpType

D = 4096
TOK = 2048
NT = 512
NTT = 4
DFF = 11008
NFB = 86
ALPHA = 2.0 ** 0.25
RMS_EPS = 1e-6
LN_EPS = 1e-5
SCALE = 128 ** -0.5
ESHIFT = -4.0

V_BG = 0
V_L1G = 64
V_L1B = 96
V_L2G = 128
V_L2B = 160
V_CW = 192
V_CB = 450
V_QN = 536
V_KN = 537
V_HGN = 538
V_LBL = 554
V_MF = 618
V_MB = 622
V_ML = 626
V_MR = 630
V_OMF = 634
V_OMB = 638
NV = 642
C_RT, C_ID, C_ONE, C_MF, C_MB = 0, 128, 256, 384, 512


class _Stop(Exception):
    pass


class Buf:
    __slots__ = ("name", "t", "last_write", "reads", "sem")

    def __init__(self, name, t=None):
        self.name = name
        self.t = t
        self.last_write = None
        self.reads = []
        self.sem = None

    def __getitem__(self, idx):
        return self.t[idx]


class Trk:
    ENG = ("pe", "act", "dve", "pool", "sp")

    def __init__(self, nc, stack):
        self.nc = nc
        self.stack = stack
        self.eng = {"pe": nc.tensor, "act": nc.scalar, "dve": nc.vector, "pool": nc.gpsimd, "sp": nc.sync}
        self.sem = {}
        self.cnt = {}
        for e in ("pe", "act", "dve", "pool"):
            self.sem[e] = stack.enter_context(nc.semaphore("c_" + e))
            self.cnt[e] = 0
        self.waited = {e: {} for e in self.ENG}
        self.dma_sems = []
        self.free_sems = {}
        self.nwaits = 0
        self.ninst = 0

    def _wait(self, e, ev):
        kind, s, v = ev
        if kind == "c":
            if s == e and e == "pe":
                return
            sem = self.sem[s]
            key = "c" + s
        else:
            sem = s[0]
            key = id(s)
            v = s[1]
        w = self.waited[e]
        if w.get(key, -1) >= v:
            return
        w[key] = v
        self.eng[e].wait_ge(sem, v)
        self.nwaits += 1

    def _deps(self, e, reads, writes):
        for b in reads:
            if b.last_write is not None:
                self._wait(e, b.last_write)
        for b in writes:
            if b.last_write is not None:
                self._wait(e, b.last_write)
            for ev in b.reads:
                self._wait(e, ev)

    def _reg(self, ev, reads, writes):
        for b in reads:
            b.reads.append(ev)
        for b in writes:
            b.last_write = ev
            b.reads = []

    def op(self, e, fn, reads=(), writes=(), signal=True):
        self._deps(e, reads, writes)
        inst = fn(self.eng[e])
        self.ninst += 1
        if not signal:
            return None
        self.cnt[e] += 1
        inst.then_inc(self.sem[e], 1)
        ev = ("c", e, self.cnt[e])
        self._reg(ev, reads, writes)
        return ev

    def _dsem(self, b, q):
        if b.sem is None:
            b.sem = {}
        if q not in b.sem:
            fl = self.free_sems.setdefault(q, [])
            if fl:
                b.sem[q] = fl.pop()
            else:
                rec = [self.stack.enter_context(self.nc.semaphore("d_%d" % len(self.dma_sems))), 0]
                self.dma_sems.append(rec)
                b.sem[q] = rec
        return b.sem[q]

    def dma(self, q, out, in_, reads=(), writes=(), sembuf=None):
        self._deps(q, reads, writes)
        s = self._dsem(sembuf, q)
        inst = self.eng[q].dma_start(out=out, in_=in_)
        s[1] += 16
        inst.then_inc(s[0], 16)
        self.ninst += 1
        ev = ("d", s, s[1])
        self._reg(ev, reads, writes)
        return ev

    def custom(self, e, fn, owner, inc, reads=(), writes=()):
        self._deps(e, reads, writes)
        s = self._dsem(owner, "cc")
        inst = fn(self.eng[e])
        s[1] += inc
        inst.then_inc(s[0], inc)
        ev = ("d", s, s[1])
        self._reg(ev, reads, writes)
        return ev

    def barrier(self, release=()):
        for e in self.ENG:
            for s in ("pe", "act", "dve", "pool"):
                if self.cnt[s] > 0:
                    self._wait(e, ("c", s, self.cnt[s]))
            for s in self.dma_sems:
                if s[1] > 0:
                    self._wait(e, ("d", s, s[1]))
        for b in release:
            if b.sem is not None:
                for q, rec in b.sem.items():
                    self.free_sems.setdefault(q, []).append(rec)
                b.sem = None


class Arena:
    def __init__(self, nc, base=0):
        self.nc = nc
        self.base = base
        self.off = base
        self.n = 0
        self.bufs = []

    def reset(self):
        self.off = self.base
        b = self.bufs
        self.bufs = []
        return b

    def sb(self, name, shape, dt):
        esz = 4 if dt == F32 else 2
        per = esz
        for s in shape[1:]:
            per *= s
        self.off = (self.off + 63) // 64 * 64
        self.n += 1
        t = self.nc.alloc_sbuf_tensor_at("%s_%d" % (name, self.n), list(shape), dt, offset=self.off)
        self.off += per
        assert self.off <= 16384 + 211000, (name, self.off)
        b = Buf(name, t)
        self.bufs.append(b)
        return b


def build_nc():
    nc = bass.Bass("TRN2", target_bir_lowering=False)
    _NC_CACHE["nc"] = nc

    KSTOP = int(os.environ.get("KSTOP", "99"))
    KP1T = int(os.environ.get("KP1T", str(NTT)))
    KP1C = int(os.environ.get("KP1C", "104"))
    KCOLL = int(os.environ.get("KCOLL", "1"))
    KH = int(os.environ.get("KH", "16"))
    KG = int(os.environ.get("KG", "4"))
    KQT = int(os.environ.get("KQT", str(NTT)))
    KT4 = int(os.environ.get("KT4", str(NTT)))
    KT5 = int(os.environ.get("KT5", str(NTT)))
    KCB = int(os.environ.get("KCB", "32"))
    KFB = int(os.environ.get("KFB", str(NFB)))
    names = _NC_CACHE.setdefault("names", set())

    def din(name, shape, dt=F32):
        if KSTOP <= 3 and name in ("pT", "w_pa", "w_pb", "w_gate", "w_o", "w_up", "w_down", "w_pg", "w_ple"):
            return None
        names.add(name)
        return nc.dram_tensor(name, shape, dt, kind="ExternalInput").ap()

    xT = din("xT", [D, TOK])
    pT = din("pT", [256, TOK])
    w_in = din("w_in", [D, 13312])
    w_pa = din("w_pa", [2048, D])
    w_pb = din("w_pb", [2048, D])
    w_gate = din("w_gate", [D, 2 * D])
    w_o = din("w_o", [D, D])
    w_up = din("w_up", [D, 2 * DFF])
    w_down = din("w_down", [DFF, D])
    w_pg = din("w_pg", [D, D])
    w_ple = din("w_ple", [256, D])
    vec_d = din("vec", [128, NV])
    cst_d = din("cst", [128, 640])
    cos_d = din("cosT", [128, TOK])
    sin_d = din("sinT", [128, TOK])
    outT = nc.dram_tensor("outT", [D, TOK], F32, kind="ExternalOutput").ap()

    def scr(name, shape, dt):
        t = nc.dram_tensor(name, shape, dt)
        _NC_CACHE.setdefault("scratch", []).append(name)
        return Buf(name, t.ap())

    QS = scr("QS", [2048, TOK], BF16)
    KCI = [scr("KCI%d" % g, [128, TOK], BF16) for g in range(4)]
    KCO = [scr("KCO%d" % g, [512, TOK], BF16) for g in range(4)]
    VCI = [scr("VCI%d" % g, [TOK, 128], BF16) for g in range(4)]
    VCO = [scr("VCO%d" % g, [4 * TOK, 128], BF16) for g in range(4)]
    HQ = scr("HQ", [2048, TOK], F32)
    HZF = scr("HZF", [2048, TOK], F32)
    HZB = scr("HZB", [2048, TOK], F32)
    HG = scr("HG", [2048, TOK], F32)
    HV = scr("HV", [TOK, 2048], BF16)
    OLOC = scr("OLOC", [2048, TOK], F32)
    QGF = scr("QGF", [2048, TOK], BF16)
    QGB = scr("QGB", [2048, TOK], BF16)
    SCI = [scr("SCI%d" % i, [8 * 128, 128], F32) for i in range(4)]
    SCO = [scr("SCO%d" % i, [4 * 8 * 128, 128], F32) for i in range(4)]
    DCI = scr("DCI", [128, 32], F32)
    DCO = scr("DCO", [512, 32], F32)
    YA = scr("YA", [2048, TOK], BF16)
    YH = scr("YH", [2048, TOK], BF16)
    H1 = scr("H1", [D, TOK + 2], F32)
    EDI = scr("EDI", [D, 2], F32)
    EDO = scr("EDO", [4 * D, 2], F32)
    WOT_t = nc.dram_tensor("WOT", [16, 128, 32 * 256], BF16)
    WPT_t = nc.dram_tensor("WPT", [16, 128, 32 * 256], BF16)
    WDT_t = nc.dram_tensor("WDT", [16, 4, 128, 22 * 256], BF16)
    WAT_t = nc.dram_tensor("WAT", [16, 128, 16 * 256], BF16)
    WBT_t = nc.dram_tensor("WBT", [16, 128, 16 * 256], BF16)
    WGT_t = nc.dram_tensor("WGT", [16, 2, 128, 32 * 256], BF16)
    WAT = [Buf("WAT%d" % i, WAT_t.ap()[i]) for i in range(16)]
    WBT = [Buf("WBT%d" % i, WBT_t.ap()[i]) for i in range(16)]
    WGT = [[Buf("WGT%d_%d" % (i, j), WGT_t.ap()[i, j]) for j in range(2)] for i in range(16)]
    WOT = [Buf("WOT%d" % i, WOT_t.ap()[i]) for i in range(16)]
    WPT = [Buf("WPT%d" % i, WPT_t.ap()[i]) for i in range(16)]
    WDT = [[Buf("WDT%d_%d" % (i, q), WDT_t.ap()[i, q]) for q in range(4)] for i in range(16)]
    QK4 = [(0, 22), (22, 21), (43, 22), (65, 21)]
    convsem = Buf("convsem")
    WUT_t = nc.dram_tensor("WUT", [43, 2, 128, 32 * 256], BF16)
    WUT = [[Buf("WUT%d_%d" % (i, j), WUT_t.ap()[i, j]) for j in range(2)] for i in range(43)]
    convsem2 = Buf("convsem2")
    GROUPS = [[0, 1, 2, 3], [4, 5, 6, 7]]

    with ExitStack() as st:
        st.enter_context(nc.allow_non_contiguous_dma(reason="single-column halo/edge transfers"))
        T = Trk(nc, st)
        PSF = [Buf("psf%d" % i, st.enter_context(nc.psum_tensor("psf%d" % i, [128, 512], F32))) for i in range(6)]
        PTB = [Buf("ptb%d" % i, st.enter_context(nc.psum_tensor("ptb%d" % i, [128, 8, 128], BF16))[:, 0:4, :]) for i in range(2)]

        A0 = Arena(nc, 16384)
        vec = A0.sb("vec", [128, NV], F32)
        cst = A0.sb("cst", [128, 640], F32)
        cb16 = A0.sb("cb16", [128, 640], BF16)
        rtq = A0.sb("rtq", [128, 128], BF16)
        rtk = A0.sb("rtk", [128, 128], BF16)
        lbt = A0.sb("lbt", [128, 64], F32)
        A = Arena(nc, A0.off)

        T.dma("sp", vec[:], vec_d[:, :], writes=[vec], sembuf=vec)
        T.dma("sp", cst[:], cst_d[:, :], writes=[cst], sembuf=cst)
        T.op("dve", lambda e: e.tensor_copy(out=cb16[:], in_=cst[:]), reads=[cst], writes=[cb16])
        T.op("dve", lambda e: e.tensor_scalar(out=rtq[:], in0=cst[:, C_RT:C_RT + 128], scalar1=vec[:, V_QN:V_QN + 1],
                                              scalar2=None, op0=ALU.mult), reads=[cst, vec], writes=[rtq])
        T.op("dve", lambda e: e.tensor_scalar(out=rtk[:], in0=cst[:, C_RT:C_RT + 128], scalar1=vec[:, V_KN:V_KN + 1],
                                              scalar2=None, op0=ALU.mult), reads=[cst, vec], writes=[rtk])
        for d in range(2):
            T.op("dve", lambda e, d=d: e.tensor_tensor(out=lbt[:, d * 16:(d + 1) * 16],
                                                       in0=vec[:, V_LBL + (d * 2) * 16:V_LBL + (d * 2) * 16 + 16],
                                                       in1=vec[:, V_LBL + (d * 2 + 1) * 16:V_LBL + (d * 2 + 1) * 16 + 16],
                                                       op=ALU.subtract), reads=[vec], writes=[lbt])
        T.op("act", lambda e: e.activation(out=lbt[:, 0:32], in_=lbt[:, 0:32], func=AF.Sigmoid), reads=[lbt], writes=[lbt])
        T.op("dve", lambda e: e.tensor_scalar(out=lbt[:, 32:64], in0=lbt[:, 0:32], scalar1=-1.0, scalar2=1.0,
                                              op0=ALU.mult, op1=ALU.add), reads=[lbt], writes=[lbt])
        onesb = cb16[:, C_ONE:C_ONE + 128]
        identb = cb16[:, C_ID:C_ID + 128]

        def mm(out, lhsT, rhs, start, stop, reads, writes, signal):
            T.op("pe", lambda e: e.matmul(out, lhsT, rhs, start=start, stop=stop), reads=reads, writes=writes, signal=signal)

        def act(out, in_, func, reads, writes, **kw):
            T.op("act", lambda e: e.activation(out=out, in_=in_, func=func, **kw), reads=reads, writes=writes)

        def tt(out, in0, in1, op, reads, writes, eng="dve"):
            T.op(eng, lambda e: e.tensor_tensor(out=out, in0=in0, in1=in1, op=op), reads=reads, writes=writes)

        def ts(out, in0, s1, s2, op0, op1, reads, writes, eng="dve"):
            if s2 is None:
                T.op(eng, lambda e: e.tensor_scalar(out=out, in0=in0, scalar1=s1, scalar2=None, op0=op0), reads=reads, writes=writes)
            else:
                T.op(eng, lambda e: e.tensor_scalar(out=out, in0=in0, scalar1=s1, scalar2=s2, op0=op0, op1=op1),
                     reads=reads, writes=writes)

        def stt(out, in0, sc, in1, op0, op1, reads, writes):
            T.op("dve", lambda e: e.scalar_tensor_tensor(out=out, in0=in0, scalar=sc, in1=in1, op0=op0, op1=op1),
                 reads=reads, writes=writes)

        def cp(out, in_, reads, writes, eng="dve"):
            T.op(eng, lambda e: e.tensor_copy(out=out, in_=in_), reads=reads, writes=writes)

        def rows(ap2d, p=128):
            return ap2d.rearrange("(k p) c -> p k c", p=p)

        def load_w(slab_view, w2d, c0, ncols, k0, kc, slab):
            src = rows(w2d)[:, k0:k0 + kc, c0:c0 + ncols]
            h = (kc + 1) // 2
            T.dma("pool", slab_view[:, 0:h, :], src[:, 0:h, :], writes=[slab], sembuf=slab)
            if kc > h:
                T.dma("pool", slab_view[:, h:kc, :], src[:, h:kc, :], writes=[slab], sembuf=slab)

        def rstd_from_ss(dst, ss_ap, ssbuf, tmpbuf, scale, eps, tmp_ap=None):
            ta = tmpbuf[:] if tmp_ap is None else tmp_ap
            ts(ta, ss_ap, scale, eps, ALU.mult, ALU.add, [ssbuf], [tmpbuf])
            act(ta, ta, AF.Ln, [tmpbuf], [tmpbuf])
            act(dst[:], ta, AF.Exp, [tmpbuf], [dst], scale=-0.5)

        xbs = [A.sb("xb%d" % i, [128, 32, NT], BF16) for i in range(2)]
        slabs = [A.sb("ws%d" % i, [128, 32, 256], BF16) for i in range(3)]
        cosbs = [A.sb("cos%d" % i, [128, NT], F32) for i in range(2)]
        sinbs = [A.sb("sin%d" % i, [128, NT], F32) for i in range(2)]
        TMP = [A.sb("tmp%d" % i, [128, NT], F32) for i in range(8)]
        TB = [A.sb("tb%d" % i, [128, NT], BF16) for i in range(8)]
        VT = [A.sb("vt%d" % i, [128, 4, 128], BF16) for i in range(2)]
        ntmp = [0]

        def tmp():
            ntmp[0] += 1
            return TMP[ntmp[0] % len(TMP)]

        ntb = [0]

        def tb():
            ntb[0] += 1
            return TB[ntb[0] % len(TB)]

        it = 0
        for st_ in range(KP1T // 2):
          for half in range(2):
            t0 = (st_ * 2 + half) * NT
            xsrc = rows(xT)[:, :, t0:t0 + NT]
            T.dma("pool", xbs[half][:, 0:16, :], xsrc[:, 0:16, :], writes=[xbs[half]], sembuf=xbs[half])
            T.dma("pool", xbs[half][:, 16:32, :], xsrc[:, 16:32, :], writes=[xbs[half]], sembuf=xbs[half])
            T.dma("sp", cosbs[half][:], cos_d[:, t0:t0 + NT], writes=[cosbs[half]], sembuf=cosbs[half])
            T.dma("sp", sinbs[half][:], sin_d[:, t0:t0 + NT], writes=[sinbs[half]], sembuf=sinbs[half])
          for cbk in range(KP1C):
            if cbk % 2 == 0:
                slab = slabs[(cbk // 2) % 3]
                load_w(slab[:], w_in, cbk * 128, 256, 0, 32, slab)
            for half in range(2):
                ti = st_ * 2 + half
                t0 = ti * NT
                xb, cosb, sinb = xbs[half], cosbs[half], sinbs[half]
                jo = (cbk % 2) * 128
                acc = PSF[it % 3]
                it += 1
                for k in range(32):
                    mm(acc[:], slab[:, k, jo:jo + 128], xb[:, k, :], k == 0, k == 31, [slab, xb], [acc], k == 31)
                if cbk < 20:
                    isq = cbk < 16
                    gcol = V_QN if isq else V_KN
                    rt = rtq if isq else rtk
                    qb_, sq_ = tb(), tb()
                    act(qb_[:], acc[:], AF.Copy, [acc], [qb_])
                    act(sq_[:], acc[:], AF.Square, [acc], [sq_])
                    ss, rq = PSF[3], PSF[4]
                    mm(ss[:], onesb, sq_[:], True, True, [cb16, sq_], [ss], True)
                    mm(rq[:], rt[:], qb_[:], True, True, [rt, qb_], [rq], True)
                    rstd, tl = tmp(), tmp()
                    rstd_from_ss(rstd, ss[:], ss, tl, 1.0 / 128, RMS_EPS)
                    a_, b_ = tmp(), tmp()
                    stt(a_[:], acc[:], vec[:, gcol:gcol + 1], cosb[:], ALU.mult, ALU.mult, [acc, vec, cosb], [a_])
                    tt(b_[:], rq[:], sinb[:], ALU.mult, [rq, sinb], [b_])
                    tt(a_[:], a_[:], b_[:], ALU.add, [a_, b_], [a_])
                    ob = tb()
                    tt(ob[:], a_[:], rstd[:], ALU.mult, [a_, rstd], [ob])
                    if isq:
                        T.dma("sp", QS[cbk * 128:(cbk + 1) * 128, t0:t0 + NT], ob[:], reads=[ob], writes=[QS], sembuf=ob)
                    else:
                        g = cbk - 16
                        T.dma("sp", KCI[g][:, t0:t0 + NT], ob[:], reads=[ob], writes=[KCI[g]], sembuf=ob)
                elif cbk < 24 or 72 <= cbk < 88:
                    vb_ = tb()
                    act(vb_[:], acc[:], AF.Copy, [acc], [vb_])
                    pt = PTB[it % 2]
                    for j in range(4):
                        T.op("pe", lambda e, j=j, pt=pt, vb_=vb_: e.transpose(pt[:, j, :], vb_[:, j * 128:(j + 1) * 128], identb),
                             reads=[vb_, cb16], writes=[pt], signal=(j == 3))
                    vt = VT[it % 2]
                    cp(vt[:], pt[:], [pt], [vt])
                    if cbk < 24:
                        g = cbk - 20
                        dst = rows(VCI[g][:, :])[:, ti * 4:ti * 4 + 4, :]
                        T.dma("sp", dst, vt[:], reads=[vt], writes=[VCI[g]], sembuf=vt)
                    else:
                        h = cbk - 72
                        dst = rows(HV[:, :])[:, ti * 4:ti * 4 + 4, h * 128:(h + 1) * 128]
                        T.dma("sp", dst, vt[:], reads=[vt], writes=[HV], sembuf=vt)
                else:
                    if cbk < 40:
                        dstb, h = HQ, cbk - 24
                    elif cbk < 56:
                        dstb, h = HZF, cbk - 40
                    elif cbk < 72:
                        dstb, h = HZB, cbk - 56
                    else:
                        dstb, h = HG, cbk - 88
                    o_ = tmp()
                    if cbk % 2 == 0:
                        act(o_[:], acc[:], AF.Copy, [acc], [o_])
                    else:
                        cp(o_[:], acc[:], [acc], [o_])
                    T.dma("sp", dstb[h * 128:(h + 1) * 128, t0:t0 + NT], o_[:], reads=[o_], writes=[dstb], sembuf=o_)

        def allgather(src, dst):
            T.custom("pool", lambda e: e.collective_compute("AllGather", ALU.bypass, replica_groups=GROUPS,
                                                            ins=[src.t.opt()], outs=[dst.t.opt()]),
                     dst, 1, reads=[src], writes=[dst])

        if KCOLL:
            for g in range(4):
                allgather(KCI[g], KCO[g])
                allgather(VCI[g], VCO[g])
        T.barrier(A.reset())
        if KSTOP <= 1:
            T.barrier()
            raise _Stop()

        qf = A.sb("qf", [128, TOK], F32)
        zr = [A.sb("zr%d" % i, [128, TOK], F32) for i in range(2)]
        kf = A.sb("kf", [128, TOK], F32)
        Pp = A.sb("Pp", [128, TOK + 64], F32)
        onesf = A.sb("onesf", [128, TOK], F32)
        dt_ = A.sb("dt", [128, TOK], F32)
        et_ = A.sb("et", [128, TOK], F32)
        oacc2 = [A.sb("oacc%d" % i, [128, TOK], F32) for i in range(2)]
        vtok2 = [A.sb("vtok%d" % i, [128, 16, 128], BF16) for i in range(2)]
        qin2 = [[A.sb("qin%d" % i, [128, TOK], BF16) for i in range(2)] for _ in range(2)]
        kin2 = [[A.sb("kin%d" % i, [128, TOK], BF16) for i in range(2)] for _ in range(2)]
        kdec = A.sb("kdec", [128, TOK], BF16)
        kdt2 = [[A.sb("kdt%d" % i, [128, 16, 128], BF16) for i in range(2)] for _ in range(2)]
        qdec2 = [[A.sb("qdec%d" % i, [128, TOK], BF16) for i in range(2)] for _ in range(2)]
        qgl = A.sb("qgl", [128, TOK], BF16)
        dec2 = [[A.sb("dec%d" % i, [128, 32], F32) for i in range(2)] for _ in range(2)]
        dct = A.sb("dct", [128, 32], F32)
        Sf = [A.sb("Sf%d" % i, [128, 128], F32) for i in range(2)]
        Sb_ = [A.sb("Sb%d" % i, [128, 128], BF16) for i in range(2)]
        amt = [A.sb("amt%d" % i, [128, 128], BF16) for i in range(2)]
        T.op("pool", lambda e: e.memset(onesf[:], 1.0), writes=[onesf])
        T.op("pool", lambda e: e.memset(dct[:], 0.0), writes=[dct])

        def bc(ref):
            return ref.unsqueeze(2).to_broadcast([128, 32, 64])

        def v3(ap):
            return ap.rearrange("p (c t) -> p c t", t=64)

        def h1_prep(h):
                hs = slice(h * 128, (h + 1) * 128)
                hp = h % 2
                oacc, vtok, qin, kin, kdt, qdec, dec = oacc2[hp], vtok2[hp], qin2[hp], kin2[hp], kdt2[hp], qdec2[hp], dec2[hp]
                T.dma("sp", qf[:], HQ[hs, :], reads=[HQ], writes=[qf], sembuf=qf)
                yield
                T.dma("sp", zr[0][:], HZF[hs, :], reads=[HZF], writes=[zr[0]], sembuf=zr[0])
                yield
                T.dma("sp", zr[1][:], HZB[hs, :], reads=[HZB], writes=[zr[1]], sembuf=zr[1])
                yield
                T.dma("sp", vtok[:], rows(HV[:, :])[:, :, hs], reads=[HV], writes=[vtok], sembuf=vtok)
                yield
                act(qf[:], qf[:], AF.Silu, [qf], [qf])
                yield
                T.op("pool", lambda e: e.memset(oacc[:], 0.0), writes=[oacc])
                yield
                for d in range(2):
                    sg = 1.0 if d == 0 else -1.0
                    o = 1 if d == 0 else 0
                    z = zr[d]
                    act(kf[:], z[:], AF.Sigmoid, [z], [kf], scale=-1.0)
                    yield
                    ts(kf[:], kf[:], lbt[:, 32 + d * 16 + h:32 + d * 16 + h + 1], None, ALU.mult, None, [kf, lbt], [kf])
                    yield
                    act(z[:], kf[:], AF.Ln, [kf], [z], scale=-1.0, bias=1.0)
                    yield
                    T.op("dve", lambda e: e.memset(Pp[:, 0:1], 0.0), writes=[Pp])
                    yield
                    T.op("dve", lambda e, z=z: e.tensor_tensor_scan(out=Pp[:, 1:TOK + 1], data0=onesf[:], data1=z[:], initial=0.0,
                                                                   op0=ALU.mult, op1=ALU.add), reads=[onesf, z], writes=[Pp])
                    yield
                    E3 = v3(Pp[:, o:o + TOK])
                    Mr = Pp[:, 32:TOK:64]
                    Lr = Pp[:, 64:TOK + 1:64] if d == 0 else Pp[:, 0:TOK:64]
                    Vr = Pp[:, 0:TOK:64] if d == 0 else Pp[:, 64:TOK + 1:64]
                    tt(v3(dt_[:]), E3, bc(Mr), ALU.subtract, [Pp], [dt_])
                    yield
                    act(et_[:], dt_[:], AF.Exp, [dt_], [et_], scale=sg)
                    yield
                    tt(qin[d][:], qf[:], et_[:], ALU.mult, [qf, et_], [qin[d]])
                    yield
                    act(et_[:], dt_[:], AF.Exp, [dt_], [et_], scale=-sg)
                    yield
                    tt(kin[d][:], kf[:], et_[:], ALU.mult, [kf, et_], [kin[d]])
                    yield
                    tt(v3(dt_[:]), E3, bc(Lr), ALU.subtract, [Pp], [dt_])
                    yield
                    act(et_[:], dt_[:], AF.Exp, [dt_], [et_], scale=-sg)
                    yield
                    tt(kdec[:], kf[:], et_[:], ALU.mult, [kf, et_], [kdec])
                    yield
                    tt(v3(dt_[:]), E3, bc(Vr), ALU.subtract, [Pp], [dt_])
                    yield
                    act(et_[:], dt_[:], AF.Exp, [dt_], [et_], scale=sg)
                    yield
                    tt(qdec[d][:], qf[:], et_[:], ALU.mult, [qf, et_], [qdec[d]])
                    yield
                    if d == 0:
                        act(et_[:], Pp[:, 1:TOK + 1], AF.Exp, [Pp], [et_])
                        yield
                    else:
                        act(et_[:], Pp[:, 0:TOK], AF.Exp, [Pp], [et_], scale=-1.0, bias=Pp[:, TOK:TOK + 1])
                        yield
                    tt(qgl[:], qf[:], et_[:], ALU.mult, [qf, et_], [qgl])
                    yield
                    T.dma("sp", (QGF if d == 0 else QGB)[hs, :], qgl[:], reads=[qgl], writes=[QGF if d == 0 else QGB], sembuf=qgl)
                    yield
                    tt(dec[d][:], Pp[:, 64:TOK + 1:64], Pp[:, 0:TOK:64], ALU.subtract, [Pp], [dec[d]])
                    yield
                    act(dec[d][:], dec[d][:], AF.Exp, [dec[d]], [dec[d]])
                    yield
                    act(dct[:, h * 2 + d:h * 2 + d + 1], Pp[:, TOK:TOK + 1], AF.Exp, [Pp], [dct])
                    yield
                    for j4 in range(4):
                        pt = PTB[j4 % 2]
                        for j in range(4):
                            jj = j4 * 4 + j
                            T.op("pe", lambda e, j=j, jj=jj, pt=pt: e.transpose(pt[:, j, :], kdec[:, jj * 128:(jj + 1) * 128], identb),
                                 reads=[kdec, cb16], writes=[pt], signal=(j == 3))
                            yield
                        cp(kdt[d][:, j4 * 4:j4 * 4 + 4, :], pt[:], [pt], [kdt[d]])
                        yield
                yield

        def h1_loop(h):
                hs = slice(h * 128, (h + 1) * 128)
                hp = h % 2
                oacc, vtok, qin, kin, kdt, qdec, dec = oacc2[hp], vtok2[hp], qin2[hp], kin2[hp], kdt2[hp], qdec2[hp], dec2[hp]
                for d in range(2):
                    T.op("pool", lambda e, d=d: e.memset(Sf[d][:], 0.0), writes=[Sf[d]])
                    T.op("pool", lambda e, d=d: e.memset(Sb_[d][:], 0.0), writes=[Sb_[d]])
                for step in range(16):
                    for d in range(2):
                        blk = step if d == 0 else 15 - step
                        bs = slice(blk * 128, (blk + 1) * 128)
                        mk = cst[:, C_MF:C_MF + 128] if d == 0 else cst[:, C_MB:C_MB + 128]
                        AT = PSF[d]
                        oI = PSF[2 + d]
                        oA = PSF[4 + d]
                        Uv = AT[:, 256:384]
                        mm(AT[:, 0:128], kin[d][:, bs], qin[d][:, bs], True, True, [kin[d], qin[d]], [AT], True)
                        tt(amt[d][:], AT[:, 0:128], mk, ALU.mult, [AT, cst], [amt[d]])
                        for ci in ((0, 1) if d == 0 else (1, 0)):
                            c = blk * 2 + ci
                            cs_ = slice(blk * 128 + ci * 64, blk * 128 + ci * 64 + 64)
                            ps_ = slice(ci * 64, ci * 64 + 64)
                            mm(oI[:, ci * 64:ci * 64 + 64], Sb_[d][:], qdec[d][:, cs_], True, True, [Sb_[d], qdec[d]], [oI], True)
                            mm(Uv, kdt[d][ps_, blk, :], vtok[ps_, blk, :], True, True, [kdt[d], vtok], [AT], True)
                            stt(Sf[d][:], Sf[d][:], dec[d][:, c:c + 1], Uv, ALU.mult, ALU.add, [Sf[d], dec[d], AT], [Sf[d]])
                            act(Sb_[d][:], Sf[d][:], AF.Copy, [Sf[d]], [Sb_[d]])
                        mm(oA[:, 0:128], vtok[:, blk, :], amt[d][:], True, True, [vtok, amt[d]], [oA], True)
                        tt(oacc[:, bs], oacc[:, bs], oA[:, 0:128], ALU.add, [oacc, oA], [oacc])
                        tt(oacc[:, bs], oacc[:, bs], oI[:, 0:128], ALU.add, [oacc, oI], [oacc])
                        yield
                for d in range(2):
                    hd = h * 2 + d
                    T.dma("sp", SCI[hd // 8][(hd % 8) * 128:(hd % 8 + 1) * 128, :], Sf[d][:], reads=[Sf[d]], writes=[SCI[hd // 8]],
                          sembuf=Sf[d])
                T.dma("sp", OLOC[hs, :], oacc[:], reads=[oacc], writes=[OLOC], sembuf=oacc)

                yield

        def run_some(gen, n):
            if gen is None:
                return None
            for _ in range(n):
                try:
                    next(gen)
                except StopIteration:
                    return None
            return gen

        g = h1_prep(0)
        while g is not None:
            g = run_some(g, 1000)
        for h in range(KH):
            lp = h1_loop(h)
            pp = h1_prep(h + 1) if h + 1 < KH else None
            while lp is not None or pp is not None:
                lp = run_some(lp, 1)
                pp = run_some(pp, 3)
        T.dma("sp", DCI[:, :], dct[:], reads=[dct], writes=[DCI], sembuf=dct)
        for i in range(4):
            allgather(SCI[i], SCO[i])
        allgather(DCI, DCO)
        T.barrier(A.reset())
        if KSTOP <= 2:
            T.barrier()
            raise _Stop()

        if KSTOP > 3:
            for i in range(16):
                T.dma("pool", WAT[i][:, :].rearrange("p (k c) -> p k c", c=256), rows(w_pa)[:, :, i * 256:(i + 1) * 256],
                      writes=[WAT[i]], sembuf=convsem)
                T.dma("pool", WBT[i][:, :].rearrange("p (k c) -> p k c", c=256), rows(w_pb)[:, :, i * 256:(i + 1) * 256],
                      writes=[WBT[i]], sembuf=convsem)
                for j in range(2):
                    T.dma("pool", WGT[i][j][:, :].rearrange("p (k c) -> p k c", c=256),
                          rows(w_gate)[:, :, j * D + i * 256:j * D + (i + 1) * 256], writes=[WGT[i][j]], sembuf=convsem)
            for i in range(16):
                T.dma("pool", WOT[i][:, :].rearrange("p (k c) -> p k c", c=256), rows(w_o)[:, :, i * 256:(i + 1) * 256],
                      writes=[WOT[i]], sembuf=convsem)
            for i in range(16):
                for q, (k0, kq) in enumerate(QK4):
                    T.dma("pool", WDT[i][q][:, 0:kq * 256].rearrange("p (k c) -> p k c", c=256),
                          rows(w_down)[:, k0:k0 + kq, i * 256:(i + 1) * 256], writes=[WDT[i][q]], sembuf=convsem)
            for i in range(16):
                T.dma("pool", WPT[i][:, :].rearrange("p (k c) -> p k c", c=256), rows(w_pg)[:, :, i * 256:(i + 1) * 256],
                      writes=[WPT[i]], sembuf=convsem)
        KT2 = [A.sb("KT%d" % i, [128, 4 * TOK], BF16) for i in range(2)]
        VG2 = [A.sb("VG%d" % i, [128, 64, 128], BF16) for i in range(2)]

        def load_kv(g):
            KT, VG = KT2[g % 2], VG2[g % 2]
            for r in range(4):
                T.dma("sp", KT[:, r * TOK:(r + 1) * TOK], KCO[g][r * 128:(r + 1) * 128, :],
                      reads=[KCO[g]], writes=[KT], sembuf=KT)
                T.dma("sp", VG[:, r * 16:(r + 1) * 16, :], rows(VCO[g][:, :])[:, r * 16:(r + 1) * 16, :],
                      reads=[VCO[g]], writes=[VG], sembuf=VG)
        qT = [A.sb("qT%d" % i, [128, NT], BF16) for i in range(2)]
        pTs = [A.sb("pT%d" % i, [128, NT], BF16) for i in range(6)]
        rl = [A.sb("rl%d" % i, [128, NT], F32) for i in range(2)]
        yo = [A.sb("yo%d" % i, [128, NT], BF16) for i in range(2)]
        accD = [A.sb("accD%d" % i, [128, NT], F32) for i in range(2)]
        accP = [A.sb("accP%d" % i, [128, NT], F32) for i in range(2)]
        lhi = [A.sb("lhi%d" % i, [128, NT], BF16) for i in range(2)]
        llo = [A.sb("llo%d" % i, [128, NT], BF16) for i in range(2)]
        LA = 2
        u = 0
        load_kv(0)
        for g in range(KG):
            KT, VG = KT2[g % 2], VG2[g % 2]
            if g + 1 < KG:
                load_kv(g + 1)
            for qt in range(KQT):
                for hh in range(4):
                    h = g * 4 + hh
                    q_ = qT[u % 2]
                    T.dma("sp", q_[:], QS[h * 128:(h + 1) * 128, qt * NT:(qt + 1) * NT], reads=[QS], writes=[q_], sembuf=q_)
                    oT = PSF[4 + u % 2]
                    lT = PSF[3]
                    aD, aP = accD[u % 2], accP[u % 2]
                    hi_, lo_ = lhi[u % 2], llo[u % 2]
                    LAST_DVE = 55

                    def qk(kb, q_=q_):
                        sT_ = PSF[kb % 3]
                        mm(sT_[:], KT[:, kb * 128:(kb + 1) * 128], q_[:], True, True, [KT, q_], [sT_], True)

                    for kb in range(LA):
                        qk(kb)
                    for kb in range(64):
                        if kb + LA < 64:
                            qk(kb + LA)
                        sT = PSF[kb % 3]
                        p_ = pTs[kb % 6]
                        act(p_[:], sT[:], AF.Exp, [sT], [p_], scale=SCALE, bias=ESHIFT)
                        mm(oT[:], VG[:, kb, :], p_[:], kb == 0, kb == 63, [VG, p_], [oT], True)
                        if kb % 2 == 0 or kb > LAST_DVE:
                            mm(lT[:], onesb, p_[:], kb == 0, False, [cb16, p_], [lT], True)
                        elif kb == 1:
                            cp(aD[:], p_[:], [p_], [aD])
                        else:
                            tt(aD[:], aD[:], p_[:], ALU.add, [aD, p_], [aD])
                        if kb == LAST_DVE:
                            act(hi_[:], aD[:], AF.Copy, [aD], [hi_])
                            tt(lo_[:], aD[:], hi_[:], ALU.subtract, [aD, hi_], [lo_])
                    mm(lT[:], onesb, hi_[:], False, False, [cb16, hi_], [lT], False)
                    mm(lT[:], onesb, lo_[:], False, True, [cb16, hi_, lo_], [lT], True)
                    r_ = rl[u % 2]
                    y_ = yo[u % 2]
                    T.op("dve", lambda e, r_=r_, lT=lT: e.reciprocal(out=r_[:], in_=lT[:]), reads=[lT], writes=[r_])
                    tt(y_[:], oT[:], r_[:], ALU.mult, [oT, r_], [y_])
                    T.dma("sp", YA[h * 128:(h + 1) * 128, qt * NT:(qt + 1) * NT], y_[:], reads=[y_], writes=[YA], sembuf=y_)
                    u += 1
        T.barrier(A.reset())
        if KSTOP <= 3:
            T.barrier()
            raise _Stop()

        Dall = A.sb("Dall", [128, 4, 32], F32)
        Dm = [A.sb("Dm%d" % i, [128, 4, 32], F32) for i in range(2)]
        Ur = [A.sb("Ur%d" % i, [128, 4, 128], F32) for i in range(2)]
        Sacc = A.sb("Sacc", [128, 128], F32)
        Sinb = A.sb("Sinb", [128, 32, 128], BF16)
        oloc2 = [A.sb("oloc%d" % i, [128, TOK], F32) for i in range(2)]
        ghs2 = [A.sb("ghs%d" % i, [128, TOK], F32) for i in range(2)]
        qg2 = [[A.sb("qg%d" % i, [128, TOK], BF16) for i in range(2)] for _ in range(2)]

        def h2_load(h):
            hs = slice(h * 128, (h + 1) * 128)
            oloc, ghs, qg = oloc2[h % 2], ghs2[h % 2], qg2[h % 2]
            T.dma("sp", oloc[:], OLOC[hs, :], reads=[OLOC], writes=[oloc], sembuf=oloc)
            T.dma("sp", ghs[:], HG[hs, :], reads=[HG], writes=[ghs], sembuf=ghs)
            T.dma("sp", qg[0][:], QGF[hs, :], reads=[QGF], writes=[qg[0]], sembuf=qg[0])
            T.dma("sp", qg[1][:], QGB[hs, :], reads=[QGB], writes=[qg[1]], sembuf=qg[1])
        TMP = [A.sb("tmp%d" % i, [128, NT], F32) for i in range(6)]
        TB = [A.sb("tb%d" % i, [128, NT], BF16) for i in range(4)]
        T.dma("sp", Dall[:], DCO[:, :].rearrange("(r p) c -> p r c", p=128), reads=[DCO], writes=[Dall], sembuf=Dall)
        for d in range(2):
            mcol = V_MF if d == 0 else V_MB
            ocol = V_OMF if d == 0 else V_OMB
            for r in range(4):
                ts(Dm[d][:, r, :], Dall[:, r, :], vec[:, mcol + r:mcol + r + 1], vec[:, ocol + r:ocol + r + 1],
                   ALU.mult, ALU.add, [Dall, vec], [Dm[d]])
        SCO4 = [SCO[i][:, :].rearrange("(r x p) e -> p r x e", r=4, p=128) for i in range(4)]
        for hd in range(2 * KH):
            d = hd % 2
            mcol = V_MF if d == 0 else V_MB
            ur = Ur[hd % 2]
            T.dma("sp", ur[:], SCO4[hd // 8][:, :, hd % 8, :], reads=[SCO[hd // 8]], writes=[ur], sembuf=ur)
            T.op("pool", lambda e: e.memset(Sacc[:], 0.0), writes=[Sacc])
            for r in ((0, 1, 2, 3) if d == 0 else (3, 2, 1, 0)):
                ts(Sacc[:], Sacc[:], Dm[d][:, r, hd:hd + 1], None, ALU.mult, None, [Sacc, Dm[d]], [Sacc])
                stt(Sacc[:], ur[:, r, :], vec[:, mcol + r:mcol + r + 1], Sacc[:], ALU.mult, ALU.add, [ur, vec, Sacc], [Sacc])
            cp(Sinb[:, hd, :], Sacc[:], [Sacc], [Sinb])
        for h in range(KH):
            hs = slice(h * 128, (h + 1) * 128)
            oloc, ghs, qg = oloc2[h % 2], ghs2[h % 2], qg2[h % 2]
            if h == 0:
                h2_load(0)
            if h + 1 < KH:
                h2_load(h + 1)
            act(ghs[:], ghs[:], AF.Silu, [ghs], [ghs])
            for ti in range(NTT):
                tsl = slice(ti * NT, (ti + 1) * NT)
                cps = PSF[ti % 2]
                mm(cps[:], Sinb[:, 2 * h, :], qg[0][:, tsl], True, False, [Sinb, qg[0]], [cps], False)
                mm(cps[:], Sinb[:, 2 * h + 1, :], qg[1][:, tsl], False, True, [Sinb, qg[0], qg[1]], [cps], True)
                o_ = TMP[(ti * 3) % 6]
                tt(o_[:], oloc[:, tsl], cps[:], ALU.add, [oloc, cps], [o_])
                sq_ = TB[(ti * 2) % 4]
                act(sq_[:], o_[:], AF.Square, [o_], [sq_])
                ss = PSF[2 + ti % 2]
                mm(ss[:], onesb, sq_[:], True, True, [cb16, sq_], [ss], True)
                rstd, tl = TMP[(ti * 3 + 1) % 6], TMP[(ti * 3 + 2) % 6]
                rstd_from_ss(rstd, ss[:], ss, tl, 1.0 / 128, RMS_EPS)
                stt(o_[:], o_[:], vec[:, V_HGN + h:V_HGN + h + 1], rstd[:], ALU.mult, ALU.mult, [o_, vec, rstd], [o_])
                yb = TB[(ti * 2 + 1) % 4]
                tt(yb[:], o_[:], ghs[:, tsl], ALU.mult, [o_, ghs], [yb])
                T.dma("sp", YH[hs, tsl], yb[:], reads=[yb], writes=[YH], sembuf=yb)
        T.barrier(A.reset())
        if KSTOP <= 4:
            T.barrier()
            raise _Stop()

        def stat_mm(s1, s2, item):
            cb, rb, rsq = item
            mm(s1[:], onesb, rb[:], cb == 0, cb == KCB - 1, [cb16, rb], [s1], True)
            mm(s2[:], onesb, rsq[:], cb == 0, cb == KCB - 1, [cb16, rsq], [s2], True)

        def ln_finish(s1, s2, mean, rstd, nmr, eps):
            ts(mean[:], s1[:], 1.0 / D, None, ALU.mult, None, [s1], [mean])
            tt(nmr[:], mean[:], mean[:], ALU.mult, [mean], [nmr])
            stt(rstd[:], s2[:], 1.0 / D, nmr[:], ALU.mult, ALU.subtract, [s2, nmr], [rstd])
            ts(rstd[:], rstd[:], eps, None, ALU.add, None, [rstd], [rstd])
            act(rstd[:], rstd[:], AF.Ln, [rstd], [rstd])
            act(rstd[:], rstd[:], AF.Exp, [rstd], [rstd], scale=-0.5)
            stt(nmr[:], mean[:], -1.0, rstd[:], ALU.mult, ALU.mult, [mean, rstd], [nmr])

        for ti in range(KT4):
            t0 = ti * NT
            ya = A.sb("ya", [128, 16, NT], BF16)
            yh = A.sb("yh", [128, 16, NT], BF16)
            xb = A.sb("xb", [128, 32, NT], BF16)
            mT = A.sb("mT", [128, 32, NT], BF16)
            off_keep = A.off
            slabs = [A.sb("ws%d" % i, [128, 96, 256], BF16) for i in range(2)]
            TMP = [A.sb("tmp%d" % i, [128, NT], F32) for i in range(2)]
            T.dma("sp", ya[:], rows(YA[:, :])[:, :, t0:t0 + NT], reads=[YA], writes=[ya], sembuf=ya)
            T.dma("sp", yh[:], rows(YH[:, :])[:, :, t0:t0 + NT], reads=[YH], writes=[yh], sembuf=yh)
            xsrc = rows(xT)[:, :, t0:t0 + NT]
            T.dma("pool", xb[:, 0:16, :], xsrc[:, 0:16, :], writes=[xb], sembuf=xb)
            T.dma("pool", xb[:, 16:32, :], xsrc[:, 16:32, :], writes=[xb], sembuf=xb)
            for i in range((43 * ti) // KT4, (43 * (ti + 1)) // KT4):
                for j in range(2):
                    T.dma("pool", WUT[i][j][:, :].rearrange("p (k c) -> p k c", c=256),
                          rows(w_up)[:, :, j * DFF + i * 256:j * DFF + (i + 1) * 256], writes=[WUT[i][j]], sembuf=convsem2)
            for cb in range(KCB):
                if cb % 2 == 0:
                    slab = slabs[(cb // 2) % 2]
                    i2 = cb // 2
                    T.dma("sp", slab[:, 0:16, :].rearrange("p k c -> p (k c)"), WAT[i2][:, :], reads=[WAT[i2]], writes=[slab], sembuf=slab)
                    T.dma("sp", slab[:, 16:32, :].rearrange("p k c -> p (k c)"), WBT[i2][:, :], reads=[WBT[i2]], writes=[slab], sembuf=slab)
                    T.dma("sp", slab[:, 32:64, :].rearrange("p k c -> p (k c)"), WGT[i2][0][:, :], reads=[WGT[i2][0]], writes=[slab],
                          sembuf=slab)
                    T.dma("sp", slab[:, 64:96, :].rearrange("p k c -> p (k c)"), WGT[i2][1][:, :], reads=[WGT[i2][1]], writes=[slab],
                          sembuf=slab)
                jo = (cb % 2) * 128
                js = slice(jo, jo + 128)
                pa, pb, ga, gh_ = PSF[0], PSF[1], PSF[2], PSF[3]
                for k in range(16):
                    mm(pa[:], slab[:, k, js], ya[:, k, :], k == 0, k == 15, [slab, ya], [pa], k == 15)
                for k in range(16):
                    mm(pb[:], slab[:, 16 + k, js], yh[:, k, :], k == 0, k == 15, [slab, yh], [pb], k == 15)
                for k in range(32):
                    mm(ga[:], slab[:, 32 + k, js], xb[:, k, :], k == 0, k == 31, [slab, xb], [ga], k == 31)
                for k in range(32):
                    mm(gh_[:], slab[:, 64 + k, js], xb[:, k, :], k == 0, k == 31, [slab, xb], [gh_], k == 31)
                sa, sh = TMP[0], TMP[1]
                act(sa[:], ga[:], AF.Sigmoid, [ga, vec], [sa], bias=vec[:, V_BG + cb:V_BG + cb + 1])
                act(sh[:], gh_[:], AF.Sigmoid, [gh_, vec], [sh], bias=vec[:, V_BG + 32 + cb:V_BG + 32 + cb + 1])
                tt(sa[:], sa[:], pa[:], ALU.mult, [sa, pa], [sa])
                tt(sh[:], sh[:], pb[:], ALU.mult, [sh, pb], [sh])
                tt(mT[:, cb, :], sa[:], sh[:], ALU.add, [sa, sh], [mT])
            T.barrier()
            A.off = A.base
            rT = A.sb("rT", [128, 32, NT], F32)
            assert A.off <= off_keep - 32 * NT * 2
            mT2 = mT
            A.off = off_keep
            slabs = [A.sb("wo%d" % i, [128, 32, 256], BF16) for i in range(3)]
            TMP = [A.sb("tq%d" % i, [128, NT], F32) for i in range(6)]
            TB = [A.sb("tbq%d" % i, [128, NT], BF16) for i in range(4)]
            s1, s2 = PSF[4], PSF[5]
            pend = []
            for cb in range(KCB):
                if cb % 2 == 0:
                    slab = slabs[(cb // 2) % 3]
                    T.dma("sp", slab[:].rearrange("p k c -> p (k c)"), WOT[cb // 2][:, :], reads=[WOT[cb // 2]], writes=[slab], sembuf=slab)
                js = slice((cb % 2) * 128, (cb % 2) * 128 + 128)
                acc = PSF[cb % 2]
                xf = TMP[cb % 2]
                T.dma("sp", xf[:], xT[cb * 128:(cb + 1) * 128, t0:t0 + NT], writes=[xf], sembuf=xf)
                for k in range(32):
                    mm(acc[:], slab[:, k, js], mT2[:, k, :], k == 0, k == 31, [slab, mT2], [acc], k == 31)
                while len(pend) > 1:
                    stat_mm(s1, s2, pend.pop(0))
                stt(rT[:, cb, :], xf[:], ALPHA, acc[:], ALU.mult, ALU.add, [xf, acc], [rT])
                rb, rsq = TB[(cb * 2) % 4], TB[(cb * 2 + 1) % 4]
                act(rb[:], rT[:, cb, :], AF.Copy, [rT], [rb])
                act(rsq[:], rT[:, cb, :], AF.Square, [rT], [rsq])
                pend.append((cb, rb, rsq))
            while pend:
                stat_mm(s1, s2, pend.pop(0))
            mean, rstd, nmr = TMP[2], TMP[3], TMP[4]
            ln_finish(s1, s2, mean, rstd, nmr, LN_EPS)
            hbufs = [TMP[0], TMP[1], TMP[5]]
            for cb in range(KCB):
                hb = hbufs[cb % 3]
                tt(hb[:], rT[:, cb, :], rstd[:], ALU.mult, [rT, rstd], [hb])
                tt(hb[:], hb[:], nmr[:], ALU.add, [hb, nmr], [hb])
                ts(hb[:], hb[:], vec[:, V_L1G + cb:V_L1G + cb + 1], vec[:, V_L1B + cb:V_L1B + cb + 1], ALU.mult, ALU.add,
                   [hb, vec], [hb])
                T.dma("sp", H1[cb * 128:(cb + 1) * 128, 1 + t0:1 + t0 + NT], hb[:], reads=[hb], writes=[H1], sembuf=hb)
                if ti == 0:
                    T.dma("sp", EDI[cb * 128:(cb + 1) * 128, 0:1], hb[:, 0:1], reads=[hb], writes=[EDI], sembuf=hb)
                if ti == NTT - 1:
                    T.dma("sp", EDI[cb * 128:(cb + 1) * 128, 1:2], hb[:, NT - 1:NT], reads=[hb], writes=[EDI], sembuf=hb)
            T.barrier(A.reset())

        allgather(EDI, EDO)
        Eg = A.sb("Eg", [128, 4, 32, 2], F32)
        hal = A.sb("hal", [128, 32, 2], F32)
        T.dma("sp", Eg[:], EDO[:, :].rearrange("(r k p) c -> p r k c", r=4, p=128), reads=[EDO], writes=[Eg], sembuf=Eg)
        T.op("pool", lambda e: e.memset(hal[:], 0.0), writes=[hal])
        for r in range(4):
            stt(hal[:, :, 0], Eg[:, r, :, 1], vec[:, V_ML + r:V_ML + r + 1], hal[:, :, 0], ALU.mult, ALU.add, [Eg, vec, hal], [hal])
            stt(hal[:, :, 1], Eg[:, r, :, 0], vec[:, V_MR + r:V_MR + r + 1], hal[:, :, 1], ALU.mult, ALU.add, [Eg, vec, hal], [hal])
        H1r = rows(H1[:, :])
        T.dma("sp", H1r[:, :, 0:1], hal[:, :, 0:1], reads=[hal], writes=[H1], sembuf=hal)
        T.dma("sp", H1r[:, :, TOK + 1:TOK + 2], hal[:, :, 1:2], reads=[hal], writes=[H1], sembuf=hal)
        T.barrier(A.reset())
        if KSTOP <= 5:
            T.barrier()
            raise _Stop()

        for ti in range(KT5):
            t0 = ti * NT
            aT = A.sb("aT", [128, NFB, NT], BF16)
            off_keep = A.off
            h1b = A.sb("h1b", [128, 32, NT + 2], BF16)
            slabs = [A.sb("wu%d" % i, [128, 64, 256], BF16) for i in range(2)]
            gsb = [A.sb("gsb%d" % i, [128, NT + 2], F32) for i in range(2)]
            TMP = [A.sb("tmp%d" % i, [128, NT], F32) for i in range(4)]
            hsrc = H1r[:, :, t0:t0 + NT + 2]
            T.dma("pool", h1b[:, 0:16, :], hsrc[:, 0:16, :], reads=[H1], writes=[h1b], sembuf=h1b)
            T.dma("pool", h1b[:, 16:32, :], hsrc[:, 16:32, :], reads=[H1], writes=[h1b], sembuf=h1b)
            for cb in range(KFB):
                if cb % 2 == 0:
                    slab = slabs[(cb // 2) % 2]
                    i2 = cb // 2
                    T.dma("sp", slab[:, 0:32, :].rearrange("p k c -> p (k c)"), WUT[i2][0][:, :], reads=[WUT[i2][0]], writes=[slab],
                          sembuf=slab)
                    T.dma("sp", slab[:, 32:64, :].rearrange("p k c -> p (k c)"), WUT[i2][1][:, :], reads=[WUT[i2][1]], writes=[slab],
                          sembuf=slab)
                js = slice((cb % 2) * 128, (cb % 2) * 128 + 128)
                up, gp, gh_ = PSF[cb % 2], PSF[2 + cb % 2], PSF[4 + cb % 2]
                for k in range(32):
                    mm(up[:], slab[:, k, js], h1b[:, k, 1:NT + 1], k == 0, k == 31, [slab, h1b], [up], k == 31)
                for k in range(32):
                    mm(gp[:], slab[:, 32 + k, js], h1b[:, k, 1:NT + 1], k == 0, k == 31, [slab, h1b], [gp], k == 31)
                for k in range(32):
                    mm(gh_[:, 0:2], slab[:, 32 + k, js], h1b[:, k, 0:NT + 2:NT + 1], k == 0, k == 31, [slab, h1b], [gh_], k == 31)
                gs = gsb[cb % 2]
                act(gs[:, 1:NT + 1], gp[:], AF.Copy, [gp], [gs])
                cp(gs[:, 0:NT + 2:NT + 1], gh_[:, 0:2], [gh_], [gs])
                c_ = TMP[cb % 2]
                ts(c_[:], gs[:, 0:NT], vec[:, V_CW + cb:V_CW + cb + 1], vec[:, V_CB + cb:V_CB + cb + 1], ALU.mult, ALU.add,
                   [gs, vec], [c_])
                stt(c_[:], gs[:, 1:NT + 1], vec[:, V_CW + NFB + cb:V_CW + NFB + cb + 1], c_[:], ALU.mult, ALU.add, [gs, vec, c_], [c_])
                stt(c_[:], gs[:, 2:NT + 2], vec[:, V_CW + 2 * NFB + cb:V_CW + 2 * NFB + cb + 1], c_[:], ALU.mult, ALU.add,
                    [gs, vec, c_], [c_])
                act(c_[:], c_[:], AF.Silu, [c_], [c_])
                tt(aT[:, cb, :], c_[:], up[:], ALU.mult, [c_, up], [aT])
            T.barrier()
            A.off = off_keep
            rT = A.sb("rT", [128, 32, NT], F32)
            TMP = [A.sb("tq%d" % i, [128, NT], F32) for i in range(5)]
            TB = [A.sb("tbq%d" % i, [128, NT], BF16) for i in range(4)]
            off_slabs = A.off
            slabs = [A.sb("wd%d" % i, [128, 22, 256], BF16) for i in range(3)]
            s1, s2 = PSF[4], PSF[5]
            pend = []
            for cb2 in range(KCB // 2):
                accs = [PSF[(cb2 % 2) * 2], PSF[(cb2 % 2) * 2 + 1]]
                for q, (k0, kq) in enumerate(QK4):
                    slab = slabs[(cb2 * 4 + q) % 3]
                    T.dma("sp", slab[:, 0:kq, :].rearrange("p k c -> p (k c)"), WDT[cb2][q][:, 0:kq * 256], reads=[WDT[cb2][q]],
                          writes=[slab], sembuf=slab)
                    for j in range(2):
                        for k in range(kq):
                            mm(accs[j][:], slab[:, k, j * 128:(j + 1) * 128], aT[:, k0 + k, :], q == 0 and k == 0,
                               q == 3 and k == kq - 1, [slab, aT], [accs[j]], k == kq - 1)
                while pend:
                    stat_mm(s1, s2, pend.pop(0))
                for j in range(2):
                    cb = cb2 * 2 + j
                    acc = accs[j]
                    hf = TMP[cb % 2]
                    T.dma("sp", hf[:], H1[cb * 128:(cb + 1) * 128, 1 + t0:1 + t0 + NT], reads=[H1], writes=[hf], sembuf=hf)
                    stt(rT[:, cb, :], hf[:], ALPHA, acc[:], ALU.mult, ALU.add, [hf, acc], [rT])
                    rb, rsq = TB[(cb * 2) % 4], TB[(cb * 2 + 1) % 4]
                    act(rb[:], rT[:, cb, :], AF.Copy, [rT], [rb])
                    act(rsq[:], rT[:, cb, :], AF.Square, [rT], [rsq])
                    pend.append((cb, rb, rsq))
            while pend:
                stat_mm(s1, s2, pend.pop(0))
            mean, rstd, nmr = TMP[2], TMP[3], TMP[4]
            ln_finish(s1, s2, mean, rstd, nmr, LN_EPS)
            T.barrier()
            A.off = A.base
            x2b = A.sb("x2b", [128, 32, NT], BF16)
            pTb = A.sb("pTb", [128, 2, NT], BF16)
            wple = A.sb("wple", [128, 2, D], BF16)
            assert A.off <= off_keep
            A.off = off_slabs
            slabs = [A.sb("wg%d" % i, [128, 32, 256], BF16) for i in range(2)]
            T.dma("pool", pTb[:], rows(pT)[:, :, t0:t0 + NT], writes=[pTb], sembuf=pTb)
            T.dma("pool", wple[:], rows(w_ple)[:, :, :], writes=[wple], sembuf=wple)
            for cb in range(KCB):
                r_ = rT[:, cb, :]
                tt(r_, r_, rstd[:], ALU.mult, [rT, rstd], [rT])
                tt(r_, r_, nmr[:], ALU.add, [rT, nmr], [rT])
                ts(r_, r_, vec[:, V_L2G + cb:V_L2G + cb + 1], vec[:, V_L2B + cb:V_L2B + cb + 1], ALU.mult, ALU.add, [rT, vec], [rT])
                act(x2b[:, cb, :], r_, AF.Copy, [rT], [x2b])
            for cb in range(KCB):
                if cb % 2 == 0:
                    slab = slabs[(cb // 2) % 2]
                    T.dma("sp", slab[:].rearrange("p k c -> p (k c)"), WPT[cb // 2][:, :], reads=[WPT[cb // 2]], writes=[slab], sembuf=slab)
                js = slice((cb % 2) * 128, (cb % 2) * 128 + 128)
                pg, pl = PSF[cb % 2], PSF[2 + cb % 2]
                for k in range(32):
                    mm(pg[:], slab[:, k, js], x2b[:, k, :], k == 0, k == 31, [slab, x2b], [pg], k == 31)
                for k in range(2):
                    mm(pl[:], wple[:, k, cb * 128:(cb + 1) * 128], pTb[:, k, :], k == 0, k == 1, [wple, pTb], [pl], k == 1)
                s_ = TMP[cb % 2]
                act(s_[:], pg[:], AF.Sigmoid, [pg], [s_])
                tt(s_[:], s_[:], pl[:], ALU.mult, [s_, pl], [s_])
                tt(s_[:], s_[:], rT[:, cb, :], ALU.add, [s_, rT], [s_])
                T.dma("sp", outT[cb * 128:(cb + 1) * 128, t0:t0 + NT], s_[:], reads=[s_], sembuf=s_)
            T.barrier(A.reset())
        T.barrier()
        print("kernel build: ninst=%d nwaits=%d dma_sems=%d" % (T.ninst, T.nwaits, len(T.dma_sems)), flush=True)
    return nc


def _consts():
    i = np.arange(128)
    R = np.zeros((128, 128), np.float32)
    for a in range(128):
        sec = a // 64
        loc = a % 64
        if loc < 32:
            R[a, sec * 64 + loc + 32] = -1.0
        else:
            R[a, sec * 64 + loc - 32] = 1.0
    RT = R.T.copy()
    ident = np.eye(128, dtype=np.float32)
    ones = np.ones((128, 128), np.float32)
    s = i[:, None]
    t = i[None, :]
    same = (s // 64) == (t // 64)
    maskF = (same & (s <= t)).astype(np.float32)
    maskB = (same & (s >= t)).astype(np.float32)
    return np.concatenate([RT, ident, ones, maskF, maskB], axis=1).astype(np.float32)


def _rope_tables(tok0):
    t = np.arange(tok0, tok0 + TOK)
    row = (t // 64).astype(np.float32)
    col = (t % 64).astype(np.float32)
    sec = 64
    inv = (10000.0 ** (-np.arange(0, sec, 2, dtype=np.float32) / sec)).astype(np.float32)
    ang_r = row[:, None] * inv[None, :]
    ang_c = col[:, None] * inv[None, :]
    ang = np.concatenate([ang_r, ang_r, ang_c, ang_c], axis=-1).astype(np.float32)
    return np.ascontiguousarray(np.cos(ang).T.astype(np.float32)), np.ascontiguousarray(np.sin(ang).T.astype(np.float32))


_NC_CACHE = {}


def kernel(x, p, w_in, q_norm, k_norm, lb_logits, hg_norm, w_pa, w_pb, w_gate, b_gate, w_o,
           ln1_g, ln1_b, w_up, conv_w, conv_b, w_down, ln2_g, ln2_b, w_pg, w_ple):
    f = lambda a: np.ascontiguousarray(np.asarray(a, dtype=np.float32))
    x = f(x); p = f(p)
    col = lambda v, n: f(v).reshape(n, 128).T
    vec = np.zeros((128, NV), np.float32)
    vec[:, V_BG:V_BG + 64] = col(b_gate[0], 64)
    vec[:, V_L1G:V_L1G + 32] = col(ln1_g[0], 32)
    vec[:, V_L1B:V_L1B + 32] = col(ln1_b[0], 32)
    vec[:, V_L2G:V_L2G + 32] = col(ln2_g[0], 32)
    vec[:, V_L2B:V_L2B + 32] = col(ln2_b[0], 32)
    cw = f(conv_w)[0]
    for tap in range(3):
        vec[:, V_CW + tap * NFB:V_CW + (tap + 1) * NFB] = col(cw[tap], NFB)
    vec[:, V_CB:V_CB + NFB] = col(conv_b[0], NFB)
    vec[:, V_QN] = f(q_norm)[0]
    vec[:, V_KN] = f(k_norm)[0]
    vec[:, V_HGN:V_HGN + 16] = col(hg_norm[0], 16)
    lbl = f(lb_logits)
    for d in range(2):
        for l in range(2):
            vec[:, V_LBL + (d * 2 + l) * 16:V_LBL + (d * 2 + l) * 16 + 16] = col(lbl[d, l], 16)
    cst = _consts()
    weights = {"w_in": f(w_in)[0], "w_pa": f(w_pa)[0], "w_pb": f(w_pb)[0], "w_gate": f(w_gate)[0], "w_o": f(w_o)[0],
               "w_up": f(w_up)[0], "w_down": f(w_down)[0], "w_pg": f(w_pg)[0], "w_ple": f(w_ple)[0]}
    in_maps = []
    for c in range(8):
        b, s = c // 4, c % 4
        v = vec.copy()
        for r in range(4):
            v[:, V_MF + r] = 1.0 if r < s else 0.0
            v[:, V_MB + r] = 1.0 if r > s else 0.0
            v[:, V_ML + r] = 1.0 if r == s - 1 else 0.0
            v[:, V_MR + r] = 1.0 if r == s + 1 else 0.0
            v[:, V_OMF + r] = 0.0 if r < s else 1.0
            v[:, V_OMB + r] = 0.0 if r > s else 1.0
        cosT, sinT = _rope_tables(s * TOK)
        m = {"xT": np.ascontiguousarray(x[b, s * TOK:(s + 1) * TOK, :].T),
             "pT": np.ascontiguousarray(p[0, b, s * TOK:(s + 1) * TOK, :].T),
             "vec": v, "cst": cst, "cosT": cosT, "sinT": sinT}
        m.update(weights)
        in_maps.append(m)
    if "nc" not in _NC_CACHE:
        try:
            build_nc()
        except _Stop:
            pass
    in_maps = [{k: v for k, v in m.items() if k in _NC_CACHE["names"]} for m in in_maps]
    res = run_bass_kernel_spmd(_NC_CACHE["nc"], in_maps, core_ids=list(range(8)))
    out = np.empty((2, 4 * TOK, D), np.float32)
    for c in range(8):
        b, s = c // 4, c % 4
        out[b, s * TOK:(s + 1) * TOK, :] = res.results[c]["outT"].T
    return out
```

```python
from contextlib import ExitStack
import os
import numpy as np
import concourse.bass as bass
import concourse.mybir as mybir
from concourse.bass_utils import run_bass_kernel_spmd

F32 = mybir.dt.float32
BF16 = mybir.dt.bfloat16
AF = mybir.ActivationFunctionType
ALU = mybir.AluOpType

D = 4096
TOK = 2048
NT = 512
NTT = 4
DFF = 11008
NFB = 86
ALPHA = 2.0 ** 0.25
RMS_EPS = 1e-6
LN_EPS = 1e-5
SCALE = 128 ** -0.5
ESHIFT = -4.0

V_BG = 0
V_L1G = 64
V_L1B = 96
V_L2G = 128
V_L2B = 160
V_CW = 192
V_CB = 450
V_QN = 536
V_KN = 537
V_HGN = 538
V_LBL = 554
V_MF = 618
V_MB = 622
V_ML = 626
V_MR = 630
V_OMF = 634
V_OMB = 638
NV = 642
C_RT, C_ID, C_ONE, C_MF, C_MB = 0, 128, 256, 384, 512


class _Stop(Exception):
    pass


class Buf:
    __slots__ = ("name", "t", "last_write", "reads", "sem")

    def __init__(self, name, t=None):
        self.name = name
        self.t = t
        self.last_write = None
        self.reads = []
        self.sem = None

    def __getitem__(self, idx):
        return self.t[idx]


class Trk:
    ENG = ("pe", "act", "dve", "pool", "sp")

    def __init__(self, nc, stack):
        self.nc = nc
        self.stack = stack
        self.eng = {"pe": nc.tensor, "act": nc.scalar, "dve": nc.vector, "pool": nc.gpsimd, "sp": nc.sync}
        self.sem = {}
        self.cnt = {}
        for e in ("pe", "act", "dve", "pool"):
            self.sem[e] = stack.enter_context(nc.semaphore("c_" + e))
            self.cnt[e] = 0
        self.waited = {e: {} for e in self.ENG}
        self.dma_sems = []
        self.free_sems = {}
        self.nwaits = 0
        self.ninst = 0

    def _wait(self, e, ev):
        kind, s, v = ev
        if kind == "c":
            if s == e and e == "pe":
                return
            sem = self.sem[s]
            key = "c" + s
        else:
            sem = s[0]
            key = id(s)
            v = s[1]
        w = self.waited[e]
        if w.get(key, -1) >= v:
            return
        w[key] = v
        self.eng[e].wait_ge(sem, v)
        self.nwaits += 1

    def _deps(self, e, reads, writes):
        for b in reads:
            if b.last_write is not None:
                self._wait(e, b.last_write)
        for b in writes:
            if b.last_write is not None:
                self._wait(e, b.last_write)
            for ev in b.reads:
                self._wait(e, ev)

    def _reg(self, ev, reads, writes):
        for b in reads:
            b.reads.append(ev)
        for b in writes:
            b.last_write = ev
            b.reads = []

    def op(self, e, fn, reads=(), writes=(), signal=True):
        self._deps(e, reads, writes)
        inst = fn(self.eng[e])
        self.ninst += 1
        if not signal:
            return None
        self.cnt[e] += 1
        inst.then_inc(self.sem[e], 1)
        ev = ("c", e, self.cnt[e])
        self._reg(ev, reads, writes)
        return ev

    def _dsem(self, b, q):
        if b.sem is None:
            b.sem = {}
        if q not in b.sem:
            fl = self.free_sems.setdefault(q, [])
            if fl:
                b.sem[q] = fl.pop()
            else:
                rec = [self.stack.enter_context(self.nc.semaphore("d_%d" % len(self.dma_sems))), 0]
                self.dma_sems.append(rec)
                b.sem[q] = rec
        return b.sem[q]

    def dma(self, q, out, in_, reads=(), writes=(), sembuf=None):
        self._deps(q, reads, writes)
        s = self._dsem(sembuf, q)
        inst = self.eng[q].dma_start(out=out, in_=in_)
        s[1] += 16
        inst.then_inc(s[0], 16)
        self.ninst += 1
        ev = ("d", s, s[1])
        self._reg(ev, reads, writes)
        return ev

    def custom(self, e, fn, owner, inc, reads=(), writes=()):
        self._deps(e, reads, writes)
        s = self._dsem(owner, "cc")
        inst = fn(self.eng[e])
        s[1] += inc
        inst.then_inc(s[0], inc)
        ev = ("d", s, s[1])
        self._reg(ev, reads, writes)
        return ev

    def barrier(self, release=()):
        for e in self.ENG:
            for s in ("pe", "act", "dve", "pool"):
                if self.cnt[s] > 0:
                    self._wait(e, ("c", s, self.cnt[s]))
            for s in self.dma_sems:
                if s[1] > 0:
                    self._wait(e, ("d", s, s[1]))
        for b in release:
            if b.sem is not None:
                for q, rec in b.sem.items():
                    self.free_sems.setdefault(q, []).append(rec)
                b.sem = None


class Arena:
    def __init__(self, nc, base=0):
        self.nc = nc
        self.base = base
        self.off = base
        self.n = 0
        self.bufs = []

    def reset(self):
        self.off = self.base
        b = self.bufs
        self.bufs = []
        return b

    def sb(self, name, shape, dt):
        esz = 4 if dt == F32 else 2
        per = esz
        for s in shape[1:]:
            per *= s
        self.off = (self.off + 63) // 64 * 64
        self.n += 1
        t = self.nc.alloc_sbuf_tensor_at("%s_%d" % (name, self.n), list(shape), dt, offset=self.off)
        self.off += per
        assert self.off <= 16384 + 211000, (name, self.off)
        b = Buf(name, t)
        self.bufs.append(b)
        return b


def build_nc():
    nc = bass.Bass("TRN2", target_bir_lowering=False)
    _NC_CACHE["nc"] = nc

    KSTOP = int(os.environ.get("KSTOP", "99"))
    KP1T = int(os.environ.get("KP1T", str(NTT)))
    KP1C = int(os.environ.get("KP1C", "104"))
    KCOLL = int(os.environ.get("KCOLL", "1"))
    KH = int(os.environ.get("KH", "16"))
    KG = int(os.environ.get("KG", "4"))
    KQT = int(os.environ.get("KQT", str(NTT)))
    KT4 = int(os.environ.get("KT4", str(NTT)))
    KT5 = int(os.environ.get("KT5", str(NTT)))
    KCB = int(os.environ.get("KCB", "32"))
    KFB = int(os.environ.get("KFB", str(NFB)))
    names = _NC_CACHE.setdefault("names", set())

    def din(name, shape, dt=F32):
        if KSTOP <= 3 and name in ("pT", "w_pa", "w_pb", "w_gate", "w_o", "w_up", "w_down", "w_pg", "w_ple"):
            return None
        names.add(name)
        return nc.dram_tensor(name, shape, dt, kind="ExternalInput").ap()

    xT = din("xT", [D, TOK])
    pT = din("pT", [256, TOK])
    w_in = din("w_in", [D, 13312])
    w_pa = din("w_pa", [2048, D])
    w_pb = din("w_pb", [2048, D])
    w_gate = din("w_gate", [D, 2 * D])
    w_o = din("w_o", [D, D])
    w_up = din("w_up", [D, 2 * DFF])
    w_down = din("w_down", [DFF, D])
    w_pg = din("w_pg", [D, D])
    w_ple = din("w_ple", [256, D])
    vec_d = din("vec", [128, NV])
    cst_d = din("cst", [128, 640])
    cos_d = din("cosT", [128, TOK])
    sin_d = din("sinT", [128, TOK])
    outT = nc.dram_tensor("outT", [D, TOK], F32, kind="ExternalOutput").ap()

    def scr(name, shape, dt):
        t = nc.dram_tensor(name, shape, dt)
        _NC_CACHE.setdefault("scratch", []).append(name)
        return Buf(name, t.ap())

    QS = scr("QS", [2048, TOK], BF16)
    KCI = [scr("KCI%d" % g, [128, TOK], BF16) for g in range(4)]
    KCO = [scr("KCO%d" % g, [512, TOK], BF16) for g in range(4)]
    VCI = [scr("VCI%d" % g, [TOK, 128], BF16) for g in range(4)]
    VCO = [scr("VCO%d" % g, [4 * TOK, 128], BF16) for g in range(4)]
    HQ = scr("HQ", [2048, TOK], F32)
    HZF = scr("HZF", [2048, TOK], F32)
    HZB = scr("HZB", [2048, TOK], F32)
    HG = scr("HG", [2048, TOK], F32)
    HV = scr("HV", [TOK, 2048], BF16)
    OLOC = scr("OLOC", [2048, TOK], F32)
    QGF = scr("QGF", [2048, TOK], BF16)
    QGB = scr("QGB", [2048, TOK], BF16)
    SCI = [scr("SCI%d" % i, [8 * 128, 128], F32) for i in range(4)]
    SCO = [scr("SCO%d" % i, [4 * 8 * 128, 128], F32) for i in range(4)]
    DCI = scr("DCI", [128, 32], F32)
    DCO = scr("DCO", [512, 32], F32)
    YA = scr("YA", [2048, TOK], BF16)
    YH = scr("YH", [2048, TOK], BF16)
    H1 = scr("H1", [D, TOK + 2], F32)
    EDI = scr("EDI", [D, 2], F32)
    EDO = scr("EDO", [4 * D, 2], F32)
    WOT_t = nc.dram_tensor("WOT", [16, 128, 32 * 256], BF16)
    WPT_t = nc.dram_tensor("WPT", [16, 128, 32 * 256], BF16)
    WDT_t = nc.dram_tensor("WDT", [16, 4, 128, 22 * 256], BF16)
    WAT_t = nc.dram_tensor("WAT", [16, 128, 16 * 256], BF16)
    WBT_t = nc.dram_tensor("WBT", [16, 128, 16 * 256], BF16)
    WGT_t = nc.dram_tensor("WGT", [16, 2, 128, 32 * 256], BF16)
    WAT = [Buf("WAT%d" % i, WAT_t.ap()[i]) for i in range(16)]
    WBT = [Buf("WBT%d" % i, WBT_t.ap()[i]) for i in range(16)]
    WGT = [[Buf("WGT%d_%d" % (i, j), WGT_t.ap()[i, j]) for j in range(2)] for i in range(16)]
    WOT = [Buf("WOT%d" % i, WOT_t.ap()[i]) for i in range(16)]
    WPT = [Buf("WPT%d" % i, WPT_t.ap()[i]) for i in range(16)]
    WDT = [[Buf("WDT%d_%d" % (i, q), WDT_t.ap()[i, q]) for q in range(4)] for i in range(16)]
    QK4 = [(0, 22), (22, 21), (43, 22), (65, 21)]
    convsem = Buf("convsem")
    WUT_t = nc.dram_tensor("WUT", [43, 2, 128, 32 * 256], BF16)
    WUT = [[Buf("WUT%d_%d" % (i, j), WUT_t.ap()[i, j]) for j in range(2)] for i in range(43)]
    convsem2 = Buf("convsem2")
    GROUPS = [[0, 1, 2, 3], [4, 5, 6, 7]]

    with ExitStack() as st:
        st.enter_context(nc.allow_non_contiguous_dma(reason="single-column halo/edge transfers"))
        T = Trk(nc, st)
        PSF = [Buf("psf%d" % i, st.enter_context(nc.psum_tensor("psf%d" % i, [128, 512], F32))) for i in range(6)]
        PTB = [Buf("ptb%d" % i, st.enter_context(nc.psum_tensor("ptb%d" % i, [128, 8, 128], BF16))[:, 0:4, :]) for i in range(2)]

        A0 = Arena(nc, 16384)
        vec = A0.sb("vec", [128, NV], F32)
        cst = A0.sb("cst", [128, 640], F32)
        cb16 = A0.sb("cb16", [128, 640], BF16)
        rtq = A0.sb("rtq", [128, 128], BF16)
        rtk = A0.sb("rtk", [128, 128], BF16)
        lbt = A0.sb("lbt", [128, 64], F32)
        A = Arena(nc, A0.off)

        T.dma("sp", vec[:], vec_d[:, :], writes=[vec], sembuf=vec)
        T.dma("sp", cst[:], cst_d[:, :], writes=[cst], sembuf=cst)
        T.op("dve", lambda e: e.tensor_copy(out=cb16[:], in_=cst[:]), reads=[cst], writes=[cb16])
        T.op("dve", lambda e: e.tensor_scalar(out=rtq[:], in0=cst[:, C_RT:C_RT + 128], scalar1=vec[:, V_QN:V_QN + 1],
                                              scalar2=None, op0=ALU.mult), reads=[cst, vec], writes=[rtq])
        T.op("dve", lambda e: e.tensor_scalar(out=rtk[:], in0=cst[:, C_RT:C_RT + 128], scalar1=vec[:, V_KN:V_KN + 1],
                                              scalar2=None, op0=ALU.mult), reads=[cst, vec], writes=[rtk])
        for d in range(2):
            T.op("dve", lambda e, d=d: e.tensor_tensor(out=lbt[:, d * 16:(d + 1) * 16],
                                                       in0=vec[:, V_LBL + (d * 2) * 16:V_LBL + (d * 2) * 16 + 16],
                                                       in1=vec[:, V_LBL + (d * 2 + 1) * 16:V_LBL + (d * 2 + 1) * 16 + 16],
                                                       op=ALU.subtract), reads=[vec], writes=[lbt])
        T.op("act", lambda e: e.activation(out=lbt[:, 0:32], in_=lbt[:, 0:32], func=AF.Sigmoid), reads=[lbt], writes=[lbt])
        T.op("dve", lambda e: e.tensor_scalar(out=lbt[:, 32:64], in0=lbt[:, 0:32], scalar1=-1.0, scalar2=1.0,
                                              op0=ALU.mult, op1=ALU.add), reads=[lbt], writes=[lbt])
        onesb = cb16[:, C_ONE:C_ONE + 128]
        identb = cb16[:, C_ID:C_ID + 128]

        def mm(out, lhsT, rhs, start, stop, reads, writes, signal):
            T.op("pe", lambda e: e.matmul(out, lhsT, rhs, start=start, stop=stop), reads=reads, writes=writes, signal=signal)

        def act(out, in_, func, reads, writes, **kw):
            T.op("act", lambda e: e.activation(out=out, in_=in_, func=func, **kw), reads=reads, writes=writes)

        def tt(out, in0, in1, op, reads, writes, eng="dve"):
            T.op(eng, lambda e: e.tensor_tensor(out=out, in0=in0, in1=in1, op=op), reads=reads, writes=writes)

        def ts(out, in0, s1, s2, op0, op1, reads, writes, eng="dve"):
            if s2 is None:
                T.op(eng, lambda e: e.tensor_scalar(out=out, in0=in0, scalar1=s1, scalar2=None, op0=op0), reads=reads, writes=writes)
            else:
                T.op(eng, lambda e: e.tensor_scalar(out=out, in0=in0, scalar1=s1, scalar2=s2, op0=op0, op1=op1),
                     reads=reads, writes=writes)

        def stt(out, in0, sc, in1, op0, op1, reads, writes):
            T.op("dve", lambda e: e.scalar_tensor_tensor(out=out, in0=in0, scalar=sc, in1=in1, op0=op0, op1=op1),
                 reads=reads, writes=writes)

        def cp(out, in_, reads, writes, eng="dve"):
            T.op(eng, lambda e: e.tensor_copy(out=out, in_=in_), reads=reads, writes=writes)

        def rows(ap2d, p=128):
            return ap2d.rearrange("(k p) c -> p k c", p=p)

        def load_w(slab_view, w2d, c0, ncols, k0, kc, slab):
            src = rows(w2d)[:, k0:k0 + kc, c0:c0 + ncols]
            h = (kc + 1) // 2
            T.dma("pool", slab_view[:, 0:h, :], src[:, 0:h, :], writes=[slab], sembuf=slab)
            if kc > h:
                T.dma("pool", slab_view[:, h:kc, :], src[:, h:kc, :], writes=[slab], sembuf=slab)

        def rstd_from_ss(dst, ss_ap, ssbuf, tmpbuf, scale, eps, tmp_ap=None):
            ta = tmpbuf[:] if tmp_ap is None else tmp_ap
            ts(ta, ss_ap, scale, eps, ALU.mult, ALU.add, [ssbuf], [tmpbuf])
            act(ta, ta, AF.Ln, [tmpbuf], [tmpbuf])
            act(dst[:], ta, AF.Exp, [tmpbuf], [dst], scale=-0.5)

        xbs = [A.sb("xb%d" % i, [128, 32, NT], BF16) for i in range(2)]
        slabs = [A.sb("ws%d" % i, [128, 32, 256], BF16) for i in range(3)]
        cosbs = [A.sb("cos%d" % i, [128, NT], F32) for i in range(2)]
        sinbs = [A.sb("sin%d" % i, [128, NT], F32) for i in range(2)]
        TMP = [A.sb("tmp%d" % i, [128, NT], F32) for i in range(8)]
        TB = [A.sb("tb%d" % i, [128, NT], BF16) for i in range(8)]
        VT = [A.sb("vt%d" % i, [128, 4, 128], BF16) for i in range(2)]
        ntmp = [0]

        def tmp():
            ntmp[0] += 1
            return TMP[ntmp[0] % len(TMP)]

        ntb = [0]

        def tb():
            ntb[0] += 1
            return TB[ntb[0] % len(TB)]

        it = 0
        for st_ in range(KP1T // 2):
          for half in range(2):
            t0 = (st_ * 2 + half) * NT
            xsrc = rows(xT)[:, :, t0:t0 + NT]
            T.dma("pool", xbs[half][:, 0:16, :], xsrc[:, 0:16, :], writes=[xbs[half]], sembuf=xbs[half])
            T.dma("pool", xbs[half][:, 16:32, :], xsrc[:, 16:32, :], writes=[xbs[half]], sembuf=xbs[half])
            T.dma("sp", cosbs[half][:], cos_d[:, t0:t0 + NT], writes=[cosbs[half]], sembuf=cosbs[half])
            T.dma("sp", sinbs[half][:], sin_d[:, t0:t0 + NT], writes=[sinbs[half]], sembuf=sinbs[half])
          for cbk in range(KP1C):
            if cbk % 2 == 0:
                slab = slabs[(cbk // 2) % 3]
                load_w(slab[:], w_in, cbk * 128, 256, 0, 32, slab)
            for half in range(2):
                ti = st_ * 2 + half
                t0 = ti * NT
                xb, cosb, sinb = xbs[half], cosbs[half], sinbs[half]
                jo = (cbk % 2) * 128
                acc = PSF[it % 3]
                it += 1
                for k in range(32):
                    mm(acc[:], slab[:, k, jo:jo + 128], xb[:, k, :], k == 0, k == 31, [slab, xb], [acc], k == 31)
                if cbk < 20:
                    isq = cbk < 16
                    gcol = V_QN if isq else V_KN
                    rt = rtq if isq else rtk
                    qb_, sq_ = tb(), tb()
                    act(qb_[:], acc[:], AF.Copy, [acc], [qb_])
                    act(sq_[:], acc[:], AF.Square, [acc], [sq_])
                    ss, rq = PSF[3], PSF[4]
                    mm(ss[:], onesb, sq_[:], True, True, [cb16, sq_], [ss], True)
                    mm(rq[:], rt[:], qb_[:], True, True, [rt, qb_], [rq], True)
                    rstd, tl = tmp(), tmp()
                    rstd_from_ss(rstd, ss[:], ss, tl, 1.0 / 128, RMS_EPS)
                    a_, b_ = tmp(), tmp()
                    stt(a_[:], acc[:], vec[:, gcol:gcol + 1], cosb[:], ALU.mult, ALU.mult, [acc, vec, cosb], [a_])
                    tt(b_[:], rq[:], sinb[:], ALU.mult, [rq, sinb], [b_])
                    tt(a_[:], a_[:], b_[:], ALU.add, [a_, b_], [a_])
                    ob = tb()
                    tt(ob[:], a_[:], rstd[:], ALU.mult, [a_, rstd], [ob])
                    if isq:
                        T.dma("sp", QS[cbk * 128:(cbk + 1) * 128, t0:t0 + NT], ob[:], reads=[ob], writes=[QS], sembuf=ob)
                    else:
                        g = cbk - 16
                        T.dma("sp", KCI[g][:, t0:t0 + NT], ob[:], reads=[ob], writes=[KCI[g]], sembuf=ob)
                elif cbk < 24 or 72 <= cbk < 88:
                    vb_ = tb()
                    act(vb_[:], acc[:], AF.Copy, [acc], [vb_])
                    pt = PTB[it % 2]
                    for j in range(4):
                        T.op("pe", lambda e, j=j, pt=pt, vb_=vb_: e.transpose(pt[:, j, :], vb_[:, j * 128:(j + 1) * 128], identb),
                             reads=[vb_, cb16], writes=[pt], signal=(j == 3))
                    vt = VT[it % 2]
                    cp(vt[:], pt[:], [pt], [vt])
                    if cbk < 24:
                        g = cbk - 20
                        dst = rows(VCI[g][:, :])[:, ti * 4:ti * 4 + 4, :]
                        T.dma("sp", dst, vt[:], reads=[vt], writes=[VCI[g]], sembuf=vt)
                    else:
                        h = cbk - 72
                        dst = rows(HV[:, :])[:, ti * 4:ti * 4 + 4, h * 128:(h + 1) * 128]
                        T.dma("sp", dst, vt[:], reads=[vt], writes=[HV], sembuf=vt)
                else:
                    if cbk < 40:
                        dstb, h = HQ, cbk - 24
                    elif cbk < 56:
                        dstb, h = HZF, cbk - 40
                    elif cbk < 72:
                        dstb, h = HZB, cbk - 56
                    else:
                        dstb, h = HG, cbk - 88
                    o_ = tmp()
                    if cbk % 2 == 0:
                        act(o_[:], acc[:], AF.Copy, [acc], [o_])
                    else:
                        cp(o_[:], acc[:], [acc], [o_])
                    T.dma("sp", dstb[h * 128:(h + 1) * 128, t0:t0 + NT], o_[:], reads=[o_], writes=[dstb], sembuf=o_)

        def allgather(src, dst):
            T.custom("pool", lambda e: e.collective_compute("AllGather", ALU.bypass, replica_groups=GROUPS,
                                                            ins=[src.t.opt()], outs=[dst.t.opt()]),
                     dst, 1, reads=[src], writes=[dst])

        if KCOLL:
            for g in range(4):
                allgather(KCI[g], KCO[g])
                allgather(VCI[g], VCO[g])
        T.barrier(A.reset())
        if KSTOP <= 1:
            T.barrier()
            raise _Stop()

        qf = A.sb("qf", [128, TOK], F32)
        zr = [A.sb("zr%d" % i, [128, TOK], F32) for i in range(2)]
        kf = A.sb("kf", [128, TOK], F32)
        Pp = A.sb("Pp", [128, TOK + 64], F32)
        onesf = A.sb("onesf", [128, TOK], F32)
        dt_ = A.sb("dt", [128, TOK], F32)
        et_ = A.sb("et", [128, TOK], F32)
        oacc2 = [A.sb("oacc%d" % i, [128, TOK], F32) for i in range(2)]
        vtok2 = [A.sb("vtok%d" % i, [128, 16, 128], BF16) for i in range(2)]
        qin2 = [[A.sb("qin%d" % i, [128, TOK], BF16) for i in range(2)] for _ in range(2)]
        kin2 = [[A.sb("kin%d" % i, [128, TOK], BF16) for i in range(2)] for _ in range(2)]
        kdec = A.sb("kdec", [128, TOK], BF16)
        kdt2 = [[A.sb("kdt%d" % i, [128, 16, 128], BF16) for i in range(2)] for _ in range(2)]
        qdec2 = [[A.sb("qdec%d" % i, [128, TOK], BF16) for i in range(2)] for _ in range(2)]
        qgl = A.sb("qgl", [128, TOK], BF16)
        dec2 = [[A.sb("dec%d" % i, [128, 32], F32) for i in range(2)] for _ in range(2)]
        dct = A.sb("dct", [128, 32], F32)
        Sf = [A.sb("Sf%d" % i, [128, 128], F32) for i in range(2)]
        Sb_ = [A.sb("Sb%d" % i, [128, 128], BF16) for i in range(2)]
        amt = [A.sb("amt%d" % i, [128, 128], BF16) for i in range(2)]
        T.op("pool", lambda e: e.memset(onesf[:], 1.0), writes=[onesf])
        T.op("pool", lambda e: e.memset(dct[:], 0.0), writes=[dct])

        def bc(ref):
            return ref.unsqueeze(2).to_broadcast([128, 32, 64])

        def v3(ap):
            return ap.rearrange("p (c t) -> p c t", t=64)

        def h1_prep(h):
                hs = slice(h * 128, (h + 1) * 128)
                hp = h % 2
                oacc, vtok, qin, kin, kdt, qdec, dec = oacc2[hp], vtok2[hp], qin2[hp], kin2[hp], kdt2[hp], qdec2[hp], dec2[hp]
                T.dma("sp", qf[:], HQ[hs, :], reads=[HQ], writes=[qf], sembuf=qf)
                yield
                T.dma("sp", zr[0][:], HZF[hs, :], reads=[HZF], writes=[zr[0]], sembuf=zr[0])
                yield
                T.dma("sp", zr[1][:], HZB[hs, :], reads=[HZB], writes=[zr[1]], sembuf=zr[1])
                yield
                T.dma("sp", vtok[:], rows(HV[:, :])[:, :, hs], reads=[HV], writes=[vtok], sembuf=vtok)
                yield
                act(qf[:], qf[:], AF.Silu, [qf], [qf])
                yield
                T.op("pool", lambda e: e.memset(oacc[:], 0.0), writes=[oacc])
                yield
                for d in range(2):
                    sg = 1.0 if d == 0 else -1.0
                    o = 1 if d == 0 else 0
                    z = zr[d]
                    act(kf[:], z[:], AF.Sigmoid, [z], [kf], scale=-1.0)
                    yield
                    ts(kf[:], kf[:], lbt[:, 32 + d * 16 + h:32 + d * 16 + h + 1], None, ALU.mult, None, [kf, lbt], [kf])
                    yield
                    act(z[:], kf[:], AF.Ln, [kf], [z], scale=-1.0, bias=1.0)
                    yield
                    T.op("dve", lambda e: e.memset(Pp[:, 0:1], 0.0), writes=[Pp])
                    yield
                    T.op("dve", lambda e, z=z: e.tensor_tensor_scan(out=Pp[:, 1:TOK + 1], data0=onesf[:], data1=z[:], initial=0.0,
                                                                   op0=ALU.mult, op1=ALU.add), reads=[onesf, z], writes=[Pp])
                    yield
                    E3 = v3(Pp[:, o:o + TOK])
                    Mr = Pp[:, 32:TOK:64]
                    Lr = Pp[:, 64:TOK + 1:64] if d == 0 else Pp[:, 0:TOK:64]
                    Vr = Pp[:, 0:TOK:64] if d == 0 else Pp[:, 64:TOK + 1:64]
                    tt(v3(dt_[:]), E3, bc(Mr), ALU.subtract, [Pp], [dt_])
                    yield
                    act(et_[:], dt_[:], AF.Exp, [dt_], [et_], scale=sg)
                    yield
                    tt(qin[d][:], qf[:], et_[:], ALU.mult, [qf, et_], [qin[d]])
                    yield
                    act(et_[:], dt_[:], AF.Exp, [dt_], [et_], scale=-sg)
                    yield
                    tt(kin[d][:], kf[:], et_[:], ALU.mult, [kf, et_], [kin[d]])
                    yield
                    tt(v3(dt_[:]), E3, bc(Lr), ALU.subtract, [Pp], [dt_])
                    yield
                    act(et_[:], dt_[:], AF.Exp, [dt_], [et_], scale=-sg)
                    yield
                    tt(kdec[:], kf[:], et_[:], ALU.mult, [kf, et_], [kdec])
                    yield
                    tt(v3(dt_[:]), E3, bc(Vr), ALU.subtract, [Pp], [dt_])
                    yield
                    act(et_[:], dt_[:], AF.Exp, [dt_], [et_], scale=sg)
                    yield
                    tt(qdec[d][:], qf[:], et_[:], ALU.mult, [qf, et_], [qdec[d]])
                    yield
                    if d == 0:
                        act(et_[:], Pp[:, 1:TOK + 1], AF.Exp, [Pp], [et_])
                        yield
                    else:
                        act(et_[:], Pp[:, 0:TOK], AF.Exp, [Pp], [et_], scale=-1.0, bias=Pp[:, TOK:TOK + 1])
                        yield
                    tt(qgl[:], qf[:], et_[:], ALU.mult, [qf, et_], [qgl])
                    yield
                    T.dma("sp", (QGF if d == 0 else QGB)[hs, :], qgl[:], reads=[qgl], writes=[QGF if d == 0 else QGB], sembuf=qgl)
                    yield
                    tt(dec[d][:], Pp[:, 64:TOK + 1:64], Pp[:, 0:TOK:64], ALU.subtract, [Pp], [dec[d]])
                    yield
                    act(dec[d][:], dec[d][:], AF.Exp, [dec[d]], [dec[d]])
                    yield
                    act(dct[:, h * 2 + d:h * 2 + d + 1], Pp[:, TOK:TOK + 1], AF.Exp, [Pp], [dct])
                    yield
                    for j4 in range(4):
                        pt = PTB[j4 % 2]
                        for j in range(4):
                            jj = j4 * 4 + j
                            T.op("pe", lambda e, j=j, jj=jj, pt=pt: e.transpose(pt[:, j, :], kdec[:, jj * 128:(jj + 1) * 128], identb),
                                 reads=[kdec, cb16], writes=[pt], signal=(j == 3))
                            yield
                        cp(kdt[d][:, j4 * 4:j4 * 4 + 4, :], pt[:], [pt], [kdt[d]])
                        yield
                yield

        def h1_loop(h):
                hs = slice(h * 128, (h + 1) * 128)
                hp = h % 2
                oacc, vtok, qin, kin, kdt, qdec, dec = oacc2[hp], vtok2[hp], qin2[hp], kin2[hp], kdt2[hp], qdec2[hp], dec2[hp]
                for d in range(2):
                    T.op("pool", lambda e, d=d: e.memset(Sf[d][:], 0.0), writes=[Sf[d]])
                    T.op("pool", lambda e, d=d: e.memset(Sb_[d][:], 0.0), writes=[Sb_[d]])
                for step in range(16):
                    for d in range(2):
                        blk = step if d == 0 else 15 - step
                        bs = slice(blk * 128, (blk + 1) * 128)
                        mk = cst[:, C_MF:C_MF + 128] if d == 0 else cst[:, C_MB:C_MB + 128]
                        AT = PSF[d]
                        oI = PSF[2 + d]
                        oA = PSF[4 + d]
                        Uv = AT[:, 256:384]
                        mm(AT[:, 0:128], kin[d][:, bs], qin[d][:, bs], True, True, [kin[d], qin[d]], [AT], True)
                        tt(amt[d][:], AT[:, 0:128], mk, ALU.mult, [AT, cst], [amt[d]])
                        for ci in ((0, 1) if d == 0 else (1, 0)):
                            c = blk * 2 + ci
                            cs_ = slice(blk * 128 + ci * 64, blk * 128 + ci * 64 + 64)
                            ps_ = slice(ci * 64, ci * 64 + 64)
                            mm(oI[:, ci * 64:ci * 64 + 64], Sb_[d][:], qdec[d][:, cs_], True, True, [Sb_[d], qdec[d]], [oI], True)
                            mm(Uv, kdt[d][ps_, blk, :], vtok[ps_, blk, :], True, True, [kdt[d], vtok], [AT], True)
                            stt(Sf[d][:], Sf[d][:], dec[d][:, c:c + 1], Uv, ALU.mult, ALU.add, [Sf[d], dec[d], AT], [Sf[d]])
                            act(Sb_[d][:], Sf[d][:], AF.Copy, [Sf[d]], [Sb_[d]])
                        mm(oA[:, 0:128], vtok[:, blk, :], amt[d][:], True, True, [vtok, amt[d]], [oA], True)
                        tt(oacc[:, bs], oacc[:, bs], oA[:, 0:128], ALU.add, [oacc, oA], [oacc])
                        tt(oacc[:, bs], oacc[:, bs], oI[:, 0:128], ALU.add, [oacc, oI], [oacc])
                        yield
                for d in range(2):
                    hd = h * 2 + d
                    T.dma("sp", SCI[hd // 8][(hd % 8) * 128:(hd % 8 + 1) * 128, :], Sf[d][:], reads=[Sf[d]], writes=[SCI[hd // 8]],
                          sembuf=Sf[d])
                T.dma("sp", OLOC[hs, :], oacc[:], reads=[oacc], writes=[OLOC], sembuf=oacc)

                yield

        def run_some(gen, n):
            if gen is None:
                return None
            for _ in range(n):
                try:
                    next(gen)
                except StopIteration:
                    return None
            return gen

        g = h1_prep(0)
        while g is not None:
            g = run_some(g, 1000)
        for h in range(KH):
            lp = h1_loop(h)
            pp = h1_prep(h + 1) if h + 1 < KH else None
            while lp is not None or pp is not None:
                lp = run_some(lp, 1)
                pp = run_some(pp, 3)
        T.dma("sp", DCI[:, :], dct[:], reads=[dct], writes=[DCI], sembuf=dct)
        for i in range(4):
            allgather(SCI[i], SCO[i])
        allgather(DCI, DCO)
        T.barrier(A.reset())
        if KSTOP <= 2:
            T.barrier()
            raise _Stop()

        if KSTOP > 3:
            for i in range(16):
                T.dma("pool", WAT[i][:, :].rearrange("p (k c) -> p k c", c=256), rows(w_pa)[:, :, i * 256:(i + 1) * 256],
                      writes=[WAT[i]], sembuf=convsem)
                T.dma("pool", WBT[i][:, :].rearrange("p (k c) -> p k c", c=256), rows(w_pb)[:, :, i * 256:(i + 1) * 256],
                      writes=[WBT[i]], sembuf=convsem)
                for j in range(2):
                    T.dma("pool", WGT[i][j][:, :].rearrange("p (k c) -> p k c", c=256),
                          rows(w_gate)[:, :, j * D + i * 256:j * D + (i + 1) * 256], writes=[WGT[i][j]], sembuf=convsem)
            for i in range(16):
                T.dma("pool", WOT[i][:, :].rearrange("p (k c) -> p k c", c=256), rows(w_o)[:, :, i * 256:(i + 1) * 256],
                      writes=[WOT[i]], sembuf=convsem)
            for i in range(16):
                for q, (k0, kq) in enumerate(QK4):
                    T.dma("pool", WDT[i][q][:, 0:kq * 256].rearrange("p (k c) -> p k c", c=256),
                          rows(w_down)[:, k0:k0 + kq, i * 256:(i + 1) * 256], writes=[WDT[i][q]], sembuf=convsem)
            for i in range(16):
                T.dma("pool", WPT[i][:, :].rearrange("p (k c) -> p k c", c=256), rows(w_pg)[:, :, i * 256:(i + 1) * 256],
                      writes=[WPT[i]], sembuf=convsem)
        KT2 = [A.sb("KT%d" % i, [128, 4 * TOK], BF16) for i in range(2)]
        VG2 = [A.sb("VG%d" % i, [128, 64, 128], BF16) for i in range(2)]

        def load_kv(g):
            KT, VG = KT2[g % 2], VG2[g % 2]
            for r in range(4):
                T.dma("sp", KT[:, r * TOK:(r + 1) * TOK], KCO[g][r * 128:(r + 1) * 128, :],
                      reads=[KCO[g]], writes=[KT], sembuf=KT)
                T.dma("sp", VG[:, r * 16:(r + 1) * 16, :], rows(VCO[g][:, :])[:, r * 16:(r + 1) * 16, :],
                      reads=[VCO[g]], writes=[VG], sembuf=VG)
        qT = [A.sb("qT%d" % i, [128, NT], BF16) for i in range(2)]
        pTs = [A.sb("pT%d" % i, [128, NT], BF16) for i in range(6)]
        rl = [A.sb("rl%d" % i, [128, NT], F32) for i in range(2)]
        yo = [A.sb("yo%d" % i, [128, NT], BF16) for i in range(2)]
        accD = [A.sb("accD%d" % i, [128, NT], F32) for i in range(2)]
        accP = [A.sb("accP%d" % i, [128, NT], F32) for i in range(2)]
        lhi = [A.sb("lhi%d" % i, [128, NT], BF16) for i in range(2)]
        llo = [A.sb("llo%d" % i, [128, NT], BF16) for i in range(2)]
        LA = 2
        u = 0
        load_kv(0)
        for g in range(KG):
            KT, VG = KT2[g % 2], VG2[g % 2]
            if g + 1 < KG:
                load_kv(g + 1)
            for qt in range(KQT):
                for hh in range(4):
                    h = g * 4 + hh
                    q_ = qT[u % 2]
                    T.dma("sp", q_[:], QS[h * 128:(h + 1) * 128, qt * NT:(qt + 1) * NT], reads=[QS], writes=[q_], sembuf=q_)
                    oT = PSF[4 + u % 2]
                    lT = PSF[3]
                    aD, aP = accD[u % 2], accP[u % 2]
                    hi_, lo_ = lhi[u % 2], llo[u % 2]
                    LAST_DVE = 55

                    def qk(kb, q_=q_):
                        sT_ = PSF[kb % 3]
                        mm(sT_[:], KT[:, kb * 128:(kb + 1) * 128], q_[:], True, True, [KT, q_], [sT_], True)

                    for kb in range(LA):
                        qk(kb)
                    for kb in range(64):
                        if kb + LA < 64:
                            qk(kb + LA)
                        sT = PSF[kb % 3]
                        p_ = pTs[kb % 6]
                        act(p_[:], sT[:], AF.Exp, [sT], [p_], scale=SCALE, bias=ESHIFT)
                        mm(oT[:], VG[:, kb, :], p_[:], kb == 0, kb == 63, [VG, p_], [oT], True)
                        if kb % 2 == 0 or kb > LAST_DVE:
                            mm(lT[:], onesb, p_[:], kb == 0, False, [cb16, p_], [lT], True)
                        elif kb == 1:
                            cp(aD[:], p_[:], [p_], [aD])
                        else:
                            tt(aD[:], aD[:], p_[:], ALU.add, [aD, p_], [aD])
                        if kb == LAST_DVE:
                            act(hi_[:], aD[:], AF.Copy, [aD], [hi_])
                            tt(lo_[:], aD[:], hi_[:], ALU.subtract, [aD, hi_], [lo_])
                    mm(lT[:], onesb, hi_[:], False, False, [cb16, hi_], [lT], False)
                    mm(lT[:], onesb, lo_[:], False, True, [cb16, hi_, lo_], [lT], True)
                    r_ = rl[u % 2]
                    y_ = yo[u % 2]
                    T.op("dve", lambda e, r_=r_, lT=lT: e.reciprocal(out=r_[:], in_=lT[:]), reads=[lT], writes=[r_])
                    tt(y_[:], oT[:], r_[:], ALU.mult, [oT, r_], [y_])
                    T.dma("sp", YA[h * 128:(h + 1) * 128, qt * NT:(qt + 1) * NT], y_[:], reads=[y_], writes=[YA], sembuf=y_)
                    u += 1
        T.barrier(A.reset())
        if KSTOP <= 3:
            T.barrier()
            raise _Stop()

        Dall = A.sb("Dall", [128, 4, 32], F32)
        Dm = [A.sb("Dm%d" % i, [128, 4, 32], F32) for i in range(2)]
        Ur = [A.sb("Ur%d" % i, [128, 4, 128], F32) for i in range(2)]
        Sacc = A.sb("Sacc", [128, 128], F32)
        Sinb = A.sb("Sinb", [128, 32, 128], BF16)
        oloc2 = [A.sb("oloc%d" % i, [128, TOK], F32) for i in range(2)]
        ghs2 = [A.sb("ghs%d" % i, [128, TOK], F32) for i in range(2)]
        qg2 = [[A.sb("qg%d" % i, [128, TOK], BF16) for i in range(2)] for _ in range(2)]

        def h2_load(h):
            hs = slice(h * 128, (h + 1) * 128)
            oloc, ghs, qg = oloc2[h % 2], ghs2[h % 2], qg2[h % 2]
            T.dma("sp", oloc[:], OLOC[hs, :], reads=[OLOC], writes=[oloc], sembuf=oloc)
            T.dma("sp", ghs[:], HG[hs, :], reads=[HG], writes=[ghs], sembuf=ghs)
            T.dma("sp", qg[0][:], QGF[hs, :], reads=[QGF], writes=[qg[0]], sembuf=qg[0])
            T.dma("sp", qg[1][:], QGB[hs, :], reads=[QGB], writes=[qg[1]], sembuf=qg[1])
        TMP = [A.sb("tmp%d" % i, [128, NT], F32) for i in range(6)]
        TB = [A.sb("tb%d" % i, [128, NT], BF16) for i in range(4)]
        T.dma("sp", Dall[:], DCO[:, :].rearrange("(r p) c -> p r c", p=128), reads=[DCO], writes=[Dall], sembuf=Dall)
        for d in range(2):
            mcol = V_MF if d == 0 else V_MB
            ocol = V_OMF if d == 0 else V_OMB
            for r in range(4):
                ts(Dm[d][:, r, :], Dall[:, r, :], vec[:, mcol + r:mcol + r + 1], vec[:, ocol + r:ocol + r + 1],
                   ALU.mult, ALU.add, [Dall, vec], [Dm[d]])
        SCO4 = [SCO[i][:, :].rearrange("(r x p) e -> p r x e", r=4, p=128) for i in range(4)]
        for hd in range(2 * KH):
            d = hd % 2
            mcol = V_MF if d == 0 else V_MB
            ur = Ur[hd % 2]
            T.dma("sp", ur[:], SCO4[hd // 8][:, :, hd % 8, :], reads=[SCO[hd // 8]], writes=[ur], sembuf=ur)
            T.op("pool", lambda e: e.memset(Sacc[:], 0.0), writes=[Sacc])
            for r in ((0, 1, 2, 3) if d == 0 else (3, 2, 1, 0)):
                ts(Sacc[:], Sacc[:], Dm[d][:, r, hd:hd + 1], None, ALU.mult, None, [Sacc, Dm[d]], [Sacc])
                stt(Sacc[:], ur[:, r, :], vec[:, mcol + r:mcol + r + 1], Sacc[:], ALU.mult, ALU.add, [ur, vec, Sacc], [Sacc])
            cp(Sinb[:, hd, :], Sacc[:], [Sacc], [Sinb])
        for h in range(KH):
            hs = slice(h * 128, (h + 1) * 128)
            oloc, ghs, qg = oloc2[h % 2], ghs2[h % 2], qg2[h % 2]
            if h == 0:
                h2_load(0)
            if h + 1 < KH:
                h2_load(h + 1)
            act(ghs[:], ghs[:], AF.Silu, [ghs], [ghs])
            for ti in range(NTT):
                tsl = slice(ti * NT, (ti + 1) * NT)
                cps = PSF[ti % 2]
                mm(cps[:], Sinb[:, 2 * h, :], qg[0][:, tsl], True, False, [Sinb, qg[0]], [cps], False)
                mm(cps[:], Sinb[:, 2 * h + 1, :], qg[1][:, tsl], False, True, [Sinb, qg[0], qg[1]], [cps], True)
                o_ = TMP[(ti * 3) % 6]
                tt(o_[:], oloc[:, tsl], cps[:], ALU.add, [oloc, cps], [o_])
                sq_ = TB[(ti * 2) % 4]
                act(sq_[:], o_[:], AF.Square, [o_], [sq_])
                ss = PSF[2 + ti % 2]
                mm(ss[:], onesb, sq_[:], True, True, [cb16, sq_], [ss], True)
                rstd, tl = TMP[(ti * 3 + 1) % 6], TMP[(ti * 3 + 2) % 6]
                rstd_from_ss(rstd, ss[:], ss, tl, 1.0 / 128, RMS_EPS)
                stt(o_[:], o_[:], vec[:, V_HGN + h:V_HGN + h + 1], rstd[:], ALU.mult, ALU.mult, [o_, vec, rstd], [o_])
                yb = TB[(ti * 2 + 1) % 4]
                tt(yb[:], o_[:], ghs[:, tsl], ALU.mult, [o_, ghs], [yb])
                T.dma("sp", YH[hs, tsl], yb[:], reads=[yb], writes=[YH], sembuf=yb)
        T.barrier(A.reset())
        if KSTOP <= 4:
            T.barrier()
            raise _Stop()

        def stat_mm(s1, s2, item):
            cb, rb, rsq = item
            mm(s1[:], onesb, rb[:], cb == 0, cb == KCB - 1, [cb16, rb], [s1], True)
            mm(s2[:], onesb, rsq[:], cb == 0, cb == KCB - 1, [cb16, rsq], [s2], True)

        def ln_finish(s1, s2, mean, rstd, nmr, eps):
            ts(mean[:], s1[:], 1.0 / D, None, ALU.mult, None, [s1], [mean])
            tt(nmr[:], mean[:], mean[:], ALU.mult, [mean], [nmr])
            stt(rstd[:], s2[:], 1.0 / D, nmr[:], ALU.mult, ALU.subtract, [s2, nmr], [rstd])
            ts(rstd[:], rstd[:], eps, None, ALU.add, None, [rstd], [rstd])
            act(rstd[:], rstd[:], AF.Ln, [rstd], [rstd])
            act(rstd[:], rstd[:], AF.Exp, [rstd], [rstd], scale=-0.5)
            stt(nmr[:], mean[:], -1.0, rstd[:], ALU.mult, ALU.mult, [mean, rstd], [nmr])

        for ti in range(KT4):
            t0 = ti * NT
            ya = A.sb("ya", [128, 16, NT], BF16)
            yh = A.sb("yh", [128, 16, NT], BF16)
            xb = A.sb("xb", [128, 32, NT], BF16)
            mT = A.sb("mT", [128, 32, NT], BF16)
            off_keep = A.off
            slabs = [A.sb("ws%d" % i, [128, 96, 256], BF16) for i in range(2)]
            TMP = [A.sb("tmp%d" % i, [128, NT], F32) for i in range(2)]
            T.dma("sp", ya[:], rows(YA[:, :])[:, :, t0:t0 + NT], reads=[YA], writes=[ya], sembuf=ya)
            T.dma("sp", yh[:], rows(YH[:, :])[:, :, t0:t0 + NT], reads=[YH], writes=[yh], sembuf=yh)
            xsrc = rows(xT)[:, :, t0:t0 + NT]
            T.dma("pool", xb[:, 0:16, :], xsrc[:, 0:16, :], writes=[xb], sembuf=xb)
            T.dma("pool", xb[:, 16:32, :], xsrc[:, 16:32, :], writes=[xb], sembuf=xb)
            for i in range((43 * ti) // KT4, (43 * (ti + 1)) // KT4):
                for j in range(2):
                    T.dma("pool", WUT[i][j][:, :].rearrange("p (k c) -> p k c", c=256),
                          rows(w_up)[:, :, j * DFF + i * 256:j * DFF + (i + 1) * 256], writes=[WUT[i][j]], sembuf=convsem2)
            for cb in range(KCB):
                if cb % 2 == 0:
                    slab = slabs[(cb // 2) % 2]
                    i2 = cb // 2
                    T.dma("sp", slab[:, 0:16, :].rearrange("p k c -> p (k c)"), WAT[i2][:, :], reads=[WAT[i2]], writes=[slab], sembuf=slab)
                    T.dma("sp", slab[:, 16:32, :].rearrange("p k c -> p (k c)"), WBT[i2][:, :], reads=[WBT[i2]], writes=[slab], sembuf=slab)
                    T.dma("sp", slab[:, 32:64, :].rearrange("p k c -> p (k c)"), WGT[i2][0][:, :], reads=[WGT[i2][0]], writes=[slab],
                          sembuf=slab)
                    T.dma("sp", slab[:, 64:96, :].rearrange("p k c -> p (k c)"), WGT[i2][1][:, :], reads=[WGT[i2][1]], writes=[slab],
                          sembuf=slab)
                jo = (cb % 2) * 128
                js = slice(jo, jo + 128)
                pa, pb, ga, gh_ = PSF[0], PSF[1], PSF[2], PSF[3]
                for k in range(16):
                    mm(pa[:], slab[:, k, js], ya[:, k, :], k == 0, k == 15, [slab, ya], [pa], k == 15)
                for k in range(16):
                    mm(pb[:], slab[:, 16 + k, js], yh[:, k, :], k == 0, k == 15, [slab, yh], [pb], k == 15)
                for k in range(32):
                    mm(ga[:], slab[:, 32 + k, js], xb[:, k, :], k == 0, k == 31, [slab, xb], [ga], k == 31)
                for k in range(32):
                    mm(gh_[:], slab[:, 64 + k, js], xb[:, k, :], k == 0, k == 31, [slab, xb], [gh_], k == 31)
                sa, sh = TMP[0], TMP[1]
                act(sa[:], ga[:], AF.Sigmoid, [ga, vec], [sa], bias=vec[:, V_BG + cb:V_BG + cb + 1])
                act(sh[:], gh_[:], AF.Sigmoid, [gh_, vec], [sh], bias=vec[:, V_BG + 32 + cb:V_BG + 32 + cb + 1])
                tt(sa[:], sa[:], pa[:], ALU.mult, [sa, pa], [sa])
                tt(sh[:], sh[:], pb[:], ALU.mult, [sh, pb], [sh])
                tt(mT[:, cb, :], sa[:], sh[:], ALU.add, [sa, sh], [mT])
            T.barrier()
            A.off = A.base
            rT = A.sb("rT", [128, 32, NT], F32)
            assert A.off <= off_keep - 32 * NT * 2
            mT2 = mT
            A.off = off_keep
            slabs = [A.sb("wo%d" % i, [128, 32, 256], BF16) for i in range(3)]
            TMP = [A.sb("tq%d" % i, [128, NT], F32) for i in range(6)]
            TB = [A.sb("tbq%d" % i, [128, NT], BF16) for i in range(4)]
            s1, s2 = PSF[4], PSF[5]
            pend = []
            for cb in range(KCB):
                if cb % 2 == 0:
                    slab = slabs[(cb // 2) % 3]
                    T.dma("sp", slab[:].rearrange("p k c -> p (k c)"), WOT[cb // 2][:, :], reads=[WOT[cb // 2]], writes=[slab], sembuf=slab)
                js = slice((cb % 2) * 128, (cb % 2) * 128 + 128)
                acc = PSF[cb % 2]
                xf = TMP[cb % 2]
                T.dma("sp", xf[:], xT[cb * 128:(cb + 1) * 128, t0:t0 + NT], writes=[xf], sembuf=xf)
                for k in range(32):
                    mm(acc[:], slab[:, k, js], mT2[:, k, :], k == 0, k == 31, [slab, mT2], [acc], k == 31)
                while len(pend) > 1:
                    stat_mm(s1, s2, pend.pop(0))
                stt(rT[:, cb, :], xf[:], ALPHA, acc[:], ALU.mult, ALU.add, [xf, acc], [rT])
                rb, rsq = TB[(cb * 2) % 4], TB[(cb * 2 + 1) % 4]
                act(rb[:], rT[:, cb, :], AF.Copy, [rT], [rb])
                act(rsq[:], rT[:, cb, :], AF.Square, [rT], [rsq])
                pend.append((cb, rb, rsq))
            while pend:
                stat_mm(s1, s2, pend.pop(0))
            mean, rstd, nmr = TMP[2], TMP[3], TMP[4]
            ln_finish(s1, s2, mean, rstd, nmr, LN_EPS)
            hbufs = [TMP[0], TMP[1], TMP[5]]
            for cb in range(KCB):
                hb = hbufs[cb % 3]
                tt(hb[:], rT[:, cb, :], rstd[:], ALU.mult, [rT, rstd], [hb])
                tt(hb[:], hb[:], nmr[:], ALU.add, [hb, nmr], [hb])
                act(hb[:], hb[:], AF.Identity, [hb, vec], [hb], scale=vec[:, V_L1G + cb:V_L1G + cb + 1],
                    bias=vec[:, V_L1B + cb:V_L1B + cb + 1])
                T.dma("sp", H1[cb * 128:(cb + 1) * 128, 1 + t0:1 + t0 + NT], hb[:], reads=[hb], writes=[H1], sembuf=hb)
                if ti == 0:
                    T.dma("sp", EDI[cb * 128:(cb + 1) * 128, 0:1], hb[:, 0:1], reads=[hb], writes=[EDI], sembuf=hb)
                if ti == NTT - 1:
                    T.dma("sp", EDI[cb * 128:(cb + 1) * 128, 1:2], hb[:, NT - 1:NT], reads=[hb], writes=[EDI], sembuf=hb)
            T.barrier(A.reset())

        allgather(EDI, EDO)
        Eg = A.sb("Eg", [128, 4, 32, 2], F32)
        hal = A.sb("hal", [128, 32, 2], F32)
        T.dma("sp", Eg[:], EDO[:, :].rearrange("(r k p) c -> p r k c", r=4, p=128), reads=[EDO], writes=[Eg], sembuf=Eg)
        T.op("pool", lambda e: e.memset(hal[:], 0.0), writes=[hal])
        for r in range(4):
            stt(hal[:, :, 0], Eg[:, r, :, 1], vec[:, V_ML + r:V_ML + r + 1], hal[:, :, 0], ALU.mult, ALU.add, [Eg, vec, hal], [hal])
            stt(hal[:, :, 1], Eg[:, r, :, 0], vec[:, V_MR + r:V_MR + r + 1], hal[:, :, 1], ALU.mult, ALU.add, [Eg, vec, hal], [hal])
        H1r = rows(H1[:, :])
        T.dma("sp", H1r[:, :, 0:1], hal[:, :, 0:1], reads=[hal], writes=[H1], sembuf=hal)
        T.dma("sp", H1r[:, :, TOK + 1:TOK + 2], hal[:, :, 1:2], reads=[hal], writes=[H1], sembuf=hal)
        T.barrier(A.reset())
        if KSTOP <= 5:
            T.barrier()
            raise _Stop()

        for ti in range(KT5):
            t0 = ti * NT
            aT = A.sb("aT", [128, NFB, NT], BF16)
            off_keep = A.off
            h1b = A.sb("h1b", [128, 32, NT + 2], BF16)
            slabs = [A.sb("wu%d" % i, [128, 64, 256], BF16) for i in range(2)]
            gsb = [A.sb("gsb%d" % i, [128, NT + 2], F32) for i in range(2)]
            TMP = [A.sb("tmp%d" % i, [128, NT], F32) for i in range(4)]
            hsrc = H1r[:, :, t0:t0 + NT + 2]
            T.dma("pool", h1b[:, 0:16, :], hsrc[:, 0:16, :], reads=[H1], writes=[h1b], sembuf=h1b)
            T.dma("pool", h1b[:, 16:32, :], hsrc[:, 16:32, :], reads=[H1], writes=[h1b], sembuf=h1b)
            for cb in range(KFB):
                if cb % 2 == 0:
                    slab = slabs[(cb // 2) % 2]
                    i2 = cb // 2
                    T.dma("sp", slab[:, 0:32, :].rearrange("p k c -> p (k c)"), WUT[i2][0][:, :], reads=[WUT[i2][0]], writes=[slab],
                          sembuf=slab)
                    T.dma("sp", slab[:, 32:64, :].rearrange("p k c -> p (k c)"), WUT[i2][1][:, :], reads=[WUT[i2][1]], writes=[slab],
                          sembuf=slab)
                js = slice((cb % 2) * 128, (cb % 2) * 128 + 128)
                up, gp, gh_ = PSF[cb % 2], PSF[2 + cb % 2], PSF[4 + cb % 2]
                for k in range(32):
                    mm(up[:], slab[:, k, js], h1b[:, k, 1:NT + 1], k == 0, k == 31, [slab, h1b], [up], k == 31)
                for k in range(32):
                    mm(gp[:], slab[:, 32 + k, js], h1b[:, k, 1:NT + 1], k == 0, k == 31, [slab, h1b], [gp], k == 31)
                for k in range(32):
                    mm(gh_[:, 0:2], slab[:, 32 + k, js], h1b[:, k, 0:NT + 2:NT + 1], k == 0, k == 31, [slab, h1b], [gh_], k == 31)
                gs = gsb[cb % 2]
                act(gs[:, 1:NT + 1], gp[:], AF.Copy, [gp], [gs])
                cp(gs[:, 0:NT + 2:NT + 1], gh_[:, 0:2], [gh_], [gs])
                c_ = TMP[cb % 2]
                ts(c_[:], gs[:, 0:NT], vec[:, V_CW + cb:V_CW + cb + 1], vec[:, V_CB + cb:V_CB + cb + 1], ALU.mult, ALU.add,
                   [gs, vec], [c_])
                stt(c_[:], gs[:, 1:NT + 1], vec[:, V_CW + NFB + cb:V_CW + NFB + cb + 1], c_[:], ALU.mult, ALU.add, [gs, vec, c_], [c_])
                stt(c_[:], gs[:, 2:NT + 2], vec[:, V_CW + 2 * NFB + cb:V_CW + 2 * NFB + cb + 1], c_[:], ALU.mult, ALU.add,
                    [gs, vec, c_], [c_])
                act(c_[:], c_[:], AF.Silu, [c_], [c_])
                tt(aT[:, cb, :], c_[:], up[:], ALU.mult, [c_, up], [aT])
            T.barrier()
            A.off = off_keep
            rT = A.sb("rT", [128, 32, NT], F32)
            TMP = [A.sb("tq%d" % i, [128, NT], F32) for i in range(5)]
            TB = [A.sb("tbq%d" % i, [128, NT], BF16) for i in range(4)]
            off_slabs = A.off
            slabs = [A.sb("wd%d" % i, [128, 22, 256], BF16) for i in range(3)]
            s1, s2 = PSF[4], PSF[5]
            pend = []
            for cb2 in range(KCB // 2):
                accs = [PSF[(cb2 % 2) * 2], PSF[(cb2 % 2) * 2 + 1]]
                for q, (k0, kq) in enumerate(QK4):
                    slab = slabs[(cb2 * 4 + q) % 3]
                    T.dma("sp", slab[:, 0:kq, :].rearrange("p k c -> p (k c)"), WDT[cb2][q][:, 0:kq * 256], reads=[WDT[cb2][q]],
                          writes=[slab], sembuf=slab)
                    for j in range(2):
                        for k in range(kq):
                            mm(accs[j][:], slab[:, k, j * 128:(j + 1) * 128], aT[:, k0 + k, :], q == 0 and k == 0,
                               q == 3 and k == kq - 1, [slab, aT], [accs[j]], k == kq - 1)
                while pend:
                    stat_mm(s1, s2, pend.pop(0))
                for j in range(2):
                    cb = cb2 * 2 + j
                    acc = accs[j]
                    hf = TMP[cb % 2]
                    T.dma("sp", hf[:], H1[cb * 128:(cb + 1) * 128, 1 + t0:1 + t0 + NT], reads=[H1], writes=[hf], sembuf=hf)
                    stt(rT[:, cb, :], hf[:], ALPHA, acc[:], ALU.mult, ALU.add, [hf, acc], [rT])
                    rb, rsq = TB[(cb * 2) % 4], TB[(cb * 2 + 1) % 4]
                    act(rb[:], rT[:, cb, :], AF.Copy, [rT], [rb])
                    act(rsq[:], rT[:, cb, :], AF.Square, [rT], [rsq])
                    pend.append((cb, rb, rsq))
            while pend:
                stat_mm(s1, s2, pend.pop(0))
            mean, rstd, nmr = TMP[2], TMP[3], TMP[4]
            ln_finish(s1, s2, mean, rstd, nmr, LN_EPS)
            T.barrier()
            A.off = A.base
            x2b = A.sb("x2b", [128, 32, NT], BF16)
            pTb = A.sb("pTb", [128, 2, NT], BF16)
            wple = A.sb("wple", [128, 2, D], BF16)
            assert A.off <= off_keep
            A.off = off_slabs
            slabs = [A.sb("wg%d" % i, [128, 32, 256], BF16) for i in range(2)]
            T.dma("pool", pTb[:], rows(pT)[:, :, t0:t0 + NT], writes=[pTb], sembuf=pTb)
            T.dma("pool", wple[:], rows(w_ple)[:, :, :], writes=[wple], sembuf=wple)
            for cb in range(KCB):
                r_ = rT[:, cb, :]
                tt(r_, r_, rstd[:], ALU.mult, [rT, rstd], [rT])
                tt(r_, r_, nmr[:], ALU.add, [rT, nmr], [rT])
                act(r_, r_, AF.Identity, [rT, vec], [rT], scale=vec[:, V_L2G + cb:V_L2G + cb + 1],
                    bias=vec[:, V_L2B + cb:V_L2B + cb + 1])
                act(x2b[:, cb, :], r_, AF.Copy, [rT], [x2b])
            for cb in range(KCB):
                if cb % 2 == 0:
                    slab = slabs[(cb // 2) % 2]
                    T.dma("sp", slab[:].rearrange("p k c -> p (k c)"), WPT[cb // 2][:, :], reads=[WPT[cb // 2]], writes=[slab], sembuf=slab)
                js = slice((cb % 2) * 128, (cb % 2) * 128 + 128)
                pg, pl = PSF[cb % 2], PSF[2 + cb % 2]
                for k in range(32):
                    mm(pg[:], slab[:, k, js], x2b[:, k, :], k == 0, k == 31, [slab, x2b], [pg], k == 31)
                for k in range(2):
                    mm(pl[:], wple[:, k, cb * 128:(cb + 1) * 128], pTb[:, k, :], k == 0, k == 1, [wple, pTb], [pl], k == 1)
                s_ = TMP[cb % 2]
                act(s_[:], pg[:], AF.Sigmoid, [pg], [s_])
                tt(s_[:], s_[:], pl[:], ALU.mult, [s_, pl], [s_])
                tt(s_[:], s_[:], rT[:, cb, :], ALU.add, [s_, rT], [s_])
                T.dma("sp", outT[cb * 128:(cb + 1) * 128, t0:t0 + NT], s_[:], reads=[s_], sembuf=s_)
            T.barrier(A.reset())
        T.barrier()
        print("kernel build: ninst=%d nwaits=%d dma_sems=%d" % (T.ninst, T.nwaits, len(T.dma_sems)), flush=True)
    return nc


def _consts():
    i = np.arange(128)
    R = np.zeros((128, 128), np.float32)
    for a in range(128):
        sec = a // 64
        loc = a % 64
        if loc < 32:
            R[a, sec * 64 + loc + 32] = -1.0
        else:
            R[a, sec * 64 + loc - 32] = 1.0
    RT = R.T.copy()
    ident = np.eye(128, dtype=np.float32)
    ones = np.ones((128, 128), np.float32)
    s = i[:, None]
    t = i[None, :]
    same = (s // 64) == (t // 64)
    maskF = (same & (s <= t)).astype(np.float32)
    maskB = (same & (s >= t)).astype(np.float32)
    return np.concatenate([RT, ident, ones, maskF, maskB], axis=1).astype(np.float32)


def _rope_tables(tok0):
    t = np.arange(tok0, tok0 + TOK)
    row = (t // 64).astype(np.float32)
    col = (t % 64).astype(np.float32)
    sec = 64
    inv = (10000.0 ** (-np.arange(0, sec, 2, dtype=np.float32) / sec)).astype(np.float32)
    ang_r = row[:, None] * inv[None, :]
    ang_c = col[:, None] * inv[None, :]
    ang = np.concatenate([ang_r, ang_r, ang_c, ang_c], axis=-1).astype(np.float32)
    return np.ascontiguousarray(np.cos(ang).T.astype(np.float32)), np.ascontiguousarray(np.sin(ang).T.astype(np.float32))


_NC_CACHE = {}


def kernel(x, p, w_in, q_norm, k_norm, lb_logits, hg_norm, w_pa, w_pb, w_gate, b_gate, w_o,
           ln1_g, ln1_b, w_up, conv_w, conv_b, w_down, ln2_g, ln2_b, w_pg, w_ple):
    f = lambda a: np.ascontiguousarray(np.asarray(a, dtype=np.float32))
    x = f(x); p = f(p)
    col = lambda v, n: f(v).reshape(n, 128).T
    vec = np.zeros((128, NV), np.float32)
    vec[:, V_BG:V_BG + 64] = col(b_gate[0], 64)
    vec[:, V_L1G:V_L1G + 32] = col(ln1_g[0], 32)
    vec[:, V_L1B:V_L1B + 32] = col(ln1_b[0], 32)
    vec[:, V_L2G:V_L2G + 32] = col(ln2_g[0], 32)
    vec[:, V_L2B:V_L2B + 32] = col(ln2_b[0], 32)
    cw = f(conv_w)[0]
    for tap in range(3):
        vec[:, V_CW + tap * NFB:V_CW + (tap + 1) * NFB] = col(cw[tap], NFB)
    vec[:, V_CB:V_CB + NFB] = col(conv_b[0], NFB)
    vec[:, V_QN] = f(q_norm)[0]
    vec[:, V_KN] = f(k_norm)[0]
    vec[:, V_HGN:V_HGN + 16] = col(hg_norm[0], 16)
    lbl = f(lb_logits)
    for d in range(2):
        for l in range(2):
            vec[:, V_LBL + (d * 2 + l) * 16:V_LBL + (d * 2 + l) * 16 + 16] = col(lbl[d, l], 16)
    cst = _consts()
    weights = {"w_in": f(w_in)[0], "w_pa": f(w_pa)[0], "w_pb": f(w_pb)[0], "w_gate": f(w_gate)[0], "w_o": f(w_o)[0],
               "w_up": f(w_up)[0], "w_down": f(w_down)[0], "w_pg": f(w_pg)[0], "w_ple": f(w_ple)[0]}
    in_maps = []
    for c in range(8):
        b, s = c // 4, c % 4
        v = vec.copy()
        for r in range(4):
            v[:, V_MF + r] = 1.0 if r < s else 0.0
            v[:, V_MB + r] = 1.0 if r > s else 0.0
            v[:, V_ML + r] = 1.0 if r == s - 1 else 0.0
            v[:, V_MR + r] = 1.0 if r == s + 1 else 0.0
            v[:, V_OMF + r] = 0.0 if r < s else 1.0
            v[:, V_OMB + r] = 0.0 if r > s else 1.0
        cosT, sinT = _rope_tables(s * TOK)
        m = {"xT": np.ascontiguousarray(x[b, s * TOK:(s + 1) * TOK, :].T),
             "pT": np.ascontiguousarray(p[0, b, s * TOK:(s + 1) * TOK, :].T),
             "vec": v, "cst": cst, "cosT": cosT, "sinT": sinT}
        m.update(weights)
        in_maps.append(m)
    if "nc" not in _NC_CACHE:
        try:
            build_nc()
        except _Stop:
            pass
    in_maps = [{k: v for k, v in m.items() if k in _NC_CACHE["names"]} for m in in_maps]
    res = run_bass_kernel_spmd(_NC_CACHE["nc"], in_maps, core_ids=list(range(8)))
    out = np.empty((2, 4 * TOK, D), np.float32)
    for c in range(8):
        b, s = c // 4, c % 4
        out[b, s * TOK:(s + 1) * TOK, :] = res.results[c]["outT"].T
    return out
```
